# Optimizing a Trainium2 kernel written in Bass

```python
import math
import jax
import jax.numpy as jnp
from jax import lax
import numpy as np

D_MODEL = 2048
BATCH = 8
SEQ = 4096
DEPTH = 2
DEC_BATCH = 8
DEC_SEQ = 64
PAST_LEN = 4096

CHUNK = 64
N_META = 16
Q_BLOCK = 128
FOX_HEADS = 8
FOX_HEAD_DIM = D_MODEL // 16
FOX_WIDTH = FOX_HEADS * FOX_HEAD_DIM
DIFF_HEADS = 4
DIFF_HEAD_DIM = D_MODEL // 16
DIFF_V_DIM = 2 * DIFF_HEAD_DIM
DIFF_QK_WIDTH = DIFF_HEADS * 2 * DIFF_HEAD_DIM
DIFF_WIDTH = DIFF_HEADS * DIFF_V_DIM
ROT_DIM = DIFF_HEAD_DIM // 4
ROPE_THETA = 500000.0
D_FF = 4 * D_MODEL
SPLIT_SIZES = (FOX_WIDTH, FOX_WIDTH, FOX_WIDTH, FOX_HEADS,
               DIFF_QK_WIDTH, DIFF_QK_WIDTH, DIFF_WIDTH, D_MODEL, D_MODEL)
N_IN = sum(SPLIT_SIZES)
DEEPNORM_ALPHA = (2 * DEPTH) ** 0.25
DEEPNORM_BETA = (8 * DEPTH) ** -0.25
FOX_SCALE = FOX_HEAD_DIM ** -0.5
DIFF_SCALE = DIFF_HEAD_DIM ** -0.5
LN_EPS = 1e-5
RMS_EPS = 1e-5
NEG = -1e30

kernel_name = 'fox_diff_gated_hybrid_stream_step'


def layer_norm(x, g, b):
    xf = x.astype(jnp.float32)
    mu = jnp.mean(xf, axis=-1, keepdims=True)
    var = jnp.mean(jnp.square(xf - mu), axis=-1, keepdims=True)
    y = (xf - mu) * lax.rsqrt(var + LN_EPS) * g.astype(jnp.float32) + b.astype(jnp.float32)
    return y.astype(x.dtype)


def partial_rope(x, pos):
    half = ROT_DIM // 2
    inv_freq = ROPE_THETA ** (-jnp.arange(half, dtype=jnp.float32) / half)
    ang = pos.astype(jnp.float32)[:, None] * inv_freq[None, :]
    bshape = (pos.shape[0],) + (1,) * (x.ndim - 3) + (half,)
    cos = jnp.cos(ang).reshape(bshape)
    sin = jnp.sin(ang).reshape(bshape)
    xr = x[..., :ROT_DIM].astype(jnp.float32)
    x1, x2 = xr[..., :half], xr[..., half:]
    rot = jnp.concatenate([x1 * cos - x2 * sin, x2 * cos + x1 * sin], axis=-1)
    return jnp.concatenate([rot.astype(x.dtype), x[..., ROT_DIM:]], axis=-1)


def chunk_id(idx):
    return jnp.where(idx < N_META, 0, 1 + (idx - N_META) // CHUNK)


def project(x, w_in, b_f, pos):
    B, L, _ = x.shape
    z = jnp.einsum('bld,dn->bln', x, w_in)
    points = []
    acc = 0
    for n in SPLIT_SIZES[:-1]:
        acc += n
        points.append(acc)
    qa, ka, va, fa, qb, kb, vb, ga, gb = jnp.split(z, points, axis=-1)
    qa = qa.reshape(B, L, FOX_HEADS, FOX_HEAD_DIM)
    ka = ka.reshape(B, L, FOX_HEADS, FOX_HEAD_DIM)
    va = va.reshape(B, L, FOX_HEADS, FOX_HEAD_DIM)
    logf = jax.nn.log_sigmoid(fa.astype(jnp.float32) + b_f.astype(jnp.float32))
    qb = partial_rope(qb.reshape(B, L, DIFF_HEADS, 2, DIFF_HEAD_DIM), pos)
    kb = partial_rope(kb.reshape(B, L, DIFF_HEADS, 2, DIFF_HEAD_DIM), pos)
    vb = vb.reshape(B, L, DIFF_HEADS, DIFF_V_DIM)
    return qa, ka, va, logf, qb, kb, vb, ga, gb


def fox_block(q, cq, pos_q, k, v, ck, pos_k):
    s = jnp.einsum('bqhd,bkhd->bhqk', q, k, preferred_element_type=jnp.float32) * FOX_SCALE
    s = s + (jnp.swapaxes(cq, 1, 2)[..., :, None] - jnp.swapaxes(ck, 1, 2)[..., None, :])
    s = jnp.where(pos_k[None, :] <= pos_q[:, None], s, NEG)
    p = jax.nn.softmax(s, axis=-1)
    return jnp.einsum('bhqk,bkhd->bqhd', p.astype(v.dtype), v)


def diff_block(q, k, v, lam, allowed):
    s = jnp.einsum('bqhcd,bkhcd->bchqk', q, k, preferred_element_type=jnp.float32) * DIFF_SCALE
    if allowed is not None:
        s = jnp.where(allowed, s, NEG)
    p = jax.nn.softmax(s, axis=-1)
    a = p[:, 0] - lam * p[:, 1]
    return jnp.einsum('bhqk,bkhe->bqhe', a.astype(v.dtype), v)


def diff_lambda(layer, lq1, lk1, lq2, lk2):
    lam_init = 0.8 - 0.6 * math.exp(-0.3 * layer)
    d1 = jnp.sum(lq1.astype(jnp.float32) * lk1.astype(jnp.float32))
    d2 = jnp.sum(lq2.astype(jnp.float32) * lk2.astype(jnp.float32))
    return jnp.exp(d1) - jnp.exp(d2) + lam_init, lam_init


def diff_output(o, g, lam_init):
    B, L = o.shape[:2]
    of = o.astype(jnp.float32)
    of = of * lax.rsqrt(jnp.mean(of * of, axis=-1, keepdims=True) + RMS_EPS) * g.astype(jnp.float32)
    return (of * (1.0 - lam_init)).astype(o.dtype).reshape(B, L, DIFF_WIDTH)


def to_blocks(a):
    L = a.shape[1]
    nb = -(-L // Q_BLOCK)
    pad = [(0, 0)] * a.ndim
    pad[1] = (0, nb * Q_BLOCK - L)
    a = jnp.pad(a, pad)
    return jnp.moveaxis(a.reshape((a.shape[0], nb, Q_BLOCK) + a.shape[2:]), 1, 0)


def from_blocks(o, L):
    o = jnp.moveaxis(o, 0, 1)
    return o.reshape((o.shape[0], -1) + o.shape[3:])[:, :L]


def merge_ffn(x, oa, ob, ga, gb, w_br_a, w_br_b, w_out, ln1_g, ln1_b, w_up, w_down, ln2_g, ln2_b):
    m = jax.nn.sigmoid(ga) * (oa @ w_br_a) + jax.nn.sigmoid(gb) * (ob @ w_br_b)
    x = layer_norm(DEEPNORM_ALPHA * x + m @ w_out, ln1_g, ln1_b)
    hid = jnp.square(jax.nn.relu(x @ w_up)) @ w_down
    return layer_norm(DEEPNORM_ALPHA * x + hid, ln2_g, ln2_b)


def setup_inputs(seed: int = 0) -> dict:
    key = jax.random.key(seed)
    ks = jax.random.split(key, 32)
    f32 = jnp.float32

    def nrm(k, shape, scale):
        return scale * jax.random.normal(k, shape, f32)

    forget_profile = jnp.linspace(1.0, 6.0, FOX_HEADS, dtype=f32)
    return {
        'x_prompt': nrm(ks[0], (BATCH, SEQ, D_MODEL), 1.0),
        'x_sample': nrm(ks[1], (DEC_BATCH, DEC_SEQ, D_MODEL), 1.0),
        'cache_fox_k': nrm(ks[2], (DEPTH, DEC_BATCH, PAST_LEN, FOX_HEADS, FOX_HEAD_DIM), 1.0),
        'cache_fox_v': nrm(ks[3], (DEPTH, DEC_BATCH, PAST_LEN, FOX_HEADS, FOX_HEAD_DIM), 1.0),
        'cache_fox_logf': jax.nn.log_sigmoid(nrm(ks[4], (DEPTH, DEC_BATCH, PAST_LEN, FOX_HEADS), 1.0) + forget_profile),
        'cache_diff_k': nrm(ks[5], (DEPTH, DEC_BATCH, PAST_LEN, DIFF_HEADS, 2 * DIFF_HEAD_DIM), 1.0),
        'cache_diff_v': nrm(ks[6], (DEPTH, DEC_BATCH, PAST_LEN, DIFF_HEADS, DIFF_V_DIM), 1.0),
        'meta_tokens': nrm(ks[7], (N_META, D_MODEL), 1.0),
        'ln_in_g': 1.0 + nrm(ks[8], (D_MODEL,), 0.02),
        'ln_in_b': nrm(ks[9], (D_MODEL,), 0.02),
        'w_in': nrm(ks[10], (DEPTH, D_MODEL, N_IN), D_MODEL ** -0.5),
        'b_f': forget_profile + nrm(ks[11], (DEPTH, FOX_HEADS), 0.1),
        'lambda_q1': nrm(ks[12], (DEPTH, DIFF_HEAD_DIM), 0.1),
        'lambda_k1': nrm(ks[13], (DEPTH, DIFF_HEAD_DIM), 0.1),
        'lambda_q2': nrm(ks[14], (DEPTH, DIFF_HEAD_DIM), 0.1),
        'lambda_k2': nrm(ks[15], (DEPTH, DIFF_HEAD_DIM), 0.1),
        'subln_g': 1.0 + nrm(ks[16], (DEPTH, DIFF_V_DIM), 0.02),
        'w_br_a': nrm(ks[17], (DEPTH, FOX_WIDTH, D_MODEL), DEEPNORM_BETA * FOX_WIDTH ** -0.5),
        'w_br_b': nrm(ks[18], (DEPTH, DIFF_WIDTH, D_MODEL), DEEPNORM_BETA * DIFF_WIDTH ** -0.5),
        'w_out': nrm(ks[19], (DEPTH, D_MODEL, D_MODEL), DEEPNORM_BETA * D_MODEL ** -0.5),
        'ln1_g': 1.0 + nrm(ks[20], (DEPTH, D_MODEL), 0.02),
        'ln1_b': nrm(ks[21], (DEPTH, D_MODEL), 0.02),
        'w_up': nrm(ks[22], (DEPTH, D_MODEL, D_FF), D_MODEL ** -0.5),
        'w_down': nrm(ks[23], (DEPTH, D_FF, D_MODEL), DEEPNORM_BETA * D_FF ** -0.5),
        'ln2_g': 1.0 + nrm(ks[24], (DEPTH, D_MODEL), 0.02),
        'ln2_b': nrm(ks[25], (DEPTH, D_MODEL), 0.02),
    }


def reference(x_prompt, x_sample, cache_fox_k, cache_fox_v, cache_fox_logf, cache_diff_k, cache_diff_v,
              meta_tokens, ln_in_g, ln_in_b, w_in, b_f, lambda_q1, lambda_k1, lambda_q2, lambda_k2,
              subln_g, w_br_a, w_br_b, w_out, ln1_g, ln1_b, w_up, w_down, ln2_g, ln2_b):
    B = x_prompt.shape[0]
    Bd, T = x_sample.shape[0], x_sample.shape[1]
    P = cache_fox_k.shape[2]

    meta = jnp.broadcast_to(meta_tokens[None].astype(x_prompt.dtype), (B, N_META, D_MODEL))
    h = layer_norm(jnp.concatenate([meta, x_prompt], axis=1), ln_in_g, ln_in_b)
    L = h.shape[1]
    n_blocks = -(-L // Q_BLOCK)
    pos_p = jnp.arange(L)
    qidx = jnp.arange(n_blocks * Q_BLOCK).reshape(n_blocks, Q_BLOCK)
    chunk_p = chunk_id(pos_p)
    chunk_qb = chunk_id(qidx)

    s = layer_norm(x_sample, ln_in_g, ln_in_b)
    pos_s = P + jnp.arange(T)
    pos_sk = jnp.arange(P + T)

    fk_p, fv_p, fl_p, dk_p, dv_p = [], [], [], [], []
    fk_s, fv_s, fl_s, dk_s, dv_s = [], [], [], [], []
    for l in range(DEPTH):
        lam, lam_init = diff_lambda(l, lambda_q1[l], lambda_k1[l], lambda_q2[l], lambda_k2[l])

        qa, ka, va, logf, qb, kb, vb, ga, gb = project(h, w_in[l], b_f[l], pos_p)
        c = jnp.cumsum(logf, axis=1)
        oa = lax.map(lambda a: fox_block(a[0], a[1], a[2], ka, va, c, pos_p),
                     (to_blocks(qa), to_blocks(c), qidx))
        oa = from_blocks(oa, L).reshape(B, L, FOX_WIDTH)
        ob = lax.map(lambda a: diff_block(a[0], kb, vb, lam, chunk_p[None, :] <= a[1][:, None]),
                     (to_blocks(qb), chunk_qb))
        ob = diff_output(from_blocks(ob, L), subln_g[l], lam_init)
        fk_p.append(ka)
        fv_p.append(va)
        fl_p.append(logf)
        dk_p.append(kb.reshape(B, L, DIFF_HEADS, 2 * DIFF_HEAD_DIM))
        dv_p.append(vb)
        h = merge_ffn(h, oa, ob, ga, gb, w_br_a[l], w_br_b[l], w_out[l], ln1_g[l], ln1_b[l],
                      w_up[l], w_down[l], ln2_g[l], ln2_b[l])

        qa2, ka2, va2, logf2, qb2, kb2, vb2, ga2, gb2 = project(s, w_in[l], b_f[l], pos_s)
        k_all = jnp.concatenate([cache_fox_k[l], ka2], axis=1)
        v_all = jnp.concatenate([cache_fox_v[l], va2], axis=1)
        c_all = jnp.cumsum(jnp.concatenate([cache_fox_logf[l].astype(jnp.float32), logf2], axis=1), axis=1)
        oa2 = fox_block(qa2, c_all[:, P:], pos_s, k_all, v_all, c_all, pos_sk).reshape(Bd, T, FOX_WIDTH)
        kb_all = jnp.concatenate([cache_diff_k[l].reshape(Bd, P, DIFF_HEADS, 2, DIFF_HEAD_DIM), kb2], axis=1)
        vb_all = jnp.concatenate([cache_diff_v[l], vb2], axis=1)
        ob2 = diff_output(diff_block(qb2, kb_all, vb_all, lam, None), subln_g[l], lam_init)
        fk_s.append(ka2)
        fv_s.append(va2)
        fl_s.append(logf2)
        dk_s.append(kb2.reshape(Bd, T, DIFF_HEADS, 2 * DIFF_HEAD_DIM))
        dv_s.append(vb2)
        s = merge_ffn(s, oa2, ob2, ga2, gb2, w_br_a[l], w_br_b[l], w_out[l], ln1_g[l], ln1_b[l],
                      w_up[l], w_down[l], ln2_g[l], ln2_b[l])

    y_prompt = h[:, N_META:]
    y_sample = s
    return (y_prompt, y_sample,
            jnp.stack(fk_p), jnp.stack(fv_p), jnp.stack(fl_p), jnp.stack(dk_p), jnp.stack(dv_p),
            jnp.stack(fk_s), jnp.stack(fv_s), jnp.stack(fl_s), jnp.stack(dk_s), jnp.stack(dv_s))
```

```python
import contextlib
import numpy as np
import concourse.bass as bass
import concourse.mybir as mybir
from concourse.bass_utils import run_bass_kernel_spmd

F32 = mybir.dt.float32
BF16 = mybir.dt.bfloat16
ALU = mybir.AluOpType
ACTF = mybir.ActivationFunctionType
AX = mybir.AxisListType


class Buf:
    __slots__ = ("name", "ap", "kind", "writers", "readers", "sem", "semcnt")

    def __init__(self, name, ap=None, kind="sbuf"):
        self.name = name
        self.ap = ap
        self.kind = kind
        self.writers = []
        self.readers = []
        self.sem = None
        self.semcnt = 0


class Op:
    __slots__ = ("eng", "emit", "deps", "is_dma", "sem", "val", "sig", "idx", "rawdeps")

    def __init__(self, eng, emit):
        self.eng = eng
        self.emit = emit
        self.deps = []
        self.rawdeps = set()
        self.is_dma = False
        self.sem = None
        self.val = 0
        self.sig = False


class Sched:
    ENGS = ("pe", "act", "dve", "pool", "sp")

    def __init__(self, nc):
        self.nc = nc
        self.stack = contextlib.ExitStack()
        self.ops = {e: [] for e in self.ENGS}
        self.all_ops = []
        self.esem = {}
        for e in ("pe", "act", "dve", "pool"):
            self.esem[e] = self.stack.enter_context(nc.semaphore("sem_" + e))
        self.dma_bufs = []
        self.free_sems = []
        self.nsem = 4
        self.n_names = 0

    def sbuf(self, name, shape, dtype):
        t = self.stack.enter_context(self.nc.sbuf_tensor(name, list(shape), dtype))
        return Buf(name, t.ap() if hasattr(t, "ap") and callable(t.ap) else t[:], "sbuf")

    def psum(self, name, shape, dtype):
        t = self.stack.enter_context(self.nc.psum_tensor(name, list(shape), dtype))
        return Buf(name, t.ap() if hasattr(t, "ap") and callable(t.ap) else t[:], "psum")

    def dram(self, name, shape, dtype, kind="Internal"):
        return self.nc.dram_tensor(name, list(shape), dtype, kind=kind).ap()

    def tok(self, name):
        return Buf(name, None, "dram")

    def sub(self, buf, name=None):
        return Buf(name or buf.name + "_s", buf.ap, buf.kind)

    def _record(self, op, reads, writes):
        deps = op.deps
        for r in reads:
            for w_ in r.writers:
                deps.append(w_)
                op.rawdeps.add(id(w_))
            if r.kind == "psum":
                deps.extend(r.readers)
        for w in writes:
            if w.kind == "dram" and not w.readers:
                continue
            deps.extend(w.writers)
            deps.extend(w.readers)
        for r in reads:
            r.readers.append(op)
        for w in writes:
            if w.kind == "dram" and not w.readers:
                w.writers.append(op)
            else:
                w.writers = [op]
                w.readers = []
        op.idx = len(self.ops[op.eng])
        self.ops[op.eng].append(op)
        self.all_ops.append(op)

    def op(self, eng, emit, reads=(), writes=()):
        o = Op(eng, emit)
        self._record(o, reads, writes)
        return o

    def dma(self, out, in_, reads=(), writes=(), eng="sp", **kw):
        o = Op(eng, lambda e: e.dma_start(out=out, in_=in_, **kw))
        o.is_dma = True
        owner = None
        for b in list(writes) + list(reads):
            if b.kind == "sbuf":
                owner = b
                break
        if owner is None:
            owner = (list(writes) + list(reads))[0]
        if owner.sem is None:
            if self.free_sems:
                owner.sem, owner.semcnt = self.free_sems.pop()
            else:
                owner.sem = self.stack.enter_context(self.nc.semaphore("dsem%d" % self.nsem))
                self.nsem += 1
            self.dma_bufs.append(owner)
        owner.semcnt += 16
        o.sem = owner.sem
        o.val = owner.semcnt
        o.sig = True
        self._record(o, reads, writes)
        return o

    @staticmethod
    def _needs_wait(op, d):
        if d.is_dma:
            return True
        if d.eng != op.eng:
            return True
        if op.eng == "pe":
            return False
        return id(d) in op.rawdeps

    def finish(self):
        nc = self.nc
        for op in self.all_ops:
            for d in op.deps:
                if not d.is_dma and self._needs_wait(op, d):
                    d.sig = True
        for e in ("pe", "act", "dve", "pool"):
            c = 0
            for op in self.ops[e]:
                if op.is_dma:
                    continue
                if op.sig:
                    c += 1
                    op.val = c
                    op.sem = self.esem[e]
        fin = {}
        for b in self.dma_bufs:
            if id(b.sem) not in fin or fin[id(b.sem)][1] < b.semcnt:
                fin[id(b.sem)] = (b.sem, b.semcnt)
        finals = list(fin.values())
        self.nwaits = 0

        def emit_engine(ename, eng):
            seen = {}
            for op in self.ops[ename]:
                need = {}
                for d in op.deps:
                    if not self._needs_wait(op, d):
                        continue
                    k = id(d.sem)
                    if seen.get(k, 0) >= d.val:
                        continue
                    if k not in need or need[k][1] < d.val:
                        need[k] = (d.sem, d.val)
                for k, (sem, val) in need.items():
                    eng.wait_ge(sem, val)
                    seen[k] = val
                    self.nwaits += 1
                inst = op.emit(eng)
                if op.sig:
                    inst.then_inc(op.sem, 16 if op.is_dma else 1)
            if ename == "sp":
                for sem, val in finals:
                    eng.wait_ge(sem, val)

        with nc.Block() as block:
            @block.tensor
            def _(e):
                emit_engine("pe", e)

            @block.scalar
            def _(e):
                emit_engine("act", e)

            @block.vector
            def _(e):
                emit_engine("dve", e)

            @block.gpsimd
            def _(e):
                emit_engine("pool", e)

            @block.sync
            def _(e):
                emit_engine("sp", e)
        self.stack.close()


class Arena:
    def __init__(self, S, nbytes):
        self.S = S
        self.n32 = nbytes // 4
        self.base = S.sbuf("arena", [128, self.n32], F32)
        self.off = 0
        self.cur = []
        self.prev_ops = []

    def reset(self):
        ops = {}
        for b in self.cur:
            for w_ in b.writers:
                ops[id(w_)] = w_
            for r in b.readers:
                ops[id(r)] = r
        self.prev_ops = list(ops.values())
        for b in self.cur:
            if b.sem is not None:
                self.S.free_sems.append((b.sem, b.semcnt))
        self.cur = []
        self.off = 0

    def alloc(self, name, free_shape, dtype):
        esz = 4 if dtype == F32 else 2
        n = 1
        for s in free_shape:
            n *= s
        n32 = (n * esz + 3) // 4
        n32 = (n32 + 7) // 8 * 8
        assert self.off + n32 <= self.n32, "arena overflow %s: %d + %d > %d" % (name, self.off, n32, self.n32)
        ap = self.base.ap[:, self.off:self.off + n32]
        if dtype != F32:
            ap = ap.bitcast(dtype)
        ap = ap[:, 0:n]
        if len(free_shape) == 2:
            ap = ap.rearrange("p (a b) -> p a b", a=free_shape[0])
        elif len(free_shape) == 3:
            ap = ap.rearrange("p (a b c) -> p a b c", a=free_shape[0], b=free_shape[1])
        self.off += n32
        b = Buf(name, ap, "sbuf")
        b.readers = list(self.prev_ops)
        self.cur.append(b)
        return b


D = 2048
KC = 16
NF = 4096
NM = 16
L = NM + NF
TS = 64
PAST = 4096
DEPTH = 2
C = 128 + NF
NIN = 10248
DFF = 8192
ALPHA = (2 * DEPTH) ** 0.25
FOX_SCALE = 128 ** -0.5
DIFF_SCALE = 128 ** -0.5
NEG = -1e30
GROUPS = [("qa", 0, 1024), ("ka", 1024, 1024), ("va", 2048, 1024), ("fa", 3072, 8),
          ("qb", 3080, 1024), ("kb", 4104, 1024), ("vb", 5128, 1024), ("ga", 6152, 2048), ("gb", 8200, 2048)]


def supertiles():
    sts = [(0, [(0, 80)])]
    for j in range(8):
        sts.append((128 + 512 * j, [(128 * i, 128) for i in range(4)]))
    return sts


class WStream:
    def __init__(self, S, nbuf, bufs):
        self.S = S
        self.bufs = bufs
        self.nbuf = nbuf
        self.order = []
        self.next_load = 0
        self.next_use = 0

    def plan(self, pieces):
        self.order = pieces

    def _issue(self, i):
        name, ap, tok = self.order[i]
        b = self.bufs[i % self.nbuf]
        kcp, ncols = ap.shape[1], ap.shape[2]
        self.S.dma(b.ap[:, 0:kcp, 0:ncols], ap, reads=list(tok), writes=[b])

    def get(self, name):
        i = self.next_use
        assert self.order[i][0] == name, (self.order[i][0], name)
        while self.next_load < len(self.order) and self.next_load < i + self.nbuf:
            self._issue(self.next_load)
            self.next_load += 1
        self.next_use += 1
        return self.bufs[i % self.nbuf]


class WMat:
    def __init__(self, S, name, src, bw=512):
        self.name = name
        self.src = src
        self.K, self.N = src.shape
        self.kc = self.K // 128
        self.bw = min(bw, self.N)
        self.nblk = self.N // self.bw
        self.scr = S.dram(name + "_bf", [self.nblk, 128, self.kc, self.bw], BF16)
        self.cp = min(self.N, 2048)
        self.ncp = self.N // self.cp
        self.toks = [[S.tok("%s_t%d_%d" % (name, k, c)) for c in range(self.ncp)] for k in range(self.kc)]

    def piece(self, blk, kc0=0, kcp=None):
        kcp = self.kc if kcp is None else kcp
        cpi = (blk * self.bw) // self.cp
        toks = [self.toks[k][cpi] for k in range(kc0, kc0 + kcp)]
        return ("%s_b%d_k%d" % (self.name, blk, kc0), self.scr[blk, :, kc0:kc0 + kcp, :], toks)


def cast_weights(S, mats, stage32, stage16):
    i = 0
    engs = ("act", "dve", "pool")
    for m in mats:
        for k in range(m.kc):
            for c in range(m.ncp):
                s32 = stage32[i % len(stage32)]
                s16 = stage16[i % len(stage16)]
                w = m.cp
                S.dma(s32.ap[:, 0:w], m.src[k * 128:(k + 1) * 128, c * w:(c + 1) * w], writes=[s32])
                e = engs[i % 3]
                if e == "act":
                    S.op("act", lambda en, s32=s32, s16=s16, w=w: en.activation(out=s16.ap[:, 0:w], in_=s32.ap[:, 0:w], func=ACTF.Copy),
                         reads=[s32], writes=[s16])
                else:
                    S.op(e, lambda en, s32=s32, s16=s16, w=w: en.tensor_copy(out=s16.ap[:, 0:w], in_=s32.ap[:, 0:w]),
                         reads=[s32], writes=[s16])
                nb = w // m.bw
                b0 = (c * w) // m.bw
                S.dma(m.scr[b0:b0 + nb, :, k, :].rearrange("b p n -> p b n"),
                      s16.ap[:, 0:w].rearrange("p (b n) -> p b n", b=nb), reads=[s16], writes=[m.toks[k][c]])
                i += 1


def out_rows(c, rows):
    if c < 128:
        return [(0, 64, "s", 0), (64, 80, "p", 0)]
    return [(0, rows, "p", c - 128 + NM)]


class K:
    pass


def build(stop_after=None):
    nc = bass.Bass("TRN2", target_bir_lowering=False)
    S = Sched(nc)
    k = K()

    def I(name, shape):
        return nc.dram_tensor(name, list(shape), F32, kind="ExternalInput").ap()

    def O(name, shape):
        return nc.dram_tensor(name, list(shape), F32, kind="ExternalOutput").ap()

    x_p = I("x_p", [NF, D]); x_s = I("x_s", [TS, D]); meta = I("meta", [NM, D])
    cfk = I("cfk", [DEPTH, PAST, 1024]); cfv = I("cfv", [DEPTH, PAST, 1024]); cfl = I("cfl", [DEPTH, PAST, 8])
    cdk = I("cdk", [DEPTH, PAST, 1024]); cdv = I("cdv", [DEPTH, PAST, 1024])
    ln_in_g = I("ln_in_g", [D]); ln_in_b = I("ln_in_b", [D])
    w_in = I("w_in", [DEPTH, D, NIN]); b_f = I("b_f", [DEPTH, 8])
    lq1 = I("lq1", [DEPTH, 128]); lk1 = I("lk1", [DEPTH, 128]); lq2 = I("lq2", [DEPTH, 128]); lk2 = I("lk2", [DEPTH, 128])
    subg = I("subg", [DEPTH, 256])
    w_br_a = I("w_br_a", [DEPTH, 1024, D]); w_br_b = I("w_br_b", [DEPTH, 1024, D]); w_out = I("w_out", [DEPTH, D, D])
    ln1_g = I("ln1_g", [DEPTH, D]); ln1_b = I("ln1_b", [DEPTH, D])
    w_up = I("w_up", [DEPTH, D, DFF]); w_down = I("w_down", [DEPTH, DFF, D])
    ln2_g = I("ln2_g", [DEPTH, D]); ln2_b = I("ln2_b", [DEPTH, D])
    c_ident = I("c_ident", [128, 128]); c_cmask = I("c_cmask", [128, 128]); c_kmask = I("c_kmask", [128, 128])
    c_cos = I("c_cos", [C, 16]); c_sin = I("c_sin", [C, 16])

    y_p = O("y_p", [NF, D]); y_s = O("y_s", [TS, D])
    outs = {}
    for nm, w in (("fk", 1024), ("fv", 1024), ("fl", 8), ("dk", 1024), ("dv", 1024)):
        outs[nm + "p"] = O(nm + "_p", [DEPTH, L, w])
        outs[nm + "s"] = O(nm + "_s", [DEPTH, TS, w])

    import os
    hbuf = S.dram("hbuf", [C, D], F32, kind=("ExternalOutput" if os.environ.get("DBG_DUMP") else "Internal"))
    hT = S.dram("hT", [128, KC, C], BF16)
    scrT = {n: S.dram(n + "T", [128, 8, C + PAST], BF16) for n in ("qa", "ka", "qb", "kb")}
    vA = S.dram("vA", [C + PAST, 8 * 130], BF16)
    vB = S.dram("vB", [C + PAST, 4 * 258], BF16)
    gsc = S.dram("gsc", [C, 4096], BF16)
    import os
    dk_ = "ExternalOutput" if os.environ.get("DBG_DUMP") else "Internal"
    oaT = S.dram("oaT", [128, 8, C], BF16, kind=dk_)
    obT = S.dram("obT", [128, 8, C], BF16, kind=dk_)
    t_oa = [S.tok("oa_st%d" % i) for i in range(9)]
    t_ob = [S.tok("ob_st%d" % i) for i in range(9)]
    lfbuf = S.dram("lfbuf", [C, 8], F32)
    lfT_d = S.dram("lfT_d", [8, C], F32)
    cbuf = S.dram("cbuf", [8, C], F32, kind=("ExternalOutput" if os.environ.get("DBG_DUMP") else "Internal"))
    t_lfT = S.tok("lfT_tok")
    t_cbuf = S.tok("cbuf_tok")
    t_cache = {n: S.tok("cache_" + n) for n in ("ka", "kb", "vA", "vB")}
    t_h = [S.tok("h_st%d" % i) for i in range(9)]
    t_hT = [S.tok("hT_st%d" % i) for i in range(9)]
    t_scr = {n: [S.tok("%s_st%d" % (n, i)) for i in range(9)] for n in ("qa", "ka", "qb", "kb", "vA", "vB", "gs", "lf")}

    identf = S.sbuf("identf", [128, 128], F32)
    identb = S.sbuf("identb", [128, 128], BF16)
    cmask = S.sbuf("cmask", [128, 128], F32)
    kmask = S.sbuf("kmask", [128, 128], F32)
    lnp = [S.sbuf("lnp%d" % i, [128, D], F32) for i in range(4)]
    bfb = S.sbuf("bfb", [128, 8], F32)
    nstat = 4
    st_bn = [S.sbuf("st_bn%d" % i, [128, 4, 6], F32) for i in range(nstat)]
    st_mv = [S.sbuf("st_mv%d" % i, [128, 2], F32) for i in range(nstat)]
    st_rs = [S.sbuf("st_rs%d" % i, [128, 1], F32) for i in range(nstat)]
    k.stat_i = 0
    banks = [S.psum("bank%d" % i, [128, 512], F32) for i in range(8)]
    banks16 = [Buf("bank%d_16" % i, b.ap.bitcast(BF16), "psum") for i, b in enumerate(banks)]
    for b16, b in zip(banks16, banks):
        pass
    k.tr_i = 0
    k.bank_i = 0
    A = Arena(S, 170 * 1024)
    negc = S.sbuf("negc", [128, 66, 8], F32)
    nlam = S.sbuf("nlam", [128, 1], F32)
    gsb = S.sbuf("gsb", [128, 256], F32)

    S.dma(identf.ap, c_ident, writes=[identf])
    S.op("dve", lambda e: e.tensor_copy(out=identb.ap, in_=identf.ap), reads=[identf], writes=[identb])
    S.dma(cmask.ap, c_cmask, writes=[cmask])
    S.dma(kmask.ap, c_kmask, writes=[kmask])

    def trbank():
        i = 6 + (k.tr_i % 2)
        k.tr_i += 1
        return banks[i], banks16[i].ap

    def transpose_chunks(src_buf, src_fn, n, rows, dst_buf, dst_ap, eng):
        pb, p16 = trbank()
        for c in range(n):
            S.op("pe", lambda e, c=c: e.transpose(p16[:, c * 128:c * 128 + rows], src_fn(c), identb.ap[:rows, :rows]),
                 reads=[src_buf, identb], writes=[pb])
        src = p16[:, 0:n * 128].rearrange("p (c r) -> p c r", c=n)[:, :, 0:rows]
        if eng == "act":
            S.op("act", lambda e: e.activation(out=dst_ap, in_=src, func=ACTF.Copy), reads=[pb], writes=[dst_buf])
        else:
            S.op(eng, lambda e: e.tensor_copy(out=dst_ap, in_=src), reads=[pb], writes=[dst_buf])

    def layernorm(xb, x_ap, rows, g_b, b_b, out_b, out_ap):
        i = k.stat_i % nstat
        k.stat_i += 1
        bn, mv, rs = st_bn[i], st_mv[i], st_rs[i]
        for c in range(4):
            S.op("dve", lambda e, c=c: e.bn_stats(out=bn.ap[:rows, c, :], in_=x_ap[:, c * 512:(c + 1) * 512]), reads=[xb], writes=[bn])
        S.op("dve", lambda e: e.bn_aggr(out=mv.ap[:rows], in_=bn.ap[:rows]), reads=[bn], writes=[mv])
        S.op("dve", lambda e: e.tensor_scalar(out=rs.ap[:rows], in0=mv.ap[:rows, 1:2], scalar1=1e-5, scalar2=None, op0=ALU.add),
             reads=[mv], writes=[rs])
        S.op("act", lambda e: e.activation(out=rs.ap[:rows], in_=rs.ap[:rows], func=ACTF.Ln), reads=[rs], writes=[rs])
        S.op("act", lambda e: e.activation(out=rs.ap[:rows], in_=rs.ap[:rows], func=ACTF.Exp, scale=-0.5), reads=[rs], writes=[rs])
        S.op("dve", lambda e: e.tensor_scalar(out=out_ap, in0=x_ap, scalar1=mv.ap[:rows, 0:1], scalar2=rs.ap[:rows],
                                              op0=ALU.subtract, op1=ALU.mult), reads=[xb, mv, rs], writes=[out_b])
        S.op("pool", lambda e: e.tensor_tensor(out=out_ap, in0=out_ap, in1=g_b.ap[:rows], op=ALU.mult), reads=[out_b, g_b], writes=[out_b])
        S.op("pool", lambda e: e.tensor_tensor(out=out_ap, in0=out_ap, in1=b_b.ap[:rows], op=ALU.add), reads=[out_b, b_b], writes=[out_b])

    mats = {}
    for l in range(DEPTH):
        for g, off, w in GROUPS:
            mats[(l, g)] = WMat(S, "w%d%s" % (l, g), w_in[l][:, off:off + w])
        mats[(l, "bra")] = WMat(S, "w%dbra" % l, w_br_a[l])
        mats[(l, "brb")] = WMat(S, "w%dbrb" % l, w_br_b[l])
        mats[(l, "out")] = WMat(S, "w%dout" % l, w_out[l])
        mats[(l, "up")] = WMat(S, "w%dup" % l, w_up[l])
        mats[(l, "down")] = WMat(S, "w%ddown" % l, w_down[l])
    k.mats = mats

    A.reset()
    s32 = [A.alloc("s32_%d" % i, [2048], F32) for i in range(3)]
    s16 = [A.alloc("s16_%d" % i, [2048], BF16) for i in range(3)]
    order = []
    for l in range(DEPTH):
        order += [mats[(l, g)] for g, _, _ in GROUPS] + [mats[(l, n)] for n in ("bra", "brb", "out", "up", "down")]
    if stop_after in ("A0", "P0", "PRE"):
        order = [mats[(0, g)] for g, _, _ in GROUPS]
    cast_weights(S, order, s32, s16)
    if stop_after == "P0":
        S.finish()
        return nc

    sts = supertiles()

    A.reset()
    xin = [A.alloc("xin%d" % i, [D], F32) for i in range(2)]
    xn16 = [A.alloc("xn16_%d" % i, [D], BF16) for i in range(2)]
    hTst = A.alloc("hTst", [KC, 512], BF16)
    S.dma(lnp[0].ap, ln_in_g.partition_broadcast(128), writes=[lnp[0]])
    S.dma(lnp[1].ap, ln_in_b.partition_broadcast(128), writes=[lnp[1]])
    it = 0
    for si, (c0, subs) in enumerate(sts):
        T = subs[-1][0] + subs[-1][1]
        for (off, rows) in subs:
            xb = xin[it % 2]; x16 = xn16[it % 2]; it += 1
            if c0 == 0:
                S.dma(xb.ap[0:64], x_s, writes=[xb])
                S.dma(xb.ap[64:80], meta, writes=[xb])
            else:
                f0 = c0 - 128 + off
                S.dma(xb.ap[:rows], x_p[f0:f0 + rows], writes=[xb])
            layernorm(xb, xb.ap[:rows], rows, lnp[0], lnp[1], xb, xb.ap[:rows])
            S.dma(hbuf[c0 + off:c0 + off + rows], xb.ap[:rows], reads=[xb], writes=[t_h[si]])
            S.op("act", lambda e, xb=xb, x16=x16, rows=rows: e.activation(out=x16.ap[:rows], in_=xb.ap[:rows], func=ACTF.Copy),
                 reads=[xb], writes=[x16])
            for hf in range(2):
                transpose_chunks(x16, lambda c, x16=x16, rows=rows, hf=hf: x16.ap[:rows, (hf * 8 + c) * 128:(hf * 8 + c + 1) * 128],
                                 8, rows, hTst, hTst.ap[:, hf * 8:hf * 8 + 8, off:off + rows], "dve" if hf == 0 else "act")
        S.dma(hT[:, :, c0:c0 + T], hTst.ap[:, :, 0:T], reads=[hTst], writes=[t_hT[si]])

    if stop_after == "PRE":
        S.finish()
        return nc
    k.S = S; k.A = A; k.nc = nc
    k.__dict__.update(locals())
    for l in range(DEPTH):
        phase_A(k, l)
        if stop_after == "A0":
            break
        phase_B(k, l)
        if stop_after == "B0":
            break
        phase_C(k, l)
        if stop_after == "C0":
            break
    S.finish()
    return nc


def dense(k, xT_b, xT_fn, kc_total, mat, blocks, subs, evac, kc_piece=16, nbanks=6):
    S = k.S
    bw = mat.bw
    npiece = kc_total // kc_piece
    for blk in blocks:
        pbs = []
        for j in range(len(subs)):
            pbs.append(k.banks[k.bank_i % nbanks])
            k.bank_i += 1
        for pi in range(npiece):
            wb = k.ws.get(mat.piece(blk, pi * kc_piece, kc_piece)[0])
            for j, (off, rows) in enumerate(subs):
                pb = pbs[j]
                for kk in range(kc_piece):
                    kg = pi * kc_piece + kk
                    S.op("pe", lambda e, pb=pb, rows=rows, off=off, kg=kg, kk=kk, wb=wb: e.matmul(
                        pb.ap[:rows, :bw], lhsT=xT_fn(kg, off, rows), rhs=wb.ap[:, kk, :bw],
                        start=(kg == 0), stop=(kg == kc_total - 1)), reads=[xT_b, wb], writes=[pb])
                if pi == npiece - 1:
                    evac(j, blk, pb, off, rows)


def phase_A(k, l):
    S, A = k.S, k.A
    sts = k.sts
    outs = k.outs
    A.reset()
    xT = [A.alloc("xT%d" % i, [KC, 512], BF16) for i in range(2)]
    wbufs = [A.alloc("wbuf%d" % i, [KC, 512], BF16) for i in range(2)]
    stT = {n: A.alloc("stT_" + n, [8, 512], BF16) for n in ("qa", "ka", "qb", "kb")}
    ev32 = [A.alloc("ev32_%d" % i, [512], F32) for i in range(3)]
    tm16 = [A.alloc("tm16_%d" % i, [512], BF16) for i in range(3)]
    vAt = [A.alloc("vAt%d" % i, [8, 130], BF16) for i in range(4)]
    vBt = [A.alloc("vBt%d" % i, [4, 258], BF16) for i in range(4)]
    gt = [A.alloc("gt%d" % i, [512], BF16) for i in range(3)]
    cs = [A.alloc("cs%d" % i, [2, 16], F32) for i in range(4)]
    rt = [A.alloc("rt%d" % i, [4, 4, 16], F32) for i in range(2)]
    lf = [A.alloc("lf%d" % i, [4, 8], F32) for i in range(2)]
    lft = [A.alloc("lft%d" % i, [128], F32) for i in range(2)]
    k.lft_i = 0
    k.ev_i = 0; k.tm_i = 0; k.gt_i = 0; k.rt_i = 0; k.lf_i = 0; k.ce = 0
    import os
    k.dbgva = os.environ.get("DBG_VA", "")
    for t in vAt + vBt:
        if "nomemset" in k.dbgva:
            break
        S.op("pool", lambda e, t=t: e.memset(t.ap, 1.0), writes=[t])
    S.dma(k.bfb.ap, k.b_f[l:l + 1, :].partition_broadcast(128) if False else k.b_f[l].partition_broadcast(128), writes=[k.bfb])

    ws = WStream(S, 2, wbufs)
    order = []
    import os
    k.groups = [g for g in GROUPS if g[0] in os.environ.get("DBG_GROUPS", "qa,ka,va,fa,qb,kb,vb,ga,gb").split(",")]
    for si in range(len(sts)):
        for g, off, w in k.groups:
            m = k.mats[(l, g)]
            for blk in range(m.nblk):
                order.append(m.piece(blk))
    ws.plan(order)
    k.ws = ws

    def outdma(name, c, rows, src_b, src_ap, col0, ncols):
        for (p0, p1, kind, r0) in out_rows(c, rows):
            dst = outs[name + kind][l, r0:r0 + (p1 - p0), col0:col0 + ncols]
            S.dma(dst, src_ap[p0:p1], reads=[src_b])

    def nxt(lst, attr):
        i = getattr(k, attr)
        setattr(k, attr, i + 1)
        return lst[i % len(lst)]

    def copy_eng():
        k.ce += 1
        return "act" if k.ce % 2 else "dve"

    for si, (c0, subs) in enumerate(sts):
        T = subs[-1][0] + subs[-1][1]
        xb = xT[si % 2]
        S.dma(xb.ap[:, :, 0:T], k.hT[:, :, c0:c0 + T], reads=[k.t_hT[si]], writes=[xb])
        xfn = lambda kg, off, rows, xb=xb: xb.ap[:, kg, off:off + rows]
        cst = []
        for j, (off, rows) in enumerate(subs):
            t = cs[j]
            S.dma(t.ap[:rows, 0, :], k.c_cos[c0 + off:c0 + off + rows], writes=[t])
            S.dma(t.ap[:rows, 1, :], k.c_sin[c0 + off:c0 + off + rows], writes=[t])
            cst.append(t)

        def ev_q(name):
            def f(j, blk, pb, off, rows):
                tm = nxt(tm16, "tm_i")
                S.op("act", lambda e: e.activation(out=tm.ap[:rows], in_=pb.ap[:rows], func=ACTF.Copy), reads=[pb], writes=[tm])
                k.transpose_chunks(tm, lambda c: tm.ap[:rows, c * 128:(c + 1) * 128], 4, rows, stT[name],
                                   stT[name].ap[:, 4 * blk:4 * blk + 4, off:off + rows], "dve")
            return f

        def ev_ka(j, blk, pb, off, rows):
            ev = nxt(ev32, "ev_i"); tm = nxt(tm16, "tm_i")
            S.op("act", lambda e: e.activation(out=ev.ap[:rows], in_=pb.ap[:rows], func=ACTF.Copy), reads=[pb], writes=[ev])
            outdma("fk", c0 + off, rows, ev, ev.ap, 512 * blk, 512)
            S.op("dve", lambda e: e.tensor_copy(out=tm.ap[:rows], in_=pb.ap[:rows]), reads=[pb], writes=[tm])
            k.transpose_chunks(tm, lambda c: tm.ap[:rows, c * 128:(c + 1) * 128], 4, rows, stT["ka"],
                               stT["ka"].ap[:, 4 * blk:4 * blk + 4, off:off + rows], "act")

        def ev_va(j, blk, pb, off, rows):
            ev = nxt(ev32, "ev_i")
            S.op("dve", lambda e: e.tensor_copy(out=ev.ap[:rows], in_=pb.ap[:rows]), reads=[pb], writes=[ev])
            outdma("fv", c0 + off, rows, ev, ev.ap, 512 * blk, 512)
            vt = vAt[j]
            S.op("act", lambda e: e.activation(out=vt.ap[:rows, 4 * blk:4 * blk + 4, 0:128],
                                               in_=pb.ap[:rows].rearrange("p (h d) -> p h d", h=4), func=ACTF.Copy), reads=[pb], writes=[vt])
            if blk == 1 and "nodma" not in k.dbgva:
                S.dma(k.vA[c0 + off:c0 + off + rows], vt.ap[:rows].rearrange("p h d -> p (h d)"), reads=[vt], writes=[k.t_scr["vA"][si]])

        def ev_fa(j, blk, pb, off, rows):
            t = nxt(lf, "lf_i")
            a = t.ap
            S.op("dve", lambda e: e.tensor_tensor(out=a[:rows, 0], in0=pb.ap[:rows, 0:8], in1=k.bfb.ap[:rows], op=ALU.add),
                 reads=[pb, k.bfb], writes=[t])
            S.op("act", lambda e: e.activation(out=a[:rows, 1], in_=a[:rows, 0], func=ACTF.Abs), reads=[t], writes=[t])
            S.op("act", lambda e: e.activation(out=a[:rows, 1], in_=a[:rows, 1], func=ACTF.Exp, scale=-1.0), reads=[t], writes=[t])
            S.op("act", lambda e: e.activation(out=a[:rows, 1], in_=a[:rows, 1], func=ACTF.Ln, bias=1.0), reads=[t], writes=[t])
            S.op("dve", lambda e: e.tensor_scalar(out=a[:rows, 2], in0=a[:rows, 0], scalar1=0.0, scalar2=None, op0=ALU.min), reads=[t], writes=[t])
            S.op("dve", lambda e: e.tensor_tensor(out=a[:rows, 3], in0=a[:rows, 2], in1=a[:rows, 1], op=ALU.subtract), reads=[t], writes=[t])
            outdma("fl", c0 + off, rows, t, a[:, 3], 0, 8)
            pbk, _ = k.trbank()
            S.op("pe", lambda e: e.matmul(pbk.ap[0:8, 0:rows], lhsT=a[:rows, 3], rhs=k.identf.ap[:rows, :rows], start=True, stop=True),
                 reads=[t, k.identf], writes=[pbk])
            lt = nxt(lft, "lft_i")
            S.op("dve", lambda e: e.tensor_copy(out=lt.ap[0:8, 0:rows], in_=pbk.ap[0:8, 0:rows]), reads=[pbk], writes=[lt])
            S.dma(k.lfT_d[:, c0 + off:c0 + off + rows], lt.ap[0:8, 0:rows], reads=[lt], writes=[k.t_lfT])
            S.dma(k.lfbuf[c0 + off:c0 + off + rows], a[:rows, 3], reads=[t], writes=[k.t_scr["lf"][si]])

        def rope(ev, rows, ct):
            r = nxt(rt, "rt_i")
            x = ev.ap[:rows].rearrange("p (c d) -> p c d", c=4)
            x1 = x[:, :, 0:16]; x2 = x[:, :, 16:32]
            cosb = ct.ap[:rows, 0:1, :].to_broadcast([rows, 4, 16])
            sinb = ct.ap[:rows, 1:2, :].to_broadcast([rows, 4, 16])
            ra = r.ap
            for (dst, a_, b_) in ((0, x1, cosb), (1, x2, sinb), (2, x2, cosb), (3, x1, sinb)):
                S.op("pool", lambda e, dst=dst, a_=a_, b_=b_: e.tensor_tensor(out=ra[:rows, dst], in0=a_, in1=b_, op=ALU.mult),
                     reads=[ev, ct], writes=[r])
            S.op("pool", lambda e: e.tensor_tensor(out=x1, in0=ra[:rows, 0], in1=ra[:rows, 1], op=ALU.subtract), reads=[r], writes=[ev])
            S.op("pool", lambda e: e.tensor_tensor(out=x2, in0=ra[:rows, 2], in1=ra[:rows, 3], op=ALU.add), reads=[r], writes=[ev])

        def ev_rope(name):
            def f(j, blk, pb, off, rows):
                ev = nxt(ev32, "ev_i"); tm = nxt(tm16, "tm_i")
                S.op("act", lambda e: e.activation(out=ev.ap[:rows], in_=pb.ap[:rows], func=ACTF.Copy), reads=[pb], writes=[ev])
                rope(ev, rows, cst[j])
                if name == "kb":
                    outdma("dk", c0 + off, rows, ev, ev.ap, 512 * blk, 512)
                S.op("dve", lambda e: e.tensor_copy(out=tm.ap[:rows], in_=ev.ap[:rows]), reads=[ev], writes=[tm])
                k.transpose_chunks(tm, lambda c: tm.ap[:rows, c * 128:(c + 1) * 128], 4, rows, stT[name],
                                   stT[name].ap[:, 4 * blk:4 * blk + 4, off:off + rows], copy_eng())
            return f

        def ev_vb(j, blk, pb, off, rows):
            ev = nxt(ev32, "ev_i")
            S.op("dve", lambda e: e.tensor_copy(out=ev.ap[:rows], in_=pb.ap[:rows]), reads=[pb], writes=[ev])
            outdma("dv", c0 + off, rows, ev, ev.ap, 512 * blk, 512)
            vt = vBt[j]
            S.op("act", lambda e: e.activation(out=vt.ap[:rows, 2 * blk:2 * blk + 2, 0:256],
                                               in_=pb.ap[:rows].rearrange("p (h d) -> p h d", h=2), func=ACTF.Copy), reads=[pb], writes=[vt])
            if blk == 1:
                S.dma(k.vB[c0 + off:c0 + off + rows], vt.ap[:rows].rearrange("p h d -> p (h d)"), reads=[vt], writes=[k.t_scr["vB"][si]])

        def ev_gate(goff):
            def f(j, blk, pb, off, rows):
                g = nxt(gt, "gt_i")
                S.op("act", lambda e: e.activation(out=g.ap[:rows], in_=pb.ap[:rows], func=ACTF.Sigmoid), reads=[pb], writes=[g])
                S.dma(k.gsc[c0 + off:c0 + off + rows, goff + 512 * blk:goff + 512 * blk + 512], g.ap[:rows], reads=[g],
                      writes=[k.t_scr["gs"][si]])
            return f

        evs = {"qa": ev_q("qa"), "ka": ev_ka, "va": ev_va, "fa": ev_fa, "qb": ev_rope("qb"), "kb": ev_rope("kb"),
               "vb": ev_vb, "ga": ev_gate(0), "gb": ev_gate(2048)}
        for g, goff, w in k.groups:
            m = k.mats[(l, g)]
            dense(k, xb, xfn, KC, m, range(m.nblk), subs, evs[g])
            if g in k.scrT:
                S.dma(k.scrT[g][:, :, c0:c0 + T], stT[g].ap[:, :, 0:T], reads=[stT[g]], writes=[k.t_scr[g][si]])


def _consts():
    ident = np.eye(128, dtype=np.float32)
    kk = np.arange(128)[:, None]
    qq = np.arange(128)[None, :]
    cmask = np.where(kk <= qq, 0.0, NEG).astype(np.float32)
    kmask = np.where((kk // 64) <= (qq // 64), 0.0, NEG).astype(np.float32)
    pos = np.zeros(C, dtype=np.float32)
    pos[0:64] = PAST + np.arange(64)
    pos[64:80] = np.arange(16)
    pos[128:] = NM + np.arange(NF)
    half = 16
    inv_freq = (np.float32(500000.0) ** (-np.arange(half, dtype=np.float32) / np.float32(half))).astype(np.float32)
    ang = (pos[:, None] * inv_freq[None, :]).astype(np.float32)
    return {"c_ident": ident, "c_cmask": cmask, "c_kmask": kmask,
            "c_cos": np.cos(ang).astype(np.float32), "c_sin": np.sin(ang).astype(np.float32)}


def make_in_maps(inp):
    cs = _consts()
    f = lambda a: np.ascontiguousarray(np.asarray(a, dtype=np.float32))
    shared = {
        "meta": f(inp["meta_tokens"]), "ln_in_g": f(inp["ln_in_g"]), "ln_in_b": f(inp["ln_in_b"]),
        "w_in": f(inp["w_in"]), "b_f": f(inp["b_f"]),
        "lq1": f(inp["lambda_q1"]), "lk1": f(inp["lambda_k1"]), "lq2": f(inp["lambda_q2"]), "lk2": f(inp["lambda_k2"]),
        "subg": f(inp["subln_g"]), "w_br_a": f(inp["w_br_a"]), "w_br_b": f(inp["w_br_b"]), "w_out": f(inp["w_out"]),
        "ln1_g": f(inp["ln1_g"]), "ln1_b": f(inp["ln1_b"]), "w_up": f(inp["w_up"]), "w_down": f(inp["w_down"]),
        "ln2_g": f(inp["ln2_g"]), "ln2_b": f(inp["ln2_b"]),
    }
    shared.update(cs)
    maps = []
    for c in range(8):
        m = dict(shared)
        m["x_p"] = f(inp["x_prompt"][c]); m["x_s"] = f(inp["x_sample"][c])
        m["cfk"] = f(np.asarray(inp["cache_fox_k"])[:, c].reshape(DEPTH, PAST, 1024))
        m["cfv"] = f(np.asarray(inp["cache_fox_v"])[:, c].reshape(DEPTH, PAST, 1024))
        m["cfl"] = f(np.asarray(inp["cache_fox_logf"])[:, c])
        m["cdk"] = f(np.asarray(inp["cache_diff_k"])[:, c].reshape(DEPTH, PAST, 1024))
        m["cdv"] = f(np.asarray(inp["cache_diff_v"])[:, c].reshape(DEPTH, PAST, 1024))
        maps.append(m)
    return maps


def gather(results):
    st = lambda n: np.stack([np.asarray(r[n]) for r in results], axis=0)
    y_p = st("y_p"); y_s = st("y_s")

    def kv(n, shp):
        a = st(n)
        a = np.moveaxis(a, 0, 1)
        return np.ascontiguousarray(a.reshape(a.shape[:3] + shp))
    return (y_p, y_s,
            kv("fk_p", (8, 128)), kv("fv_p", (8, 128)), kv("fl_p", (8,)), kv("dk_p", (4, 256)), kv("dv_p", (4, 256)),
            kv("fk_s", (8, 128)), kv("fv_s", (8, 128)), kv("fl_s", (8,)), kv("dk_s", (4, 256)), kv("dv_s", (4, 256)))


_NC_CACHE = {}


def kernel(**inputs):
    if "nc" not in _NC_CACHE:
        _NC_CACHE["nc"] = build()
    nc = _NC_CACHE["nc"]
    maps = make_in_maps(inputs)
    res = run_bass_kernel_spmd(nc, maps, core_ids=list(range(8)))
    return gather(res.results)


def phase_C(k, l):
    S, A = k.S, k.A
    sts = k.sts
    A.reset()
    wbufs = [A.alloc("wbuf%d" % i, [KC, 512], BF16) for i in range(2)]
    oTa = A.alloc("oTa", [8, 512], BF16)
    oTb = A.alloc("oTb", [8, 512], BF16)
    gpa = [A.alloc("gpa%d" % i, [512], BF16) for i in range(4)]
    gpb = [A.alloc("gpb%d" % i, [512], BF16) for i in range(4)]
    m16 = [A.alloc("m16_%d" % i, [D], BF16) for i in range(4)]
    XT = A.alloc("XT", [KC, 512], BF16)
    xres = [A.alloc("xres%d" % i, [D], F32) for i in range(4)]
    hidT = A.alloc("hidT", [64, 256], BF16)
    brt = [A.alloc("brt%d" % i, [512], F32) for i in range(8)]
    k.gp_i = 0; k.brt_i = 0
    lnp = k.lnp
    S.dma(lnp[0].ap, k.ln1_g[l].partition_broadcast(128), writes=[lnp[0]])
    S.dma(lnp[1].ap, k.ln1_b[l].partition_broadcast(128), writes=[lnp[1]])
    S.dma(lnp[2].ap, k.ln2_g[l].partition_broadcast(128), writes=[lnp[2]])
    S.dma(lnp[3].ap, k.ln2_b[l].partition_broadcast(128), writes=[lnp[3]])
    M = k.mats
    ws = WStream(S, 2, wbufs)
    order = []
    for si in range(len(sts)):
        for blk in range(4):
            order.append(M[(l, "bra")].piece(blk, 0, 8)); order.append(M[(l, "brb")].piece(blk, 0, 8))
        for blk in range(4):
            order.append(M[(l, "out")].piece(blk))
        for hh in range(1 if si == 0 else 2):
            for blk in range(16):
                order.append(M[(l, "up")].piece(blk))
            for blk in range(4):
                for pi in range(4):
                    order.append(M[(l, "down")].piece(blk, 16 * pi, 16))
    ws.plan(order)
    k.ws = ws

    for si, (c0, subs) in enumerate(sts):
        T = subs[-1][0] + subs[-1][1]
        S.dma(oTa.ap[:, :, 0:T], k.oaT[:, :, c0:c0 + T], reads=[k.t_oa[si]], writes=[oTa])
        S.dma(oTb.ap[:, :, 0:T], k.obT[:, :, c0:c0 + T], reads=[k.t_ob[si]], writes=[oTb])
        for j, (off, rows) in enumerate(subs):
            S.dma(xres[j].ap[:rows], k.hbuf[c0 + off:c0 + off + rows], reads=[k.t_h[si]], writes=[xres[j]])
        gps = {}

        def ev_bra(j, blk, pb, off, rows):
            g = gpa[j]
            S.dma(g.ap[:rows], k.gsc[c0 + off:c0 + off + rows, 512 * blk:512 * blk + 512], reads=[k.t_scr["gs"][si]], writes=[g])
            t = brt[k.brt_i % 8]; k.brt_i += 1
            gps[(j, blk, "t")] = t
            S.op("dve", lambda e: e.tensor_tensor(out=t.ap[:rows], in0=pb.ap[:rows], in1=g.ap[:rows], op=ALU.mult), reads=[pb, g], writes=[t])

        def ev_brb(j, blk, pb, off, rows):
            g = gpb[j]; t = gps[(j, blk, "t")]
            S.dma(g.ap[:rows], k.gsc[c0 + off:c0 + off + rows, 2048 + 512 * blk:2048 + 512 * blk + 512], reads=[k.t_scr["gs"][si]], writes=[g])
            t2 = brt[k.brt_i % 8]; k.brt_i += 1
            S.op("dve", lambda e: e.tensor_tensor(out=t2.ap[:rows], in0=pb.ap[:rows], in1=g.ap[:rows], op=ALU.mult), reads=[pb, g], writes=[t2])
            S.op("pool", lambda e: e.tensor_tensor(out=m16[j].ap[:rows, 512 * blk:512 * blk + 512], in0=t2.ap[:rows],
                                                   in1=t.ap[:rows], op=ALU.add), reads=[t2, t], writes=[m16[j]])

        for blk in range(4):
            dense(k, oTa, lambda kg, off, rows: oTa.ap[:, kg, off:off + rows], 8, M[(l, "bra")], [blk], subs, ev_bra, kc_piece=8)
            dense(k, oTb, lambda kg, off, rows: oTb.ap[:, kg, off:off + rows], 8, M[(l, "brb")], [blk], subs, ev_brb, kc_piece=8)
        for j, (off, rows) in enumerate(subs):
            for hf in range(2):
                k.transpose_chunks(m16[j], lambda c, j=j, rows=rows, hf=hf: m16[j].ap[:rows, (hf * 8 + c) * 128:(hf * 8 + c + 1) * 128],
                                   8, rows, XT, XT.ap[:, hf * 8:hf * 8 + 8, off:off + rows], "dve" if hf == 0 else "act")

        def ev_res(j, blk, pb, off, rows):
            xs = xres[j].ap[:rows, 512 * blk:512 * blk + 512]
            S.op("dve", lambda e: e.scalar_tensor_tensor(out=xs, in0=xs, scalar=float(ALPHA), in1=pb.ap[:rows], op0=ALU.mult, op1=ALU.add),
                 reads=[xres[j], pb], writes=[xres[j]])

        dense(k, XT, lambda kg, off, rows: XT.ap[:, kg, off:off + rows], KC, M[(l, "out")], range(4), subs, ev_res)
        for j, (off, rows) in enumerate(subs):
            k.layernorm(xres[j], xres[j].ap[:rows], rows, lnp[0], lnp[1], xres[j], xres[j].ap[:rows])
            S.op("act", lambda e, j=j, rows=rows: e.activation(out=m16[j].ap[:rows], in_=xres[j].ap[:rows], func=ACTF.Copy), reads=[xres[j]], writes=[m16[j]])
            for hf in range(2):
                k.transpose_chunks(m16[j], lambda c, j=j, rows=rows, hf=hf: m16[j].ap[:rows, (hf * 8 + c) * 128:(hf * 8 + c + 1) * 128],
                                   8, rows, XT, XT.ap[:, hf * 8:hf * 8 + 8, off:off + rows], "dve" if hf == 0 else "act")
        mu = M[(l, "up")]
        halves = [subs] if len(subs) == 1 else [subs[0:2], subs[2:4]]
        for hs in halves:
            h0 = hs[0][0]
            Th = hs[-1][0] + hs[-1][1] - h0
            for blk in range(16):
                wb = ws.get(mu.piece(blk)[0])
                for hc in range(4):
                    pb = k.banks[k.bank_i % 6]; k.bank_i += 1
                    for kg in range(KC):
                        S.op("pe", lambda e, pb=pb, wb=wb, hc=hc, kg=kg, h0=h0, Th=Th: e.matmul(
                            pb.ap[:, :Th], lhsT=wb.ap[:, kg, hc * 128:(hc + 1) * 128], rhs=XT.ap[:, kg, h0:h0 + Th],
                            start=(kg == 0), stop=(kg == KC - 1)), reads=[XT, wb], writes=[pb])
                    t = brt[k.brt_i % 8]; k.brt_i += 1
                    S.op("act", lambda e, pb=pb, t=t, Th=Th: e.activation(out=t.ap[:, :Th], in_=pb.ap[:, :Th], func=ACTF.Relu), reads=[pb], writes=[t])
                    S.op("pool", lambda e, t=t, blk=blk, hc=hc, Th=Th: e.tensor_tensor(out=hidT.ap[:, blk * 4 + hc, 0:Th], in0=t.ap[:, :Th], in1=t.ap[:, :Th], op=ALU.mult),
                         reads=[t], writes=[hidT])
            j0 = subs.index(hs[0])

            def ev_res2(j, blk, pb, off, rows, j0=j0):
                ev_res(j + j0, blk, pb, off, rows)
            dense(k, hidT, lambda kg, off, rows, h0=h0: hidT.ap[:, kg, off - h0:off - h0 + rows], 64, M[(l, "down")], range(4), hs, ev_res2, kc_piece=16, nbanks=6)
        for j, (off, rows) in enumerate(subs):
            k.layernorm(xres[j], xres[j].ap[:rows], rows, lnp[2], lnp[3], xres[j], xres[j].ap[:rows])
            c = c0 + off
            if l == DEPTH - 1:
                if c < 128:
                    S.dma(k.y_s, xres[j].ap[0:64], reads=[xres[j]])
                else:
                    S.dma(k.y_p[c - 128:c - 128 + rows], xres[j].ap[:rows], reads=[xres[j]])
            else:
                S.dma(k.hbuf[c:c + rows], xres[j].ap[:rows], reads=[xres[j]], writes=[k.t_h[si]])
                S.op("act", lambda e, j=j, rows=rows: e.activation(out=m16[j].ap[:rows], in_=xres[j].ap[:rows], func=ACTF.Copy), reads=[xres[j]], writes=[m16[j]])
                for hf in range(2):
                    k.transpose_chunks(m16[j], lambda cc, j=j, rows=rows, hf=hf: m16[j].ap[:rows, (hf * 8 + cc) * 128:(hf * 8 + cc + 1) * 128],
                                       8, rows, XT, XT.ap[:, hf * 8:hf * 8 + 8, off:off + rows], "dve" if hf == 0 else "act")
        if l < DEPTH - 1:
            S.dma(k.hT[:, :, c0:c0 + T], XT.ap[:, :, 0:T], reads=[XT], writes=[k.t_hT[si]])


def ktile_table():
    kts = [("meta", 64, 16), ("snew", 0, 64)]
    for t in range(32):
        kts.append(("f%d" % t, 128 + 128 * t, 128))
    for t in range(32):
        kts.append(("c%d" % t, C + 128 * t, 128))
    return kts


KT_META, KT_SNEW, KT_F0, KT_C0 = 0, 1, 2, 34


def phase_B(k, l):
    S, A = k.S, k.A
    lam_init = 0.8 - 0.6 * float(np.exp(-0.3 * l))
    kts = ktile_table()
    sts = k.sts
    all_scr = lambda n: list(k.t_scr[n])

    def nxt(lst, attr):
        i = getattr(k, attr, 0)
        setattr(k, attr, i + 1)
        return lst[i % len(lst)]

    A.reset()
    ck32 = [A.alloc("ck32_%d" % i, [1024], F32) for i in range(2)]
    ck16 = [A.alloc("ck16_%d" % i, [1024], BF16) for i in range(2)]
    cstg = [A.alloc("cstg%d" % i, [8, 512], BF16) for i in range(2)]
    cvA = [A.alloc("cvA%d" % i, [8, 130], BF16) for i in range(2)]
    cvB = [A.alloc("cvB%d" % i, [4, 258], BF16) for i in range(2)]
    lT = A.alloc("lT", [C], F32)
    cT = A.alloc("cT", [C], F32)
    caT = A.alloc("caT", [PAST], F32)
    ccT = A.alloc("ccT", [PAST], F32)
    zer = A.alloc("zer", [512], F32)
    cl = A.alloc("cl", [32, 8], F32)
    lam4 = [A.alloc("lam4_%d" % i, [128], F32) for i in range(4)]
    lamt = A.alloc("lamt", [2, 128], F32)
    lamd = A.alloc("lamd", [2], F32)
    for t in cvA + cvB:
        S.op("pool", lambda e, t=t: e.memset(t.ap, 1.0), writes=[t])
    S.op("pool", lambda e: e.memset(zer.ap, 0.0), writes=[zer])
    for i, src in enumerate((k.lq1, k.lk1, k.lq2, k.lk2)):
        S.dma(lam4[i].ap, src[l].partition_broadcast(128), writes=[lam4[i]])
    for j in range(2):
        S.op("dve", lambda e, j=j: e.tensor_tensor(out=lamt.ap[:, j, :], in0=lam4[2 * j].ap, in1=lam4[2 * j + 1].ap, op=ALU.mult),
             reads=[lam4[2 * j], lam4[2 * j + 1]], writes=[lamt])
    S.op("dve", lambda e: e.tensor_reduce(out=lamd.ap, in_=lamt.ap, axis=AX.X, op=ALU.add), reads=[lamt], writes=[lamd])
    S.op("act", lambda e: e.activation(out=lamd.ap, in_=lamd.ap, func=ACTF.Exp), reads=[lamd], writes=[lamd])
    S.op("dve", lambda e: e.tensor_tensor(out=k.nlam.ap, in0=lamd.ap[:, 1:2], in1=lamd.ap[:, 0:1], op=ALU.subtract), reads=[lamd], writes=[k.nlam])
    S.op("dve", lambda e: e.tensor_scalar(out=k.nlam.ap, in0=k.nlam.ap, scalar1=-lam_init, scalar2=None, op0=ALU.add), reads=[k.nlam], writes=[k.nlam])
    S.dma(k.gsb.ap, k.subg[l].partition_broadcast(128), writes=[k.gsb])
    S.op("dve", lambda e: e.tensor_scalar(out=k.gsb.ap, in0=k.gsb.ap, scalar1=1.0 - lam_init, scalar2=None, op0=ALU.mult), reads=[k.gsb], writes=[k.gsb])

    it = 0
    for (src, name) in ((k.cfk, "ka"), (k.cdk, "kb")):
        for g4 in range(8):
            stg = cstg[g4 % 2]
            for tt in range(4):
                t = g4 * 4 + tt
                c32 = ck32[it % 2]; c16 = ck16[it % 2]; it += 1
                S.dma(c32.ap, src[l, t * 128:(t + 1) * 128, :], writes=[c32])
                S.op("pool", lambda e, c32=c32, c16=c16: e.tensor_copy(out=c16.ap, in_=c32.ap), reads=[c32], writes=[c16])
                k.transpose_chunks(c16, lambda c, c16=c16: c16.ap[:, c * 128:(c + 1) * 128], 8, 128, stg,
                                   stg.ap[:, :, tt * 128:(tt + 1) * 128], "act" if tt % 2 else "dve")
            S.dma(k.scrT[name][:, :, C + g4 * 512:C + (g4 + 1) * 512], stg.ap, reads=[stg], writes=[k.t_cache[name]])
    for t in range(32):
        c32 = ck32[it % 2]; it += 1
        vt = cvA[t % 2]
        S.dma(c32.ap, k.cfv[l, t * 128:(t + 1) * 128, :], writes=[c32])
        S.op("pool", lambda e, c32=c32, vt=vt: e.tensor_copy(out=vt.ap[:, :, 0:128], in_=c32.ap.rearrange("p (h d) -> p h d", h=8)), reads=[c32], writes=[vt])
        S.dma(k.vA[C + t * 128:C + (t + 1) * 128], vt.ap.rearrange("p h d -> p (h d)"), reads=[vt], writes=[k.t_cache["vA"]])
        c32 = ck32[it % 2]; it += 1
        vt = cvB[t % 2]
        S.dma(c32.ap, k.cdv[l, t * 128:(t + 1) * 128, :], writes=[c32])
        S.op("pool", lambda e, c32=c32, vt=vt: e.tensor_copy(out=vt.ap[:, :, 0:256], in_=c32.ap.rearrange("p (h d) -> p h d", h=4)), reads=[c32], writes=[vt])
        S.dma(k.vB[C + t * 128:C + (t + 1) * 128], vt.ap.rearrange("p h d -> p (h d)"), reads=[vt], writes=[k.t_cache["vB"]])

    S.dma(lT.ap[0:8], k.lfT_d, reads=[k.t_lfT], writes=[lT])
    S.dma(cl.ap, k.cfl[l].rearrange("(t p) h -> p t h", p=128), writes=[cl])
    for g in range(8):
        pbk, _ = k.trbank()
        for tt in range(4):
            t = g * 4 + tt
            S.op("pe", lambda e, pbk=pbk, tt=tt, t=t: e.matmul(pbk.ap[0:8, tt * 128:(tt + 1) * 128], lhsT=cl.ap[:, t, :], rhs=k.identf.ap,
                                                               start=True, stop=True), reads=[cl, k.identf], writes=[pbk])
        S.op("dve", lambda e, pbk=pbk, g=g: e.tensor_copy(out=caT.ap[0:8, g * 512:(g + 1) * 512], in_=pbk.ap[0:8, :]), reads=[pbk], writes=[caT])

    def scan(dst_b, dst_ap, src_b, src_ap, n, init, init_b=None):
        last = init
        for o in range(0, n, 512):
            w = min(512, n - o)
            ini = 0.0 if last is None else last
            S.op("dve", lambda e, o=o, w=w, ini=ini: e.tensor_tensor_scan(out=dst_ap[0:8, o:o + w], data0=src_ap[0:8, o:o + w], data1=zer.ap[0:8, 0:w],
                                                                         initial=ini, op0=ALU.add, op1=ALU.add),
                 reads=[src_b, zer, dst_b] + ([init_b] if (init_b is not None and o == 0) else []), writes=[dst_b])
            last = dst_ap[0:8, o + w - 1:o + w]
        return last

    last_m = scan(cT, cT.ap[:, 64:80], lT, lT.ap[:, 64:80], 16, None)
    scan(cT, cT.ap[:, 128:C], lT, lT.ap[:, 128:C], NF, last_m)
    last_c = scan(ccT, ccT.ap, caT, caT.ap, PAST, None)
    scan(cT, cT.ap[:, 0:64], lT, lT.ap[:, 0:64], TS, last_c, ccT)
    S.dma(k.cbuf, cT.ap[0:8], reads=[cT], writes=[k.t_cbuf])
    for half, (lo, hi) in enumerate(((0, 64), (64, 66))):
        pbk, _ = k.trbank()
        for kt in range(lo, hi):
            name, col, K_ = kts[kt]
            srcb, srcap = (ccT, ccT.ap[0:8, col - C:col - C + K_]) if kt >= KT_C0 else (cT, cT.ap[0:8, col:col + K_])
            S.op("pe", lambda e, pbk=pbk, kt=kt, lo=lo, K_=K_, srcap=srcap: e.matmul(pbk.ap[0:K_, (kt - lo) * 8:(kt - lo + 1) * 8], lhsT=srcap,
                                                                                   rhs=k.identf.ap[0:8, 0:8], start=True, stop=True),
                 reads=[srcb, k.identf], writes=[pbk])
        S.op("act", lambda e, pbk=pbk, lo=lo, hi=hi: e.activation(out=k.negc.ap[:, lo:hi, :], in_=pbk.ap[:, 0:(hi - lo) * 8].rearrange("p (t h) -> p t h", h=8),
                                                                 func=ACTF.Copy, scale=-1.0), reads=[pbk], writes=[k.negc])

    def groups(nq):
        gs = [(64, 16, [(KT_META, 0, True)])]
        per = nq // 128
        for J in range(NF // nq):
            lst = [(KT_META, 0, False)] + [(KT_F0 + t, 0, False) for t in range(per * J)]
            lst += [(KT_F0 + per * J + i, 128 * i, True) for i in range(per)]
            gs.append((128 + nq * J, nq, lst))
        gs.append((0, 64, [(KT_C0 + t, 0, False) for t in range(32)] + [(KT_SNEW, 0, True)]))
        return gs

    def load_V(Vh, src, h, w):
        S.dma(Vh.ap[0:16, KT_META, :], src[64:80, h * w:(h + 1) * w], reads=all_scr("vA" if w == 130 else "vB"), writes=[Vh])
        S.dma(Vh.ap[0:64, KT_SNEW, :], src[0:64, h * w:(h + 1) * w], reads=[], writes=[Vh])
        S.dma(Vh.ap[:, KT_F0:KT_F0 + 32, :], src[128:C, h * w:(h + 1) * w].rearrange("(t p) d -> p t d", p=128), reads=[], writes=[Vh])
        S.dma(Vh.ap[:, KT_C0:KT_C0 + 32, :], src[C:C + PAST, h * w:(h + 1) * w].rearrange("(t p) d -> p t d", p=128),
              reads=[k.t_cache["vA" if w == 130 else "vB"]], writes=[Vh])

    A.reset()
    Kh = A.alloc("Kh", [C + PAST], BF16)
    Qh = A.alloc("Qh", [C], BF16)
    Vh = A.alloc("Vh", [66, 130], BF16)
    crow = [A.alloc("crow%d" % i, [512], F32) for i in range(2)]
    tt32 = [A.alloc("tt32_%d" % i, [512], F32) for i in range(3)]
    PT = [A.alloc("PT%d" % i, [512], BF16) for i in range(3)]
    o16 = [A.alloc("o16_%d" % i, [256], BF16) for i in range(2)]
    oTs = [A.alloc("oTs%d" % i, [2, 512], BF16) for i in range(2)]
    rz = [A.alloc("rz%d" % i, [4], F32) for i in range(4)]
    osq = [A.alloc("osq%d" % i, [256], F32) for i in range(2)]
    o32 = [A.alloc("o32_%d" % i, [256], F32) for i in range(2)]
    k.sb_i = 0
    banks = k.banks
    fgroups = groups(512)
    for h in range(8):
        S.dma(Kh.ap, k.scrT["ka"][:, h, :], reads=all_scr("ka") + [k.t_cache["ka"]], writes=[Kh])
        S.dma(Qh.ap, k.scrT["qa"][:, h, 0:C], reads=all_scr("qa"), writes=[Qh])
        load_V(Vh, k.vA, h, 130)
        for gi, (qc, Nq, lst) in enumerate(fgroups):
            cr = nxt(crow, "cr_i")
            S.dma(cr.ap[:, 0:Nq], k.cbuf[h, qc:qc + Nq].partition_broadcast(128), reads=[k.t_cbuf], writes=[cr])
            nsub = (Nq + 127) // 128
            first = {}; last = {}
            for idx, (kt, qoff, msk) in enumerate(lst):
                for s_ in range(nsub):
                    if qoff <= s_ * 128:
                        first.setdefault(s_, idx); last[s_] = idx
            for idx, (kt, qoff, msk) in enumerate(lst):
                name, col, K_ = kts[kt]
                n = Nq - qoff
                ps = banks[k.sb_i % 2]; k.sb_i += 1
                S.op("pe", lambda e, ps=ps, K_=K_, n=n, col=col, qc=qc, qoff=qoff, Nq=Nq: e.matmul(
                    ps.ap[:K_, :n], lhsT=Kh.ap[:, col:col + K_], rhs=Qh.ap[:, qc + qoff:qc + Nq], start=True, stop=True), reads=[Kh, Qh], writes=[ps])
                t32 = nxt(tt32, "t32_i"); pt = nxt(PT, "pt_i")
                S.op("dve", lambda e, ps=ps, K_=K_, n=n, t32=t32, cr=cr, qoff=qoff, Nq=Nq: e.scalar_tensor_tensor(
                    out=t32.ap[:K_, :n], in0=ps.ap[:K_, :n], scalar=float(FOX_SCALE), in1=cr.ap[:K_, qoff:Nq], op0=ALU.mult, op1=ALU.add),
                    reads=[ps, cr], writes=[t32])
                if msk:
                    mw = min(128, n)
                    S.op("pool", lambda e, t32=t32, K_=K_, mw=mw: e.tensor_tensor(out=t32.ap[:K_, :mw], in0=t32.ap[:K_, :mw], in1=k.cmask.ap[:K_, :mw], op=ALU.add),
                         reads=[t32, k.cmask], writes=[t32])
                S.op("act", lambda e, t32=t32, pt=pt, K_=K_, n=n, kt=kt, h=h: e.activation(out=pt.ap[:K_, :n], in_=t32.ap[:K_, :n], func=ACTF.Exp,
                                                                                          bias=k.negc.ap[:K_, kt, h:h + 1]), reads=[t32, k.negc], writes=[pt])
                for s_ in range(nsub):
                    if qoff > s_ * 128:
                        continue
                    qs = s_ * 128 - qoff
                    qn = min(128, Nq - s_ * 128)
                    ob = banks[2 + s_]
                    S.op("pe", lambda e, ob=ob, pt=pt, K_=K_, qs=qs, qn=qn, kt=kt, st=(first[s_] == idx), sp=(last[s_] == idx): e.matmul(
                        ob.ap[:qn, 0:129], lhsT=pt.ap[:K_, qs:qs + qn], rhs=Vh.ap[:K_, kt, 0:129], start=st, stop=sp), reads=[pt, Vh], writes=[ob])
            ot = nxt(oTs, "ots_i")
            for s_ in range(nsub):
                qn = min(128, Nq - s_ * 128)
                ob = banks[2 + s_]
                r = nxt(rz, "rz_i"); o6 = nxt(o16, "o16_i")
                S.op("dve", lambda e, ob=ob, r=r, qn=qn: e.reciprocal(out=r.ap[:qn, 0:1], in_=ob.ap[:qn, 128:129]), reads=[ob], writes=[r])
                S.op("dve", lambda e, ob=ob, r=r, o6=o6, qn=qn: e.tensor_scalar(out=o6.ap[:qn, 0:128], in0=ob.ap[:qn, 0:128], scalar1=r.ap[:qn, 0:1], scalar2=None,
                                                                              op0=ALU.mult), reads=[ob, r], writes=[o6])
                k.transpose_chunks(o6, lambda c, o6=o6, qn=qn: o6.ap[:qn, 0:128], 1, qn, ot, ot.ap[:, 0:1, s_ * 128:s_ * 128 + qn], "act")
            S.dma(k.oaT[:, h, qc:qc + Nq], ot.ap[:, 0, 0:Nq], reads=[ot], writes=[k.t_oa[0 if qc < 128 else (qc - 128) // 512 + 1]])

    A.reset()
    Kd = A.alloc("Kd", [2, C + PAST], BF16)
    Qd = A.alloc("Qd", [2, C], BF16)
    Vd = A.alloc("Vd", [66, 258], BF16)
    tt32 = [A.alloc("dtt32_%d" % i, [128], F32) for i in range(3)]
    PT = [A.alloc("dPT%d" % i, [256], BF16) for i in range(4)]
    o16 = [A.alloc("do16_%d" % i, [256], BF16) for i in range(2)]
    oTs = [A.alloc("doTs%d" % i, [2, 256], BF16) for i in range(2)]
    rz = [A.alloc("drz%d" % i, [4], F32) for i in range(4)]
    osq = [A.alloc("dosq%d" % i, [256], F32) for i in range(2)]
    o32 = [A.alloc("do32_%d" % i, [256], F32) for i in range(2)]
    t1 = [A.alloc("dt1_%d" % i, [256], F32) for i in range(2)]
    dgroups = groups(256)
    for h in range(4):
        S.dma(Kd.ap, k.scrT["kb"][:, 2 * h:2 * h + 2, :], reads=all_scr("kb") + [k.t_cache["kb"]], writes=[Kd])
        S.dma(Qd.ap, k.scrT["qb"][:, 2 * h:2 * h + 2, 0:C], reads=all_scr("qb"), writes=[Qd])
        load_V(Vd, k.vB, h, 258)
        for gi, (qc, Nq, lst) in enumerate(dgroups):
            nsub = (Nq + 127) // 128
            first = {}; last = {}
            for idx, (kt, qoff, msk) in enumerate(lst):
                for s_ in range(nsub):
                    if qoff <= s_ * 128:
                        first.setdefault(s_, idx); last[s_] = idx
            is_meta_or_s = qc < 128
            for idx, (kt, qoff, msk) in enumerate(lst):
                name, col, K_ = kts[kt]
                n = Nq - qoff
                msk = msk and not is_meta_or_s
                ps = banks[k.sb_i % 2]; k.sb_i += 1
                psv = ps.ap.rearrange("p (c n) -> p c n", c=2)
                pts = []
                for c_ in range(2):
                    S.op("pe", lambda e, psv=psv, c_=c_, K_=K_, n=n, col=col, qc=qc, qoff=qoff, Nq=Nq: e.matmul(
                        psv[:K_, c_, :n], lhsT=Kd.ap[:, c_, col:col + K_], rhs=Qd.ap[:, c_, qc + qoff:qc + Nq], start=True, stop=True),
                        reads=[Kd, Qd], writes=[ps])
                for c_ in range(2):
                    pt = nxt(PT, "dpt_i"); pts.append(pt)
                    if msk:
                        mw = min(128, n)
                        t32 = nxt(tt32, "dt32_i")
                        S.op("dve", lambda e, psv=psv, c_=c_, K_=K_, mw=mw, t32=t32: e.scalar_tensor_tensor(
                            out=t32.ap[:K_, :mw], in0=psv[:K_, c_, :mw], scalar=float(DIFF_SCALE), in1=k.kmask.ap[:K_, :mw], op0=ALU.mult, op1=ALU.add),
                            reads=[ps, k.kmask], writes=[t32])
                        S.op("act", lambda e, t32=t32, pt=pt, K_=K_, mw=mw: e.activation(out=pt.ap[:K_, :mw], in_=t32.ap[:K_, :mw], func=ACTF.Exp),
                             reads=[t32], writes=[pt])
                        if n > mw:
                            S.op("act", lambda e, psv=psv, c_=c_, pt=pt, K_=K_, mw=mw, n=n: e.activation(out=pt.ap[:K_, mw:n], in_=psv[:K_, c_, mw:n], func=ACTF.Exp,
                                                                                                        scale=float(DIFF_SCALE)), reads=[ps], writes=[pt])
                    else:
                        S.op("act", lambda e, psv=psv, c_=c_, pt=pt, K_=K_, n=n: e.activation(out=pt.ap[:K_, :n], in_=psv[:K_, c_, :n], func=ACTF.Exp,
                                                                                            scale=float(DIFF_SCALE)), reads=[ps], writes=[pt])
                for c_ in range(2):
                    pt = pts[c_]
                    for s_ in range(nsub):
                        if qoff > s_ * 128:
                            continue
                        qs = s_ * 128 - qoff
                        qn = min(128, Nq - s_ * 128)
                        ob = banks[2 + 2 * c_ + s_]
                        S.op("pe", lambda e, ob=ob, pt=pt, K_=K_, qs=qs, qn=qn, kt=kt, st=(first[s_] == idx), sp=(last[s_] == idx): e.matmul(
                            ob.ap[:qn, 0:257], lhsT=pt.ap[:K_, qs:qs + qn], rhs=Vd.ap[:K_, kt, 0:257], start=st, stop=sp), reads=[pt, Vd], writes=[ob])
            ot = nxt(oTs, "dots_i")
            for s_ in range(nsub):
                qn = min(128, Nq - s_ * 128)
                ob0 = banks[2 + s_]; ob1 = banks[4 + s_]
                r = nxt(rz, "drz_i"); o6 = nxt(o16, "do16_i"); tt = nxt(t1, "dt1_i"); oo = nxt(o32, "do32_i"); sq = nxt(osq, "dosq_i")
                S.op("dve", lambda e, ob0=ob0, r=r, qn=qn: e.reciprocal(out=r.ap[:qn, 0:1], in_=ob0.ap[:qn, 256:257]), reads=[ob0], writes=[r])
                S.op("dve", lambda e, ob1=ob1, r=r, qn=qn: e.reciprocal(out=r.ap[:qn, 1:2], in_=ob1.ap[:qn, 256:257]), reads=[ob1], writes=[r])
                S.op("dve", lambda e, r=r, qn=qn: e.tensor_tensor(out=r.ap[:qn, 1:2], in0=r.ap[:qn, 1:2], in1=k.nlam.ap[:qn], op=ALU.mult), reads=[r, k.nlam], writes=[r])
                S.op("dve", lambda e, ob0=ob0, r=r, tt=tt, qn=qn: e.tensor_scalar(out=tt.ap[:qn], in0=ob0.ap[:qn, 0:256], scalar1=r.ap[:qn, 0:1], scalar2=None, op0=ALU.mult),
                     reads=[ob0, r], writes=[tt])
                S.op("dve", lambda e, ob1=ob1, r=r, tt=tt, oo=oo, qn=qn: e.scalar_tensor_tensor(out=oo.ap[:qn], in0=ob1.ap[:qn, 0:256], scalar=r.ap[:qn, 1:2], in1=tt.ap[:qn],
                                                                                            op0=ALU.mult, op1=ALU.add), reads=[ob1, r, tt], writes=[oo])
                S.op("act", lambda e, oo=oo, sq=sq, r=r, qn=qn: e.activation(out=sq.ap[:qn], in_=oo.ap[:qn], func=ACTF.Square, accum_out=r.ap[:qn, 2:3]),
                     reads=[oo], writes=[sq, r])
                S.op("dve", lambda e, r=r, qn=qn: e.tensor_scalar(out=r.ap[:qn, 2:3], in0=r.ap[:qn, 2:3], scalar1=1.0 / 256.0, scalar2=1e-5, op0=ALU.mult, op1=ALU.add),
                     reads=[r], writes=[r])
                S.op("act", lambda e, r=r, qn=qn: e.activation(out=r.ap[:qn, 2:3], in_=r.ap[:qn, 2:3], func=ACTF.Ln), reads=[r], writes=[r])
                S.op("act", lambda e, r=r, qn=qn: e.activation(out=r.ap[:qn, 2:3], in_=r.ap[:qn, 2:3], func=ACTF.Exp, scale=-0.5), reads=[r], writes=[r])
                S.op("dve", lambda e, oo=oo, r=r, o6=o6, qn=qn: e.scalar_tensor_tensor(out=o6.ap[:qn], in0=oo.ap[:qn], scalar=r.ap[:qn, 2:3], in1=k.gsb.ap[:qn],
                                                                                   op0=ALU.mult, op1=ALU.mult), reads=[oo, r, k.gsb], writes=[o6])
                k.transpose_chunks(o6, lambda c, o6=o6, qn=qn: o6.ap[:qn, c * 128:(c + 1) * 128], 2, qn, ot, ot.ap[:, :, s_ * 128:s_ * 128 + qn], "act")
            si = 0 if qc < 128 else (qc - 128) // 512 + 1
            S.dma(k.obT[:, 2 * h:2 * h + 2, qc:qc + Nq], ot.ap[:, :, 0:Nq], reads=[ot], writes=[k.t_ob[si]])
```

```python
import contextlib
import numpy as np
import concourse.bass as bass
import concourse.mybir as mybir
from concourse.bass_utils import run_bass_kernel_spmd

F32 = mybir.dt.float32
BF16 = mybir.dt.bfloat16
ALU = mybir.AluOpType
ACTF = mybir.ActivationFunctionType
AX = mybir.AxisListType


class Buf:
    __slots__ = ("name", "ap", "kind", "writers", "readers", "sem", "semcnt")

    def __init__(self, name, ap=None, kind="sbuf"):
        self.name = name
        self.ap = ap
        self.kind = kind
        self.writers = []
        self.readers = []
        self.sem = None
        self.semcnt = 0


class Op:
    __slots__ = ("eng", "emit", "deps", "is_dma", "sem", "val", "sig", "idx", "rawdeps")

    def __init__(self, eng, emit):
        self.eng = eng
        self.emit = emit
        self.deps = []
        self.rawdeps = set()
        self.is_dma = False
        self.sem = None
        self.val = 0
        self.sig = False


class Sched:
    ENGS = ("pe", "act", "dve", "pool", "sp")

    def __init__(self, nc):
        self.nc = nc
        self.stack = contextlib.ExitStack()
        self.ops = {e: [] for e in self.ENGS}
        self.all_ops = []
        self.esem = {}
        for e in ("pe", "act", "dve", "pool"):
            self.esem[e] = self.stack.enter_context(nc.semaphore("sem_" + e))
        self.dma_bufs = []
        self.free_sems = []
        self.store_eng = "sp"
        self.nsem = 4
        self.n_names = 0

    def sbuf(self, name, shape, dtype):
        t = self.stack.enter_context(self.nc.sbuf_tensor(name, list(shape), dtype))
        return Buf(name, t.ap() if hasattr(t, "ap") and callable(t.ap) else t[:], "sbuf")

    def psum(self, name, shape, dtype):
        t = self.stack.enter_context(self.nc.psum_tensor(name, list(shape), dtype))
        return Buf(name, t.ap() if hasattr(t, "ap") and callable(t.ap) else t[:], "psum")

    def dram(self, name, shape, dtype, kind="Internal"):
        return self.nc.dram_tensor(name, list(shape), dtype, kind=kind).ap()

    def tok(self, name):
        return Buf(name, None, "dram")

    def sub(self, buf, name=None):
        return Buf(name or buf.name + "_s", buf.ap, buf.kind)

    def _record(self, op, reads, writes):
        deps = op.deps
        for r in reads:
            for w_ in r.writers:
                deps.append(w_)
                op.rawdeps.add(id(w_))
            if r.kind == "psum":
                deps.extend(r.readers)
        for w in writes:
            if w.kind == "dram" and not w.readers:
                continue
            deps.extend(w.writers)
            deps.extend(w.readers)
        for r in reads:
            r.readers.append(op)
        for w in writes:
            if w.kind == "dram" and not w.readers:
                w.writers.append(op)
            else:
                w.writers = [op]
                w.readers = []
        op.idx = len(self.ops[op.eng])
        self.ops[op.eng].append(op)
        self.all_ops.append(op)

    def op(self, eng, emit, reads=(), writes=()):
        o = Op(eng, emit)
        self._record(o, reads, writes)
        return o

    def dma(self, out, in_, reads=(), writes=(), eng="sp", **kw):
        if eng == "sp" and not any(b.kind == "sbuf" for b in writes):
            eng = self.store_eng
        o = Op(eng, lambda e: e.dma_start(out=out, in_=in_, **kw))
        o.is_dma = True
        owner = None
        for b in list(writes) + list(reads):
            if b.kind == "sbuf":
                owner = b
                break
        if owner is None:
            owner = (list(writes) + list(reads))[0]
        if owner.sem is None:
            if self.free_sems:
                owner.sem, owner.semcnt = self.free_sems.pop()
            else:
                owner.sem = self.stack.enter_context(self.nc.semaphore("dsem%d" % self.nsem))
                self.nsem += 1
            self.dma_bufs.append(owner)
        owner.semcnt += 16
        o.sem = owner.sem
        o.val = owner.semcnt
        o.sig = True
        self._record(o, reads, writes)
        return o

    @staticmethod
    def _needs_wait(op, d):
        if d.is_dma:
            return True
        if d.eng != op.eng:
            return True
        if op.eng == "pe":
            return False
        return id(d) in op.rawdeps

    def finish(self):
        nc = self.nc
        for op in self.all_ops:
            for d in op.deps:
                if not d.is_dma and self._needs_wait(op, d):
                    d.sig = True
        for e in ("pe", "act", "dve", "pool"):
            c = 0
            for op in self.ops[e]:
                if op.is_dma:
                    continue
                if op.sig:
                    c += 1
                    op.val = c
                    op.sem = self.esem[e]
        fin = {}
        for b in self.dma_bufs:
            if id(b.sem) not in fin or fin[id(b.sem)][1] < b.semcnt:
                fin[id(b.sem)] = (b.sem, b.semcnt)
        finals = list(fin.values())
        self.nwaits = 0

        def emit_engine(ename, eng):
            seen = {}
            for op in self.ops[ename]:
                need = {}
                for d in op.deps:
                    if not self._needs_wait(op, d):
                        continue
                    k = id(d.sem)
                    if seen.get(k, 0) >= d.val:
                        continue
                    if k not in need or need[k][1] < d.val:
                        need[k] = (d.sem, d.val)
                for k, (sem, val) in need.items():
                    eng.wait_ge(sem, val)
                    seen[k] = val
                    self.nwaits += 1
                inst = op.emit(eng)
                if op.sig:
                    inst.then_inc(op.sem, 16 if op.is_dma else 1)
            if ename == "sp":
                for sem, val in finals:
                    eng.wait_ge(sem, val)

        with nc.Block() as block:
            @block.tensor
            def _(e):
                emit_engine("pe", e)

            @block.scalar
            def _(e):
                emit_engine("act", e)

            @block.vector
            def _(e):
                emit_engine("dve", e)

            @block.gpsimd
            def _(e):
                emit_engine("pool", e)

            @block.sync
            def _(e):
                emit_engine("sp", e)
        self.stack.close()


class Arena:
    def __init__(self, S, nbytes):
        self.S = S
        self.n32 = nbytes // 4
        self.base = S.sbuf("arena", [128, self.n32], F32)
        self.off = 0
        self.cur = []
        self.prev_ops = []

    def reset(self):
        ops = {}
        for b in self.cur:
            for w_ in b.writers:
                ops[id(w_)] = w_
            for r in b.readers:
                ops[id(r)] = r
        self.prev_ops = list(ops.values())
        for b in self.cur:
            if b.sem is not None:
                self.S.free_sems.append((b.sem, b.semcnt))
        self.cur = []
        self.off = 0

    def alloc_at(self, name, off32, free_shape, dtype):
        save = self.off
        self.off = off32
        b = self.alloc(name, free_shape, dtype)
        self.off = max(save, self.off)
        return b

    def alloc(self, name, free_shape, dtype):
        esz = 4 if dtype == F32 else 2
        n = 1
        for s in free_shape:
            n *= s
        n32 = (n * esz + 3) // 4
        n32 = (n32 + 7) // 8 * 8
        assert self.off + n32 <= self.n32, "arena overflow %s: %d + %d > %d" % (name, self.off, n32, self.n32)
        ap = self.base.ap[:, self.off:self.off + n32]
        if dtype != F32:
            ap = ap.bitcast(dtype)
        ap = ap[:, 0:n]
        if len(free_shape) == 2:
            ap = ap.rearrange("p (a b) -> p a b", a=free_shape[0])
        elif len(free_shape) == 3:
            ap = ap.rearrange("p (a b c) -> p a b c", a=free_shape[0], b=free_shape[1])
        self.off += n32
        b = Buf(name, ap, "sbuf")
        b.readers = list(self.prev_ops)
        self.cur.append(b)
        return b


D = 2048
KC = 16
NF = 4096
NM = 16
L = NM + NF
TS = 64
PAST = 4096
DEPTH = 2
C = 128 + NF
NIN = 10248
DFF = 8192
ALPHA = (2 * DEPTH) ** 0.25
FOX_SCALE = 128 ** -0.5
DIFF_SCALE = 128 ** -0.5
NEG = -1e30
GROUPS = [("qa", 0, 1024), ("ka", 1024, 1024), ("va", 2048, 1024), ("fa", 3072, 8),
          ("qb", 3080, 1024), ("kb", 4104, 1024), ("vb", 5128, 1024), ("ga", 6152, 2048), ("gb", 8200, 2048)]


def supertiles():
    sts = [(0, [(0, 80)])]
    for j in range(8):
        sts.append((128 + 512 * j, [(128 * i, 128) for i in range(4)]))
    return sts


class WStream:
    def __init__(self, S, nbuf, bufs):
        self.S = S
        self.bufs = bufs
        self.nbuf = nbuf
        self.order = []
        self.next_load = 0
        self.next_use = 0

    def plan(self, pieces):
        self.order = pieces

    def _issue(self, i):
        name, ap, tok = self.order[i]
        b = self.bufs[i % self.nbuf]
        kcp, ncols = ap.shape[1], ap.shape[2]
        self.S.dma(b.ap[:, 0:kcp, 0:ncols], ap, reads=list(tok), writes=[b])

    def get(self, name):
        i = self.next_use
        assert self.order[i][0] == name, (self.order[i][0], name)
        while self.next_load < len(self.order) and self.next_load < i + self.nbuf:
            self._issue(self.next_load)
            self.next_load += 1
        self.next_use += 1
        return self.bufs[i % self.nbuf]


class WMat:
    def __init__(self, S, name, src, bw=512):
        self.name = name
        self.src = src
        self.K, self.N = src.shape
        self.kc = self.K // 128
        self.bw = min(bw, self.N)
        self.nblk = self.N // self.bw
        self.scr = S.dram(name + "_bf", [self.nblk, 128, self.kc, self.bw], BF16)
        self.cp = min(self.N, 2048)
        self.ncp = self.N // self.cp
        self.toks = [[S.tok("%s_t%d_%d" % (name, k, c)) for c in range(self.ncp)] for k in range(self.kc)]

    def piece(self, blk, kc0=0, kcp=None):
        kcp = self.kc if kcp is None else kcp
        cpi = (blk * self.bw) // self.cp
        toks = [self.toks[k][cpi] for k in range(kc0, kc0 + kcp)]
        return ("%s_b%d_k%d" % (self.name, blk, kc0), self.scr[blk, :, kc0:kc0 + kcp, :], toks)


def cast_weights(S, mats, stage32, stage16):
    i = 0
    engs = ("act", "dve", "pool")
    for m in mats:
        for k in range(m.kc):
            for c in range(m.ncp):
                s32 = stage32[i % len(stage32)]
                s16 = stage16[i % len(stage16)]
                w = m.cp
                S.dma(s32.ap[:, 0:w], m.src[k * 128:(k + 1) * 128, c * w:(c + 1) * w], writes=[s32])
                e = engs[i % 3]
                if e == "act":
                    S.op("act", lambda en, s32=s32, s16=s16, w=w: en.activation(out=s16.ap[:, 0:w], in_=s32.ap[:, 0:w], func=ACTF.Copy),
                         reads=[s32], writes=[s16])
                else:
                    S.op(e, lambda en, s32=s32, s16=s16, w=w: en.tensor_copy(out=s16.ap[:, 0:w], in_=s32.ap[:, 0:w]),
                         reads=[s32], writes=[s16])
                nb = w // m.bw
                b0 = (c * w) // m.bw
                S.dma(m.scr[b0:b0 + nb, :, k, :].rearrange("b p n -> p b n"),
                      s16.ap[:, 0:w].rearrange("p (b n) -> p b n", b=nb), reads=[s16], writes=[m.toks[k][c]])
                i += 1


def out_rows(c, rows):
    if c < 128:
        return [(0, 64, "s", 0), (64, 80, "p", 0)]
    return [(0, rows, "p", c - 128 + NM)]


class K:
    pass


def build(stop_after=None):
    nc = bass.Bass("TRN2", target_bir_lowering=False)
    S = Sched(nc)
    k = K()

    def I(name, shape):
        return nc.dram_tensor(name, list(shape), F32, kind="ExternalInput").ap()

    def O(name, shape):
        return nc.dram_tensor(name, list(shape), F32, kind="ExternalOutput").ap()

    x_p = I("x_p", [NF, D]); x_s = I("x_s", [TS, D]); meta = I("meta", [NM, D])
    cfk = I("cfk", [DEPTH, PAST, 1024]); cfv = I("cfv", [DEPTH, PAST, 1024]); cfl = I("cfl", [DEPTH, PAST, 8])
    cdk = I("cdk", [DEPTH, PAST, 1024]); cdv = I("cdv", [DEPTH, PAST, 1024])
    ln_in_g = I("ln_in_g", [D]); ln_in_b = I("ln_in_b", [D])
    w_in = I("w_in", [DEPTH, D, NIN]); b_f = I("b_f", [DEPTH, 8])
    lq1 = I("lq1", [DEPTH, 128]); lk1 = I("lk1", [DEPTH, 128]); lq2 = I("lq2", [DEPTH, 128]); lk2 = I("lk2", [DEPTH, 128])
    subg = I("subg", [DEPTH, 256])
    w_br_a = I("w_br_a", [DEPTH, 1024, D]); w_br_b = I("w_br_b", [DEPTH, 1024, D]); w_out = I("w_out", [DEPTH, D, D])
    ln1_g = I("ln1_g", [DEPTH, D]); ln1_b = I("ln1_b", [DEPTH, D])
    w_up = I("w_up", [DEPTH, D, DFF]); w_down = I("w_down", [DEPTH, DFF, D])
    ln2_g = I("ln2_g", [DEPTH, D]); ln2_b = I("ln2_b", [DEPTH, D])
    c_ident = I("c_ident", [128, 128]); c_cmask = I("c_cmask", [128, 128]); c_kmask = I("c_kmask", [128, 128])
    c_cos = I("c_cos", [C, 16]); c_sin = I("c_sin", [C, 16])

    y_p = O("y_p", [NF, D]); y_s = O("y_s", [TS, D])
    outs = {}
    for nm, w in (("fk", 1024), ("fv", 1024), ("fl", 8), ("dk", 1024), ("dv", 1024)):
        outs[nm + "p"] = O(nm + "_p", [DEPTH, L, w])
        outs[nm + "s"] = O(nm + "_s", [DEPTH, TS, w])

    import os
    hbuf = S.dram("hbuf", [C, D], F32, kind=("ExternalOutput" if os.environ.get("DBG_DUMP") else "Internal"))
    hT = S.dram("hT", [128, KC, C], BF16)
    scrT = {n: S.dram(n + "T", [128, 8, C + PAST], BF16) for n in ("qa", "ka", "qb", "kb")}
    vA = S.dram("vA", [C + PAST, 8 * 130], BF16)
    vB = S.dram("vB", [C + PAST, 4 * 258], BF16)
    gsc = S.dram("gsc", [C, 4096], BF16)
    import os
    dk_ = "ExternalOutput" if os.environ.get("DBG_DUMP") else "Internal"
    oaT = S.dram("oaT", [128, 8, C], BF16, kind=dk_)
    obT = S.dram("obT", [128, 8, C], BF16, kind=dk_)
    t_oa = [S.tok("oa_st%d" % i) for i in range(9)]
    t_ob = [S.tok("ob_st%d" % i) for i in range(9)]
    lfbuf = S.dram("lfbuf", [C, 8], F32)
    lfT_d = S.dram("lfT_d", [8, C], F32)
    cbuf = S.dram("cbuf", [8, C], F32, kind=("ExternalOutput" if os.environ.get("DBG_DUMP") else "Internal"))
    t_lfT = S.tok("lfT_tok")
    t_cbuf = S.tok("cbuf_tok")
    t_cache = {n: S.tok("cache_" + n) for n in ("ka", "kb", "vA", "vB")}
    t_h = [S.tok("h_st%d" % i) for i in range(9)]
    t_hT = [S.tok("hT_st%d" % i) for i in range(9)]
    t_scr = {n: [S.tok("%s_st%d" % (n, i)) for i in range(9)] for n in ("qa", "ka", "qb", "kb", "vA", "vB", "gs", "lf")}

    identf = S.sbuf("identf", [128, 128], F32)
    identb = S.sbuf("identb", [128, 128], BF16)
    cmask = S.sbuf("cmask", [128, 128], F32)
    kmask = S.sbuf("kmask", [128, 128], F32)
    lnp = [S.sbuf("lnp%d" % i, [128, D], F32) for i in range(4)]
    bfb = S.sbuf("bfb", [128, 8], F32)
    nstat = 4
    st_bn = [S.sbuf("st_bn%d" % i, [128, 4, 6], F32) for i in range(nstat)]
    st_mv = [S.sbuf("st_mv%d" % i, [128, 2], F32) for i in range(nstat)]
    st_rs = [S.sbuf("st_rs%d" % i, [128, 1], F32) for i in range(nstat)]
    k.stat_i = 0
    banks = [S.psum("bank%d" % i, [128, 512], F32) for i in range(8)]
    banks16 = [Buf("bank%d_16" % i, b.ap.bitcast(BF16), "psum") for i, b in enumerate(banks)]
    for b16, b in zip(banks16, banks):
        pass
    k.tr_i = 0
    k.bank_i = 0
    A = Arena(S, 170 * 1024)
    negc = S.sbuf("negc", [128, 66, 8], F32)
    nlam = S.sbuf("nlam", [128, 1], F32)
    gsb = S.sbuf("gsb", [128, 256], F32)

    S.dma(identf.ap, c_ident, writes=[identf])
    S.op("dve", lambda e: e.tensor_copy(out=identb.ap, in_=identf.ap), reads=[identf], writes=[identb])
    S.dma(cmask.ap, c_cmask, writes=[cmask])
    S.dma(kmask.ap, c_kmask, writes=[kmask])

    def trbank():
        i = 6 + (k.tr_i % 2)
        k.tr_i += 1
        return banks[i], banks16[i].ap

    def transpose_chunks(src_buf, src_fn, n, rows, dst_buf, dst_ap, eng):
        pb, p16 = trbank()
        for c in range(n):
            S.op("pe", lambda e, c=c: e.transpose(p16[:, c * 128:c * 128 + rows], src_fn(c), identb.ap[:rows, :rows]),
                 reads=[src_buf, identb], writes=[pb])
        src = p16[:, 0:n * 128].rearrange("p (c r) -> p c r", c=n)[:, :, 0:rows]
        if eng == "act":
            S.op("act", lambda e: e.activation(out=dst_ap, in_=src, func=ACTF.Copy), reads=[pb], writes=[dst_buf])
        else:
            S.op(eng, lambda e: e.tensor_copy(out=dst_ap, in_=src), reads=[pb], writes=[dst_buf])

    def layernorm(xb, x_ap, rows, g_b, b_b, out_b, out_ap):
        i = k.stat_i % nstat
        k.stat_i += 1
        bn, mv, rs = st_bn[i], st_mv[i], st_rs[i]
        for c in range(4):
            S.op("dve", lambda e, c=c: e.bn_stats(out=bn.ap[:rows, c, :], in_=x_ap[:, c * 512:(c + 1) * 512]), reads=[xb], writes=[bn])
        S.op("dve", lambda e: e.bn_aggr(out=mv.ap[:rows], in_=bn.ap[:rows]), reads=[bn], writes=[mv])
        S.op("dve", lambda e: e.tensor_scalar(out=rs.ap[:rows], in0=mv.ap[:rows, 1:2], scalar1=1e-5, scalar2=None, op0=ALU.add),
             reads=[mv], writes=[rs])
        S.op("act", lambda e: e.activation(out=rs.ap[:rows], in_=rs.ap[:rows], func=ACTF.Ln), reads=[rs], writes=[rs])
        S.op("act", lambda e: e.activation(out=rs.ap[:rows], in_=rs.ap[:rows], func=ACTF.Exp, scale=-0.5), reads=[rs], writes=[rs])
        S.op("dve", lambda e: e.tensor_scalar(out=out_ap, in0=x_ap, scalar1=mv.ap[:rows, 0:1], scalar2=rs.ap[:rows],
                                              op0=ALU.subtract, op1=ALU.mult), reads=[xb, mv, rs], writes=[out_b])
        S.op("pool", lambda e: e.tensor_tensor(out=out_ap, in0=out_ap, in1=g_b.ap[:rows], op=ALU.mult), reads=[out_b, g_b], writes=[out_b])
        S.op("pool", lambda e: e.tensor_tensor(out=out_ap, in0=out_ap, in1=b_b.ap[:rows], op=ALU.add), reads=[out_b, b_b], writes=[out_b])

    mats = {}
    for l in range(DEPTH):
        for g, off, w in GROUPS:
            mats[(l, g)] = WMat(S, "w%d%s" % (l, g), w_in[l][:, off:off + w])
        mats[(l, "bra")] = WMat(S, "w%dbra" % l, w_br_a[l])
        mats[(l, "brb")] = WMat(S, "w%dbrb" % l, w_br_b[l])
        mats[(l, "out")] = WMat(S, "w%dout" % l, w_out[l])
        mats[(l, "up")] = WMat(S, "w%dup" % l, w_up[l])
        mats[(l, "down")] = WMat(S, "w%ddown" % l, w_down[l])
    k.mats = mats

    A.reset()
    s32 = [A.alloc("s32_%d" % i, [2048], F32) for i in range(3)]
    s16 = [A.alloc("s16_%d" % i, [2048], BF16) for i in range(3)]
    order = []
    for l in range(DEPTH):
        order += [mats[(l, g)] for g, _, _ in GROUPS] + [mats[(l, n)] for n in ("bra", "brb", "out", "up", "down")]
    if stop_after in ("A0", "P0", "PRE"):
        order = [mats[(0, g)] for g, _, _ in GROUPS]
    cast_weights(S, order, s32, s16)
    if stop_after == "P0":
        S.finish()
        return nc

    sts = supertiles()

    A.reset()
    xin = [A.alloc("xin%d" % i, [D], F32) for i in range(2)]
    xn16 = [A.alloc("xn16_%d" % i, [D], BF16) for i in range(2)]
    hTst = A.alloc("hTst", [KC, 512], BF16)
    S.dma(lnp[0].ap, ln_in_g.partition_broadcast(128), writes=[lnp[0]])
    S.dma(lnp[1].ap, ln_in_b.partition_broadcast(128), writes=[lnp[1]])
    it = 0
    for si, (c0, subs) in enumerate(sts):
        T = subs[-1][0] + subs[-1][1]
        for (off, rows) in subs:
            xb = xin[it % 2]; x16 = xn16[it % 2]; it += 1
            if c0 == 0:
                S.dma(xb.ap[0:64], x_s, writes=[xb])
                S.dma(xb.ap[64:80], meta, writes=[xb])
            else:
                f0 = c0 - 128 + off
                S.dma(xb.ap[:rows], x_p[f0:f0 + rows], writes=[xb])
            layernorm(xb, xb.ap[:rows], rows, lnp[0], lnp[1], xb, xb.ap[:rows])
            S.dma(hbuf[c0 + off:c0 + off + rows], xb.ap[:rows], reads=[xb], writes=[t_h[si]])
            S.op("act", lambda e, xb=xb, x16=x16, rows=rows: e.activation(out=x16.ap[:rows], in_=xb.ap[:rows], func=ACTF.Copy),
                 reads=[xb], writes=[x16])
            for hf in range(2):
                transpose_chunks(x16, lambda c, x16=x16, rows=rows, hf=hf: x16.ap[:rows, (hf * 8 + c) * 128:(hf * 8 + c + 1) * 128],
                                 8, rows, hTst, hTst.ap[:, hf * 8:hf * 8 + 8, off:off + rows], "dve" if hf == 0 else "act")
        S.dma(hT[:, :, c0:c0 + T], hTst.ap[:, :, 0:T], reads=[hTst], writes=[t_hT[si]])

    if stop_after == "PRE":
        S.finish()
        return nc
    k.S = S; k.A = A; k.nc = nc
    k.__dict__.update(locals())
    for l in range(DEPTH):
        phase_A(k, l)
        if stop_after == "A0":
            break
        phase_B(k, l)
        if stop_after == "B0":
            break
        phase_C(k, l)
        if stop_after == "C0":
            break
    S.finish()
    return nc


def dense(k, xT_b, xT_fn, kc_total, mat, blocks, subs, evac, kc_piece=16, nbanks=6):
    S = k.S
    bw = mat.bw
    npiece = kc_total // kc_piece
    for blk in blocks:
        pbs = []
        for j in range(len(subs)):
            pbs.append(k.banks[k.bank_i % nbanks])
            k.bank_i += 1
        for pi in range(npiece):
            wb = k.ws.get(mat.piece(blk, pi * kc_piece, kc_piece)[0])
            for j, (off, rows) in enumerate(subs):
                pb = pbs[j]
                for kk in range(kc_piece):
                    kg = pi * kc_piece + kk
                    S.op("pe", lambda e, pb=pb, rows=rows, off=off, kg=kg, kk=kk, wb=wb: e.matmul(
                        pb.ap[:rows, :bw], lhsT=xT_fn(kg, off, rows), rhs=wb.ap[:, kk, :bw],
                        start=(kg == 0), stop=(kg == kc_total - 1)), reads=(list(xT_b) if isinstance(xT_b, (list, tuple)) else [xT_b]) + [wb], writes=[pb])
                if pi == npiece - 1:
                    evac(j, blk, pb, off, rows)


def phase_A(k, l):
    S, A = k.S, k.A
    sts = k.sts
    outs = k.outs
    A.reset()
    xT = [A.alloc("xT%d" % i, [KC, 512], BF16) for i in range(2)]
    wbufs = [A.alloc("wbuf%d" % i, [KC, 512], BF16) for i in range(4)]
    stT = {n: A.alloc("stT_" + n, [8, 512], BF16) for n in ("qa", "ka", "qb", "kb")}
    ev32 = [A.alloc("ev32_%d" % i, [512], F32) for i in range(3)]
    tm16 = [A.alloc("tm16_%d" % i, [512], BF16) for i in range(3)]
    vAt = [A.alloc("vAt%d" % i, [8, 130], BF16) for i in range(4)]
    vBt = [A.alloc("vBt%d" % i, [4, 258], BF16) for i in range(4)]
    gt = [A.alloc("gt%d" % i, [512], BF16) for i in range(3)]
    cs = [A.alloc("cs%d" % i, [2, 16], F32) for i in range(4)]
    rt = [A.alloc("rt%d" % i, [4, 4, 16], F32) for i in range(2)]
    lf = [A.alloc("lf%d" % i, [4, 8], F32) for i in range(2)]
    lft = [A.alloc("lft%d" % i, [128], F32) for i in range(2)]
    k.lft_i = 0
    k.ev_i = 0; k.tm_i = 0; k.gt_i = 0; k.rt_i = 0; k.lf_i = 0; k.ce = 0
    import os
    k.dbgva = os.environ.get("DBG_VA", "")
    for t in vAt + vBt:
        if "nomemset" in k.dbgva:
            break
        S.op("pool", lambda e, t=t: e.memset(t.ap, 1.0), writes=[t])
    S.dma(k.bfb.ap, k.b_f[l:l + 1, :].partition_broadcast(128) if False else k.b_f[l].partition_broadcast(128), writes=[k.bfb])

    ws = WStream(S, 4, wbufs)
    order = []
    import os
    k.groups = [g for g in GROUPS if g[0] in os.environ.get("DBG_GROUPS", "qa,ka,va,fa,qb,kb,vb,ga,gb").split(",")]
    for si in range(len(sts)):
        for g, off, w in k.groups:
            m = k.mats[(l, g)]
            for blk in range(m.nblk):
                order.append(m.piece(blk))
    ws.plan(order)
    k.ws = ws

    def outdma(name, c, rows, src_b, src_ap, col0, ncols):
        for (p0, p1, kind, r0) in out_rows(c, rows):
            dst = outs[name + kind][l, r0:r0 + (p1 - p0), col0:col0 + ncols]
            S.dma(dst, src_ap[p0:p1], reads=[src_b])

    def nxt(lst, attr):
        i = getattr(k, attr)
        setattr(k, attr, i + 1)
        return lst[i % len(lst)]

    def copy_eng():
        k.ce += 1
        return "act" if k.ce % 2 else "dve"

    for si, (c0, subs) in enumerate(sts):
        T = subs[-1][0] + subs[-1][1]
        xb = xT[si % 2]
        S.dma(xb.ap[:, :, 0:T], k.hT[:, :, c0:c0 + T], reads=[k.t_hT[si]], writes=[xb])
        xfn = lambda kg, off, rows, xb=xb: xb.ap[:, kg, off:off + rows]
        cst = []
        for j, (off, rows) in enumerate(subs):
            t = cs[j]
            S.dma(t.ap[:rows, 0, :], k.c_cos[c0 + off:c0 + off + rows], writes=[t])
            S.dma(t.ap[:rows, 1, :], k.c_sin[c0 + off:c0 + off + rows], writes=[t])
            cst.append(t)

        def ev_q(name):
            def f(j, blk, pb, off, rows):
                tm = nxt(tm16, "tm_i")
                S.op("act", lambda e: e.activation(out=tm.ap[:rows], in_=pb.ap[:rows], func=ACTF.Copy), reads=[pb], writes=[tm])
                k.transpose_chunks(tm, lambda c: tm.ap[:rows, c * 128:(c + 1) * 128], 4, rows, stT[name],
                                   stT[name].ap[:, 4 * blk:4 * blk + 4, off:off + rows], "dve")
            return f

        def ev_ka(j, blk, pb, off, rows):
            ev = nxt(ev32, "ev_i"); tm = nxt(tm16, "tm_i")
            S.op("act", lambda e: e.activation(out=ev.ap[:rows], in_=pb.ap[:rows], func=ACTF.Copy), reads=[pb], writes=[ev])
            outdma("fk", c0 + off, rows, ev, ev.ap, 512 * blk, 512)
            S.op("dve", lambda e: e.tensor_copy(out=tm.ap[:rows], in_=pb.ap[:rows]), reads=[pb], writes=[tm])
            k.transpose_chunks(tm, lambda c: tm.ap[:rows, c * 128:(c + 1) * 128], 4, rows, stT["ka"],
                               stT["ka"].ap[:, 4 * blk:4 * blk + 4, off:off + rows], "act")

        def ev_va(j, blk, pb, off, rows):
            ev = nxt(ev32, "ev_i")
            S.op("dve", lambda e: e.tensor_copy(out=ev.ap[:rows], in_=pb.ap[:rows]), reads=[pb], writes=[ev])
            outdma("fv", c0 + off, rows, ev, ev.ap, 512 * blk, 512)
            vt = vAt[j]
            S.op("act", lambda e: e.activation(out=vt.ap[:rows, 4 * blk:4 * blk + 4, 0:128],
                                               in_=pb.ap[:rows].rearrange("p (h d) -> p h d", h=4), func=ACTF.Copy), reads=[pb], writes=[vt])
            if blk == 1 and "nodma" not in k.dbgva:
                S.dma(k.vA[c0 + off:c0 + off + rows], vt.ap[:rows].rearrange("p h d -> p (h d)"), reads=[vt], writes=[k.t_scr["vA"][si]])

        def ev_fa(j, blk, pb, off, rows):
            t = nxt(lf, "lf_i")
            a = t.ap
            S.op("dve", lambda e: e.tensor_tensor(out=a[:rows, 0], in0=pb.ap[:rows, 0:8], in1=k.bfb.ap[:rows], op=ALU.add),
                 reads=[pb, k.bfb], writes=[t])
            S.op("act", lambda e: e.activation(out=a[:rows, 1], in_=a[:rows, 0], func=ACTF.Abs), reads=[t], writes=[t])
            S.op("act", lambda e: e.activation(out=a[:rows, 1], in_=a[:rows, 1], func=ACTF.Exp, scale=-1.0), reads=[t], writes=[t])
            S.op("act", lambda e: e.activation(out=a[:rows, 1], in_=a[:rows, 1], func=ACTF.Ln, bias=1.0), reads=[t], writes=[t])
            S.op("dve", lambda e: e.tensor_scalar(out=a[:rows, 2], in0=a[:rows, 0], scalar1=0.0, scalar2=None, op0=ALU.min), reads=[t], writes=[t])
            S.op("dve", lambda e: e.tensor_tensor(out=a[:rows, 3], in0=a[:rows, 2], in1=a[:rows, 1], op=ALU.subtract), reads=[t], writes=[t])
            outdma("fl", c0 + off, rows, t, a[:, 3], 0, 8)
            pbk, _ = k.trbank()
            S.op("pe", lambda e: e.matmul(pbk.ap[0:8, 0:rows], lhsT=a[:rows, 3], rhs=k.identf.ap[:rows, :rows], start=True, stop=True),
                 reads=[t, k.identf], writes=[pbk])
            lt = nxt(lft, "lft_i")
            S.op("dve", lambda e: e.tensor_copy(out=lt.ap[0:8, 0:rows], in_=pbk.ap[0:8, 0:rows]), reads=[pbk], writes=[lt])
            S.dma(k.lfT_d[:, c0 + off:c0 + off + rows], lt.ap[0:8, 0:rows], reads=[lt], writes=[k.t_lfT])
            S.dma(k.lfbuf[c0 + off:c0 + off + rows], a[:rows, 3], reads=[t], writes=[k.t_scr["lf"][si]])

        def rope(ev, rows, ct):
            r = nxt(rt, "rt_i")
            x = ev.ap[:rows].rearrange("p (c d) -> p c d", c=4)
            x1 = x[:, :, 0:16]; x2 = x[:, :, 16:32]
            cosb = ct.ap[:rows, 0:1, :].to_broadcast([rows, 4, 16])
            sinb = ct.ap[:rows, 1:2, :].to_broadcast([rows, 4, 16])
            ra = r.ap
            for (dst, a_, b_) in ((0, x1, cosb), (1, x2, sinb), (2, x2, cosb), (3, x1, sinb)):
                S.op("pool", lambda e, dst=dst, a_=a_, b_=b_: e.tensor_tensor(out=ra[:rows, dst], in0=a_, in1=b_, op=ALU.mult),
                     reads=[ev, ct], writes=[r])
            S.op("pool", lambda e: e.tensor_tensor(out=x1, in0=ra[:rows, 0], in1=ra[:rows, 1], op=ALU.subtract), reads=[r], writes=[ev])
            S.op("pool", lambda e: e.tensor_tensor(out=x2, in0=ra[:rows, 2], in1=ra[:rows, 3], op=ALU.add), reads=[r], writes=[ev])

        def ev_rope(name):
            def f(j, blk, pb, off, rows):
                ev = nxt(ev32, "ev_i"); tm = nxt(tm16, "tm_i")
                S.op("act", lambda e: e.activation(out=ev.ap[:rows], in_=pb.ap[:rows], func=ACTF.Copy), reads=[pb], writes=[ev])
                rope(ev, rows, cst[j])
                if name == "kb":
                    outdma("dk", c0 + off, rows, ev, ev.ap, 512 * blk, 512)
                S.op("dve", lambda e: e.tensor_copy(out=tm.ap[:rows], in_=ev.ap[:rows]), reads=[ev], writes=[tm])
                k.transpose_chunks(tm, lambda c: tm.ap[:rows, c * 128:(c + 1) * 128], 4, rows, stT[name],
                                   stT[name].ap[:, 4 * blk:4 * blk + 4, off:off + rows], copy_eng())
            return f

        def ev_vb(j, blk, pb, off, rows):
            ev = nxt(ev32, "ev_i")
            S.op("dve", lambda e: e.tensor_copy(out=ev.ap[:rows], in_=pb.ap[:rows]), reads=[pb], writes=[ev])
            outdma("dv", c0 + off, rows, ev, ev.ap, 512 * blk, 512)
            vt = vBt[j]
            S.op("act", lambda e: e.activation(out=vt.ap[:rows, 2 * blk:2 * blk + 2, 0:256],
                                               in_=pb.ap[:rows].rearrange("p (h d) -> p h d", h=2), func=ACTF.Copy), reads=[pb], writes=[vt])
            if blk == 1:
                S.dma(k.vB[c0 + off:c0 + off + rows], vt.ap[:rows].rearrange("p h d -> p (h d)"), reads=[vt], writes=[k.t_scr["vB"][si]])

        def ev_gate(goff):
            def f(j, blk, pb, off, rows):
                g = nxt(gt, "gt_i")
                S.op("act", lambda e: e.activation(out=g.ap[:rows], in_=pb.ap[:rows], func=ACTF.Sigmoid), reads=[pb], writes=[g])
                S.dma(k.gsc[c0 + off:c0 + off + rows, goff + 512 * blk:goff + 512 * blk + 512], g.ap[:rows], reads=[g],
                      writes=[k.t_scr["gs"][si]])
            return f

        evs = {"qa": ev_q("qa"), "ka": ev_ka, "va": ev_va, "fa": ev_fa, "qb": ev_rope("qb"), "kb": ev_rope("kb"),
               "vb": ev_vb, "ga": ev_gate(0), "gb": ev_gate(2048)}
        for g, goff, w in k.groups:
            m = k.mats[(l, g)]
            dense(k, xb, xfn, KC, m, range(m.nblk), subs, evs[g])
            if g in k.scrT:
                S.dma(k.scrT[g][:, :, c0:c0 + T], stT[g].ap[:, :, 0:T], reads=[stT[g]], writes=[k.t_scr[g][si]])


def _consts():
    ident = np.eye(128, dtype=np.float32)
    kk = np.arange(128)[:, None]
    qq = np.arange(128)[None, :]
    cmask = np.where(kk <= qq, 0.0, NEG).astype(np.float32)
    kmask = np.where((kk // 64) <= (qq // 64), 0.0, NEG).astype(np.float32)
    pos = np.zeros(C, dtype=np.float32)
    pos[0:64] = PAST + np.arange(64)
    pos[64:80] = np.arange(16)
    pos[128:] = NM + np.arange(NF)
    half = 16
    inv_freq = (np.float32(500000.0) ** (-np.arange(half, dtype=np.float32) / np.float32(half))).astype(np.float32)
    ang = (pos[:, None] * inv_freq[None, :]).astype(np.float32)
    return {"c_ident": ident, "c_cmask": cmask, "c_kmask": kmask,
            "c_cos": np.cos(ang).astype(np.float32), "c_sin": np.sin(ang).astype(np.float32)}


def make_in_maps(inp):
    cs = _consts()
    f = lambda a: np.ascontiguousarray(np.asarray(a, dtype=np.float32))
    shared = {
        "meta": f(inp["meta_tokens"]), "ln_in_g": f(inp["ln_in_g"]), "ln_in_b": f(inp["ln_in_b"]),
        "w_in": f(inp["w_in"]), "b_f": f(inp["b_f"]),
        "lq1": f(inp["lambda_q1"]), "lk1": f(inp["lambda_k1"]), "lq2": f(inp["lambda_q2"]), "lk2": f(inp["lambda_k2"]),
        "subg": f(inp["subln_g"]), "w_br_a": f(inp["w_br_a"]), "w_br_b": f(inp["w_br_b"]), "w_out": f(inp["w_out"]),
        "ln1_g": f(inp["ln1_g"]), "ln1_b": f(inp["ln1_b"]), "w_up": f(inp["w_up"]), "w_down": f(inp["w_down"]),
        "ln2_g": f(inp["ln2_g"]), "ln2_b": f(inp["ln2_b"]),
    }
    shared.update(cs)
    maps = []
    for c in range(8):
        m = dict(shared)
        m["x_p"] = f(inp["x_prompt"][c]); m["x_s"] = f(inp["x_sample"][c])
        m["cfk"] = f(np.asarray(inp["cache_fox_k"])[:, c].reshape(DEPTH, PAST, 1024))
        m["cfv"] = f(np.asarray(inp["cache_fox_v"])[:, c].reshape(DEPTH, PAST, 1024))
        m["cfl"] = f(np.asarray(inp["cache_fox_logf"])[:, c])
        m["cdk"] = f(np.asarray(inp["cache_diff_k"])[:, c].reshape(DEPTH, PAST, 1024))
        m["cdv"] = f(np.asarray(inp["cache_diff_v"])[:, c].reshape(DEPTH, PAST, 1024))
        maps.append(m)
    return maps


def gather(results):
    st = lambda n: np.stack([np.asarray(r[n]) for r in results], axis=0)
    y_p = st("y_p"); y_s = st("y_s")

    def kv(n, shp):
        a = st(n)
        a = np.moveaxis(a, 0, 1)
        return np.ascontiguousarray(a.reshape(a.shape[:3] + shp))
    return (y_p, y_s,
            kv("fk_p", (8, 128)), kv("fv_p", (8, 128)), kv("fl_p", (8,)), kv("dk_p", (4, 256)), kv("dv_p", (4, 256)),
            kv("fk_s", (8, 128)), kv("fv_s", (8, 128)), kv("fl_s", (8,)), kv("dk_s", (4, 256)), kv("dv_s", (4, 256)))


_NC_CACHE = {}


def kernel(**inputs):
    if "nc" not in _NC_CACHE:
        _NC_CACHE["nc"] = build()
    nc = _NC_CACHE["nc"]
    maps = make_in_maps(inputs)
    res = run_bass_kernel_spmd(nc, maps, core_ids=list(range(8)))
    return gather(res.results)


def phase_C(k, l):
    S, A = k.S, k.A
    sts = k.sts
    A.reset()
    wbufs = [A.alloc("wbuf%d" % i, [KC, 512], BF16) for i in range(2)]
    XT = A.alloc("XT", [KC, 512], BF16)
    xres = [A.alloc("xres%d" % i, [D], F32) for i in range(4)]
    m16 = [A.alloc("m16_%d" % i, [D], BF16) for i in range(4)]
    brt = [A.alloc("brt%d" % i, [512], F32) for i in range(2)]
    reg0 = A.off
    oTa = A.alloc("oTa", [8, 512], BF16)
    oTb = A.alloc("oTb", [8, 512], BF16)
    gpa = [A.alloc("gpa%d" % i, [512], BF16) for i in range(4)]
    gpb = [A.alloc("gpb%d" % i, [512], BF16) for i in range(4)]
    brt += [A.alloc("brt%d" % (i + 2), [512], F32) for i in range(6)]
    OV = [oTa, oTb] + gpa + gpb + brt[2:8]
    hidT = A.alloc_at("hidT", reg0, [64, 512], BF16)
    k.gp_i = 0; k.brt_i = 0
    lnp = k.lnp
    S.dma(lnp[0].ap, k.ln1_g[l].partition_broadcast(128), writes=[lnp[0]])
    S.dma(lnp[1].ap, k.ln1_b[l].partition_broadcast(128), writes=[lnp[1]])
    S.dma(lnp[2].ap, k.ln2_g[l].partition_broadcast(128), writes=[lnp[2]])
    S.dma(lnp[3].ap, k.ln2_b[l].partition_broadcast(128), writes=[lnp[3]])
    M = k.mats
    ws = WStream(S, 2, wbufs)
    order = []
    for si in range(len(sts)):
        for blk in range(4):
            order.append(M[(l, "bra")].piece(blk, 0, 8)); order.append(M[(l, "brb")].piece(blk, 0, 8))
        for blk in range(4):
            order.append(M[(l, "out")].piece(blk))
        for blk in range(16):
            order.append(M[(l, "up")].piece(blk))
        for blk in range(4):
            for pi in range(4):
                order.append(M[(l, "down")].piece(blk, 16 * pi, 16))
    ws.plan(order)
    k.ws = ws

    for si, (c0, subs) in enumerate(sts):
        T = subs[-1][0] + subs[-1][1]
        S.dma(oTa.ap[:, :, 0:T], k.oaT[:, :, c0:c0 + T], reads=[k.t_oa[si]], writes=[oTa])
        S.dma(oTb.ap[:, :, 0:T], k.obT[:, :, c0:c0 + T], reads=[k.t_ob[si]], writes=[oTb])
        for j, (off, rows) in enumerate(subs):
            S.dma(xres[j].ap[:rows], k.hbuf[c0 + off:c0 + off + rows], reads=[k.t_h[si]], writes=[xres[j]])
        gps = {}

        def ev_bra(j, blk, pb, off, rows):
            g = gpa[j]
            S.dma(g.ap[:rows], k.gsc[c0 + off:c0 + off + rows, 512 * blk:512 * blk + 512], reads=[k.t_scr["gs"][si]], writes=[g])
            t = brt[k.brt_i % 8]; k.brt_i += 1
            gps[(j, blk, "t")] = t
            S.op("dve", lambda e: e.tensor_tensor(out=t.ap[:rows], in0=pb.ap[:rows], in1=g.ap[:rows], op=ALU.mult), reads=[pb, g], writes=[t])

        def ev_brb(j, blk, pb, off, rows):
            g = gpb[j]; t = gps[(j, blk, "t")]
            S.dma(g.ap[:rows], k.gsc[c0 + off:c0 + off + rows, 2048 + 512 * blk:2048 + 512 * blk + 512], reads=[k.t_scr["gs"][si]], writes=[g])
            t2 = brt[k.brt_i % 8]; k.brt_i += 1
            S.op("dve", lambda e: e.tensor_tensor(out=t2.ap[:rows], in0=pb.ap[:rows], in1=g.ap[:rows], op=ALU.mult), reads=[pb, g], writes=[t2])
            S.op("pool", lambda e: e.tensor_tensor(out=m16[j].ap[:rows, 512 * blk:512 * blk + 512], in0=t2.ap[:rows],
                                                   in1=t.ap[:rows], op=ALU.add), reads=[t2, t], writes=[m16[j]])

        for blk in range(4):
            dense(k, oTa, lambda kg, off, rows: oTa.ap[:, kg, off:off + rows], 8, M[(l, "bra")], [blk], subs, ev_bra, kc_piece=8)
            dense(k, oTb, lambda kg, off, rows: oTb.ap[:, kg, off:off + rows], 8, M[(l, "brb")], [blk], subs, ev_brb, kc_piece=8)
        for j, (off, rows) in enumerate(subs):
            for hf in range(2):
                k.transpose_chunks(m16[j], lambda c, j=j, rows=rows, hf=hf: m16[j].ap[:rows, (hf * 8 + c) * 128:(hf * 8 + c + 1) * 128],
                                   8, rows, XT, XT.ap[:, hf * 8:hf * 8 + 8, off:off + rows], "dve" if hf == 0 else "act")

        def ev_res(j, blk, pb, off, rows):
            xs = xres[j].ap[:rows, 512 * blk:512 * blk + 512]
            S.op("dve", lambda e: e.scalar_tensor_tensor(out=xs, in0=xs, scalar=float(ALPHA), in1=pb.ap[:rows], op0=ALU.mult, op1=ALU.add),
                 reads=[xres[j], pb], writes=[xres[j]])

        dense(k, XT, lambda kg, off, rows: XT.ap[:, kg, off:off + rows], KC, M[(l, "out")], range(4), subs, ev_res)
        for j, (off, rows) in enumerate(subs):
            k.layernorm(xres[j], xres[j].ap[:rows], rows, lnp[0], lnp[1], xres[j], xres[j].ap[:rows])
            S.op("act", lambda e, j=j, rows=rows: e.activation(out=m16[j].ap[:rows], in_=xres[j].ap[:rows], func=ACTF.Copy), reads=[xres[j]], writes=[m16[j]])
            for hf in range(2):
                k.transpose_chunks(m16[j], lambda c, j=j, rows=rows, hf=hf: m16[j].ap[:rows, (hf * 8 + c) * 128:(hf * 8 + c + 1) * 128],
                                   8, rows, XT, XT.ap[:, hf * 8:hf * 8 + 8, off:off + rows], "dve" if hf == 0 else "act")
        mu = M[(l, "up")]
        halves = [subs]
        for hs in halves:
            h0 = hs[0][0]
            Th = hs[-1][0] + hs[-1][1] - h0
            for blk in range(16):
                wb = ws.get(mu.piece(blk)[0])
                for hc in range(4):
                    pb = k.banks[k.bank_i % 6]; k.bank_i += 1
                    for kg in range(KC):
                        S.op("pe", lambda e, pb=pb, wb=wb, hc=hc, kg=kg, h0=h0, Th=Th: e.matmul(
                            pb.ap[:, :Th], lhsT=wb.ap[:, kg, hc * 128:(hc + 1) * 128], rhs=XT.ap[:, kg, h0:h0 + Th],
                            start=(kg == 0), stop=(kg == KC - 1)), reads=[XT, wb], writes=[pb])
                    t = brt[k.brt_i % 2]; k.brt_i += 1
                    S.op("act", lambda e, pb=pb, t=t, Th=Th: e.activation(out=t.ap[:, :Th], in_=pb.ap[:, :Th], func=ACTF.Relu), reads=[pb], writes=[t])
                    S.op("pool", lambda e, t=t, blk=blk, hc=hc, Th=Th: e.tensor_tensor(out=hidT.ap[:, blk * 4 + hc, 0:Th], in0=t.ap[:, :Th], in1=t.ap[:, :Th], op=ALU.mult),
                         reads=[t], writes=[hidT] + OV)
            j0 = subs.index(hs[0])

            def ev_res2(j, blk, pb, off, rows, j0=j0):
                ev_res(j + j0, blk, pb, off, rows)
            dense(k, [hidT] + OV, lambda kg, off, rows, h0=h0: hidT.ap[:, kg, off - h0:off - h0 + rows], 64, M[(l, "down")], range(4), hs, ev_res2, kc_piece=16, nbanks=6)
        for j, (off, rows) in enumerate(subs):
            k.layernorm(xres[j], xres[j].ap[:rows], rows, lnp[2], lnp[3], xres[j], xres[j].ap[:rows])
            c = c0 + off
            if l == DEPTH - 1:
                if c < 128:
                    S.dma(k.y_s, xres[j].ap[0:64], reads=[xres[j]])
                else:
                    S.dma(k.y_p[c - 128:c - 128 + rows], xres[j].ap[:rows], reads=[xres[j]])
            else:
                S.dma(k.hbuf[c:c + rows], xres[j].ap[:rows], reads=[xres[j]], writes=[k.t_h[si]])
                S.op("act", lambda e, j=j, rows=rows: e.activation(out=m16[j].ap[:rows], in_=xres[j].ap[:rows], func=ACTF.Copy), reads=[xres[j]], writes=[m16[j]])
                for hf in range(2):
                    k.transpose_chunks(m16[j], lambda cc, j=j, rows=rows, hf=hf: m16[j].ap[:rows, (hf * 8 + cc) * 128:(hf * 8 + cc + 1) * 128],
                                       8, rows, XT, XT.ap[:, hf * 8:hf * 8 + 8, off:off + rows], "dve" if hf == 0 else "act")
        if l < DEPTH - 1:
            S.dma(k.hT[:, :, c0:c0 + T], XT.ap[:, :, 0:T], reads=[XT], writes=[k.t_hT[si]])


def ktile_table():
    kts = [("meta", 64, 16), ("snew", 0, 64)]
    for t in range(32):
        kts.append(("f%d" % t, 128 + 128 * t, 128))
    for t in range(32):
        kts.append(("c%d" % t, C + 128 * t, 128))
    return kts


KT_META, KT_SNEW, KT_F0, KT_C0 = 0, 1, 2, 34


def phase_B(k, l):
    S, A = k.S, k.A
    lam_init = 0.8 - 0.6 * float(np.exp(-0.3 * l))
    kts = ktile_table()
    sts = k.sts
    all_scr = lambda n: list(k.t_scr[n])

    def nxt(lst, attr):
        i = getattr(k, attr, 0)
        setattr(k, attr, i + 1)
        return lst[i % len(lst)]

    A.reset()
    ck32 = [A.alloc("ck32_%d" % i, [1024], F32) for i in range(2)]
    ck16 = [A.alloc("ck16_%d" % i, [1024], BF16) for i in range(2)]
    cstg = [A.alloc("cstg%d" % i, [8, 512], BF16) for i in range(2)]
    cvA = [A.alloc("cvA%d" % i, [8, 130], BF16) for i in range(2)]
    cvB = [A.alloc("cvB%d" % i, [4, 258], BF16) for i in range(2)]
    lT = A.alloc("lT", [C], F32)
    cT = A.alloc("cT", [C], F32)
    caT = A.alloc("caT", [PAST], F32)
    ccT = A.alloc("ccT", [PAST], F32)
    zer = A.alloc("zer", [512], F32)
    cl = A.alloc("cl", [32, 8], F32)
    lam4 = [A.alloc("lam4_%d" % i, [128], F32) for i in range(4)]
    lamt = A.alloc("lamt", [2, 128], F32)
    lamd = A.alloc("lamd", [2], F32)
    for t in cvA + cvB:
        S.op("pool", lambda e, t=t: e.memset(t.ap, 1.0), writes=[t])
    S.op("pool", lambda e: e.memset(zer.ap, 0.0), writes=[zer])
    for i, src in enumerate((k.lq1, k.lk1, k.lq2, k.lk2)):
        S.dma(lam4[i].ap, src[l].partition_broadcast(128), writes=[lam4[i]])
    for j in range(2):
        S.op("dve", lambda e, j=j: e.tensor_tensor(out=lamt.ap[:, j, :], in0=lam4[2 * j].ap, in1=lam4[2 * j + 1].ap, op=ALU.mult),
             reads=[lam4[2 * j], lam4[2 * j + 1]], writes=[lamt])
    S.op("dve", lambda e: e.tensor_reduce(out=lamd.ap, in_=lamt.ap, axis=AX.X, op=ALU.add), reads=[lamt], writes=[lamd])
    S.op("act", lambda e: e.activation(out=lamd.ap, in_=lamd.ap, func=ACTF.Exp), reads=[lamd], writes=[lamd])
    S.op("dve", lambda e: e.tensor_tensor(out=k.nlam.ap, in0=lamd.ap[:, 1:2], in1=lamd.ap[:, 0:1], op=ALU.subtract), reads=[lamd], writes=[k.nlam])
    S.op("dve", lambda e: e.tensor_scalar(out=k.nlam.ap, in0=k.nlam.ap, scalar1=-lam_init, scalar2=None, op0=ALU.add), reads=[k.nlam], writes=[k.nlam])
    S.dma(k.gsb.ap, k.subg[l].partition_broadcast(128), writes=[k.gsb])
    S.op("dve", lambda e: e.tensor_scalar(out=k.gsb.ap, in0=k.gsb.ap, scalar1=1.0 - lam_init, scalar2=None, op0=ALU.mult), reads=[k.gsb], writes=[k.gsb])

    it = 0
    for (src, name) in ((k.cfk, "ka"), (k.cdk, "kb")):
        for g4 in range(8):
            stg = cstg[g4 % 2]
            for tt in range(4):
                t = g4 * 4 + tt
                c32 = ck32[it % 2]; c16 = ck16[it % 2]; it += 1
                S.dma(c32.ap, src[l, t * 128:(t + 1) * 128, :], writes=[c32])
                S.op("pool", lambda e, c32=c32, c16=c16: e.tensor_copy(out=c16.ap, in_=c32.ap), reads=[c32], writes=[c16])
                k.transpose_chunks(c16, lambda c, c16=c16: c16.ap[:, c * 128:(c + 1) * 128], 8, 128, stg,
                                   stg.ap[:, :, tt * 128:(tt + 1) * 128], "act" if tt % 2 else "dve")
            S.dma(k.scrT[name][:, :, C + g4 * 512:C + (g4 + 1) * 512], stg.ap, reads=[stg], writes=[k.t_cache[name]])
    for t in range(32):
        c32 = ck32[it % 2]; it += 1
        vt = cvA[t % 2]
        S.dma(c32.ap, k.cfv[l, t * 128:(t + 1) * 128, :], writes=[c32])
        S.op("pool", lambda e, c32=c32, vt=vt: e.tensor_copy(out=vt.ap[:, :, 0:128], in_=c32.ap.rearrange("p (h d) -> p h d", h=8)), reads=[c32], writes=[vt])
        S.dma(k.vA[C + t * 128:C + (t + 1) * 128], vt.ap.rearrange("p h d -> p (h d)"), reads=[vt], writes=[k.t_cache["vA"]])
        c32 = ck32[it % 2]; it += 1
        vt = cvB[t % 2]
        S.dma(c32.ap, k.cdv[l, t * 128:(t + 1) * 128, :], writes=[c32])
        S.op("pool", lambda e, c32=c32, vt=vt: e.tensor_copy(out=vt.ap[:, :, 0:256], in_=c32.ap.rearrange("p (h d) -> p h d", h=4)), reads=[c32], writes=[vt])
        S.dma(k.vB[C + t * 128:C + (t + 1) * 128], vt.ap.rearrange("p h d -> p (h d)"), reads=[vt], writes=[k.t_cache["vB"]])

    S.dma(lT.ap[0:8], k.lfT_d, reads=[k.t_lfT], writes=[lT])
    S.dma(cl.ap, k.cfl[l].rearrange("(t p) h -> p t h", p=128), writes=[cl])
    for g in range(8):
        pbk, _ = k.trbank()
        for tt in range(4):
            t = g * 4 + tt
            S.op("pe", lambda e, pbk=pbk, tt=tt, t=t: e.matmul(pbk.ap[0:8, tt * 128:(tt + 1) * 128], lhsT=cl.ap[:, t, :], rhs=k.identf.ap,
                                                               start=True, stop=True), reads=[cl, k.identf], writes=[pbk])
        S.op("dve", lambda e, pbk=pbk, g=g: e.tensor_copy(out=caT.ap[0:8, g * 512:(g + 1) * 512], in_=pbk.ap[0:8, :]), reads=[pbk], writes=[caT])

    def scan(dst_b, dst_ap, src_b, src_ap, n, init, init_b=None):
        last = init
        for o in range(0, n, 512):
            w = min(512, n - o)
            ini = 0.0 if last is None else last
            S.op("dve", lambda e, o=o, w=w, ini=ini: e.tensor_tensor_scan(out=dst_ap[0:8, o:o + w], data0=src_ap[0:8, o:o + w], data1=zer.ap[0:8, 0:w],
                                                                         initial=ini, op0=ALU.add, op1=ALU.add),
                 reads=[src_b, zer, dst_b] + ([init_b] if (init_b is not None and o == 0) else []), writes=[dst_b])
            last = dst_ap[0:8, o + w - 1:o + w]
        return last

    last_m = scan(cT, cT.ap[:, 64:80], lT, lT.ap[:, 64:80], 16, None)
    scan(cT, cT.ap[:, 128:C], lT, lT.ap[:, 128:C], NF, last_m)
    last_c = scan(ccT, ccT.ap, caT, caT.ap, PAST, None)
    scan(cT, cT.ap[:, 0:64], lT, lT.ap[:, 0:64], TS, last_c, ccT)
    S.dma(k.cbuf, cT.ap[0:8], reads=[cT], writes=[k.t_cbuf])
    for half, (lo, hi) in enumerate(((0, 64), (64, 66))):
        pbk, _ = k.trbank()
        for kt in range(lo, hi):
            name, col, K_ = kts[kt]
            srcb, srcap = (ccT, ccT.ap[0:8, col - C:col - C + K_]) if kt >= KT_C0 else (cT, cT.ap[0:8, col:col + K_])
            S.op("pe", lambda e, pbk=pbk, kt=kt, lo=lo, K_=K_, srcap=srcap: e.matmul(pbk.ap[0:K_, (kt - lo) * 8:(kt - lo + 1) * 8], lhsT=srcap,
                                                                                   rhs=k.identf.ap[0:8, 0:8], start=True, stop=True),
                 reads=[srcb, k.identf], writes=[pbk])
        S.op("act", lambda e, pbk=pbk, lo=lo, hi=hi: e.activation(out=k.negc.ap[:, lo:hi, :], in_=pbk.ap[:, 0:(hi - lo) * 8].rearrange("p (t h) -> p t h", h=8),
                                                                 func=ACTF.Copy, scale=-1.0), reads=[pbk], writes=[k.negc])

    def groups(nq):
        gs = [(64, 16, [(KT_META, 0, True)])]
        per = nq // 128
        for J in range(NF // nq):
            lst = [(KT_META, 0, False)] + [(KT_F0 + t, 0, False) for t in range(per * J)]
            lst += [(KT_F0 + per * J + i, 128 * i, True) for i in range(per)]
            gs.append((128 + nq * J, nq, lst))
        gs.append((0, 64, [(KT_C0 + t, 0, False) for t in range(32)] + [(KT_SNEW, 0, True)]))
        return gs

    def load_V(Vh, src, h, w):
        S.dma(Vh.ap[0:16, KT_META, :], src[64:80, h * w:(h + 1) * w], reads=all_scr("vA" if w == 130 else "vB"), writes=[Vh])
        S.dma(Vh.ap[0:64, KT_SNEW, :], src[0:64, h * w:(h + 1) * w], reads=[], writes=[Vh])
        S.dma(Vh.ap[:, KT_F0:KT_F0 + 32, :], src[128:C, h * w:(h + 1) * w].rearrange("(t p) d -> p t d", p=128), reads=[], writes=[Vh])
        S.dma(Vh.ap[:, KT_C0:KT_C0 + 32, :], src[C:C + PAST, h * w:(h + 1) * w].rearrange("(t p) d -> p t d", p=128),
              reads=[k.t_cache["vA" if w == 130 else "vB"]], writes=[Vh])

    A.reset()
    Kh = A.alloc("Kh", [C + PAST], BF16)
    Qh = A.alloc("Qh", [C], BF16)
    Vh = A.alloc("Vh", [66, 130], BF16)
    crow = [A.alloc("crow%d" % i, [512], F32) for i in range(2)]
    tt32 = [A.alloc("tt32_%d" % i, [512], F32) for i in range(3)]
    PT = [A.alloc("PT%d" % i, [512], BF16) for i in range(3)]
    o16 = [A.alloc("o16_%d" % i, [256], BF16) for i in range(2)]
    oTs = [A.alloc("oTs%d" % i, [2, 512], BF16) for i in range(2)]
    rz = [A.alloc("rz%d" % i, [4], F32) for i in range(4)]
    osq = [A.alloc("osq%d" % i, [256], F32) for i in range(2)]
    o32 = [A.alloc("o32_%d" % i, [256], F32) for i in range(2)]
    k.sb_i = 0
    banks = k.banks
    fgroups = groups(512)
    for h in range(8):
        S.dma(Kh.ap, k.scrT["ka"][:, h, :], reads=all_scr("ka") + [k.t_cache["ka"]], writes=[Kh])
        S.dma(Qh.ap, k.scrT["qa"][:, h, 0:C], reads=all_scr("qa"), writes=[Qh])
        load_V(Vh, k.vA, h, 130)
        for gi, (qc, Nq, lst) in enumerate(fgroups):
            cr = nxt(crow, "cr_i")
            S.dma(cr.ap[:, 0:Nq], k.cbuf[h, qc:qc + Nq].partition_broadcast(128), reads=[k.t_cbuf], writes=[cr])
            nsub = (Nq + 127) // 128
            first = {}; last = {}
            for idx, (kt, qoff, msk) in enumerate(lst):
                for s_ in range(nsub):
                    if qoff <= s_ * 128:
                        first.setdefault(s_, idx); last[s_] = idx
            for idx, (kt, qoff, msk) in enumerate(lst):
                name, col, K_ = kts[kt]
                n = Nq - qoff
                ps = banks[k.sb_i % 2]; k.sb_i += 1
                S.op("pe", lambda e, ps=ps, K_=K_, n=n, col=col, qc=qc, qoff=qoff, Nq=Nq: e.matmul(
                    ps.ap[:K_, :n], lhsT=Kh.ap[:, col:col + K_], rhs=Qh.ap[:, qc + qoff:qc + Nq], start=True, stop=True), reads=[Kh, Qh], writes=[ps])
                t32 = nxt(tt32, "t32_i"); pt = nxt(PT, "pt_i")
                S.op("dve", lambda e, ps=ps, K_=K_, n=n, t32=t32, cr=cr, qoff=qoff, Nq=Nq: e.scalar_tensor_tensor(
                    out=t32.ap[:K_, :n], in0=ps.ap[:K_, :n], scalar=float(FOX_SCALE), in1=cr.ap[:K_, qoff:Nq], op0=ALU.mult, op1=ALU.add),
                    reads=[ps, cr], writes=[t32])
                if msk:
                    mw = min(128, n)
                    S.op("pool", lambda e, t32=t32, K_=K_, mw=mw: e.tensor_tensor(out=t32.ap[:K_, :mw], in0=t32.ap[:K_, :mw], in1=k.cmask.ap[:K_, :mw], op=ALU.add),
                         reads=[t32, k.cmask], writes=[t32])
                S.op("act", lambda e, t32=t32, pt=pt, K_=K_, n=n, kt=kt, h=h: e.activation(out=pt.ap[:K_, :n], in_=t32.ap[:K_, :n], func=ACTF.Exp,
                                                                                          bias=k.negc.ap[:K_, kt, h:h + 1]), reads=[t32, k.negc], writes=[pt])
                for s_ in range(nsub):
                    if qoff > s_ * 128:
                        continue
                    qs = s_ * 128 - qoff
                    qn = min(128, Nq - s_ * 128)
                    ob = banks[2 + s_]
                    S.op("pe", lambda e, ob=ob, pt=pt, K_=K_, qs=qs, qn=qn, kt=kt, st=(first[s_] == idx), sp=(last[s_] == idx): e.matmul(
                        ob.ap[:qn, 0:129], lhsT=pt.ap[:K_, qs:qs + qn], rhs=Vh.ap[:K_, kt, 0:129], start=st, stop=sp), reads=[pt, Vh], writes=[ob])
            ot = nxt(oTs, "ots_i")
            for s_ in range(nsub):
                qn = min(128, Nq - s_ * 128)
                ob = banks[2 + s_]
                r = nxt(rz, "rz_i"); o6 = nxt(o16, "o16_i")
                S.op("dve", lambda e, ob=ob, r=r, qn=qn: e.reciprocal(out=r.ap[:qn, 0:1], in_=ob.ap[:qn, 128:129]), reads=[ob], writes=[r])
                S.op("dve", lambda e, ob=ob, r=r, o6=o6, qn=qn: e.tensor_scalar(out=o6.ap[:qn, 0:128], in0=ob.ap[:qn, 0:128], scalar1=r.ap[:qn, 0:1], scalar2=None,
                                                                              op0=ALU.mult), reads=[ob, r], writes=[o6])
                k.transpose_chunks(o6, lambda c, o6=o6, qn=qn: o6.ap[:qn, 0:128], 1, qn, ot, ot.ap[:, 0:1, s_ * 128:s_ * 128 + qn], "act")
            S.dma(k.oaT[:, h, qc:qc + Nq], ot.ap[:, 0, 0:Nq], reads=[ot], writes=[k.t_oa[0 if qc < 128 else (qc - 128) // 512 + 1]])

    A.reset()
    Kd = A.alloc("Kd", [2, C + PAST], BF16)
    Qd = A.alloc("Qd", [2, C], BF16)
    Vd = A.alloc("Vd", [66, 258], BF16)
    tt32 = [A.alloc("dtt32_%d" % i, [128], F32) for i in range(3)]
    PT = [A.alloc("dPT%d" % i, [256], BF16) for i in range(4)]
    o16 = [A.alloc("do16_%d" % i, [256], BF16) for i in range(2)]
    oTs = [A.alloc("doTs%d" % i, [2, 256], BF16) for i in range(2)]
    rz = [A.alloc("drz%d" % i, [4], F32) for i in range(4)]
    osq = [A.alloc("dosq%d" % i, [256], F32) for i in range(2)]
    o32 = [A.alloc("do32_%d" % i, [256], F32) for i in range(2)]
    t1 = [A.alloc("dt1_%d" % i, [256], F32) for i in range(2)]
    dgroups = groups(256)
    for h in range(4):
        S.dma(Kd.ap, k.scrT["kb"][:, 2 * h:2 * h + 2, :], reads=all_scr("kb") + [k.t_cache["kb"]], writes=[Kd])
        S.dma(Qd.ap, k.scrT["qb"][:, 2 * h:2 * h + 2, 0:C], reads=all_scr("qb"), writes=[Qd])
        load_V(Vd, k.vB, h, 258)
        for gi, (qc, Nq, lst) in enumerate(dgroups):
            nsub = (Nq + 127) // 128
            first = {}; last = {}
            for idx, (kt, qoff, msk) in enumerate(lst):
                for s_ in range(nsub):
                    if qoff <= s_ * 128:
                        first.setdefault(s_, idx); last[s_] = idx
            is_meta_or_s = qc < 128
            for idx, (kt, qoff, msk) in enumerate(lst):
                name, col, K_ = kts[kt]
                n = Nq - qoff
                msk = msk and not is_meta_or_s
                ps = banks[k.sb_i % 2]; k.sb_i += 1
                psv = ps.ap.rearrange("p (c n) -> p c n", c=2)
                pts = []
                for c_ in range(2):
                    S.op("pe", lambda e, psv=psv, c_=c_, K_=K_, n=n, col=col, qc=qc, qoff=qoff, Nq=Nq: e.matmul(
                        psv[:K_, c_, :n], lhsT=Kd.ap[:, c_, col:col + K_], rhs=Qd.ap[:, c_, qc + qoff:qc + Nq], start=True, stop=True),
                        reads=[Kd, Qd], writes=[ps])
                for c_ in range(2):
                    pt = nxt(PT, "dpt_i"); pts.append(pt)
                    if msk:
                        mw = min(128, n)
                        t32 = nxt(tt32, "dt32_i")
                        S.op("dve", lambda e, psv=psv, c_=c_, K_=K_, mw=mw, t32=t32: e.scalar_tensor_tensor(
                            out=t32.ap[:K_, :mw], in0=psv[:K_, c_, :mw], scalar=float(DIFF_SCALE), in1=k.kmask.ap[:K_, :mw], op0=ALU.mult, op1=ALU.add),
                            reads=[ps, k.kmask], writes=[t32])
                        S.op("act", lambda e, t32=t32, pt=pt, K_=K_, mw=mw: e.activation(out=pt.ap[:K_, :mw], in_=t32.ap[:K_, :mw], func=ACTF.Exp),
                             reads=[t32], writes=[pt])
                        if n > mw:
                            S.op("act", lambda e, psv=psv, c_=c_, pt=pt, K_=K_, mw=mw, n=n: e.activation(out=pt.ap[:K_, mw:n], in_=psv[:K_, c_, mw:n], func=ACTF.Exp,
                                                                                                        scale=float(DIFF_SCALE)), reads=[ps], writes=[pt])
                    else:
                        S.op("act", lambda e, psv=psv, c_=c_, pt=pt, K_=K_, n=n: e.activation(out=pt.ap[:K_, :n], in_=psv[:K_, c_, :n], func=ACTF.Exp,
                                                                                            scale=float(DIFF_SCALE)), reads=[ps], writes=[pt])
                for c_ in range(2):
                    pt = pts[c_]
                    for s_ in range(nsub):
                        if qoff > s_ * 128:
                            continue
                        qs = s_ * 128 - qoff
                        qn = min(128, Nq - s_ * 128)
                        ob = banks[2 + 2 * c_ + s_]
                        S.op("pe", lambda e, ob=ob, pt=pt, K_=K_, qs=qs, qn=qn, kt=kt, st=(first[s_] == idx), sp=(last[s_] == idx): e.matmul(
                            ob.ap[:qn, 0:257], lhsT=pt.ap[:K_, qs:qs + qn], rhs=Vd.ap[:K_, kt, 0:257], start=st, stop=sp), reads=[pt, Vd], writes=[ob])
            ot = nxt(oTs, "dots_i")
            for s_ in range(nsub):
                qn = min(128, Nq - s_ * 128)
                ob0 = banks[2 + s_]; ob1 = banks[4 + s_]
                r = nxt(rz, "drz_i"); o6 = nxt(o16, "do16_i"); tt = nxt(t1, "dt1_i"); oo = nxt(o32, "do32_i"); sq = nxt(osq, "dosq_i")
                S.op("dve", lambda e, ob0=ob0, r=r, qn=qn: e.reciprocal(out=r.ap[:qn, 0:1], in_=ob0.ap[:qn, 256:257]), reads=[ob0], writes=[r])
                S.op("dve", lambda e, ob1=ob1, r=r, qn=qn: e.reciprocal(out=r.ap[:qn, 1:2], in_=ob1.ap[:qn, 256:257]), reads=[ob1], writes=[r])
                S.op("dve", lambda e, r=r, qn=qn: e.tensor_tensor(out=r.ap[:qn, 1:2], in0=r.ap[:qn, 1:2], in1=k.nlam.ap[:qn], op=ALU.mult), reads=[r, k.nlam], writes=[r])
                S.op("dve", lambda e, ob0=ob0, r=r, tt=tt, qn=qn: e.tensor_scalar(out=tt.ap[:qn], in0=ob0.ap[:qn, 0:256], scalar1=r.ap[:qn, 0:1], scalar2=None, op0=ALU.mult),
                     reads=[ob0, r], writes=[tt])
                S.op("dve", lambda e, ob1=ob1, r=r, tt=tt, oo=oo, qn=qn: e.scalar_tensor_tensor(out=oo.ap[:qn], in0=ob1.ap[:qn, 0:256], scalar=r.ap[:qn, 1:2], in1=tt.ap[:qn],
                                                                                            op0=ALU.mult, op1=ALU.add), reads=[ob1, r, tt], writes=[oo])
                S.op("act", lambda e, oo=oo, sq=sq, r=r, qn=qn: e.activation(out=sq.ap[:qn], in_=oo.ap[:qn], func=ACTF.Square, accum_out=r.ap[:qn, 2:3]),
                     reads=[oo], writes=[sq, r])
                S.op("dve", lambda e, r=r, qn=qn: e.tensor_scalar(out=r.ap[:qn, 2:3], in0=r.ap[:qn, 2:3], scalar1=1.0 / 256.0, scalar2=1e-5, op0=ALU.mult, op1=ALU.add),
                     reads=[r], writes=[r])
                S.op("act", lambda e, r=r, qn=qn: e.activation(out=r.ap[:qn, 2:3], in_=r.ap[:qn, 2:3], func=ACTF.Ln), reads=[r], writes=[r])
                S.op("act", lambda e, r=r, qn=qn: e.activation(out=r.ap[:qn, 2:3], in_=r.ap[:qn, 2:3], func=ACTF.Exp, scale=-0.5), reads=[r], writes=[r])
                S.op("dve", lambda e, oo=oo, r=r, o6=o6, qn=qn: e.scalar_tensor_tensor(out=o6.ap[:qn], in0=oo.ap[:qn], scalar=r.ap[:qn, 2:3], in1=k.gsb.ap[:qn],
                                                                                   op0=ALU.mult, op1=ALU.mult), reads=[oo, r, k.gsb], writes=[o6])
                k.transpose_chunks(o6, lambda c, o6=o6, qn=qn: o6.ap[:qn, c * 128:(c + 1) * 128], 2, qn, ot, ot.ap[:, :, s_ * 128:s_ * 128 + qn], "act")
            si = 0 if qc < 128 else (qc - 128) // 512 + 1
            S.dma(k.obT[:, 2 * h:2 * h + 2, qc:qc + Nq], ot.ap[:, :, 0:Nq], reads=[ot], writes=[k.t_ob[si]])
```

```python
import contextlib
import numpy as np
import concourse.bass as bass
import concourse.mybir as mybir
from concourse.bass_utils import run_bass_kernel_spmd

F32 = mybir.dt.float32
BF16 = mybir.dt.bfloat16
ALU = mybir.AluOpType
ACTF = mybir.ActivationFunctionType
AX = mybir.AxisListType


class Buf:
    __slots__ = ("name", "ap", "kind", "writers", "readers", "sem", "semcnt")

    def __init__(self, name, ap=None, kind="sbuf"):
        self.name = name
        self.ap = ap
        self.kind = kind
        self.writers = []
        self.readers = []
        self.sem = None
        self.semcnt = 0


class Op:
    __slots__ = ("eng", "emit", "deps", "is_dma", "sem", "val", "sig", "idx", "rawdeps")

    def __init__(self, eng, emit):
        self.eng = eng
        self.emit = emit
        self.deps = []
        self.rawdeps = set()
        self.is_dma = False
        self.sem = None
        self.val = 0
        self.sig = False


class Sched:
    ENGS = ("pe", "act", "dve", "pool", "sp")

    def __init__(self, nc):
        self.nc = nc
        self.stack = contextlib.ExitStack()
        self.ops = {e: [] for e in self.ENGS}
        self.all_ops = []
        self.esem = {}
        for e in ("pe", "act", "dve", "pool"):
            self.esem[e] = self.stack.enter_context(nc.semaphore("sem_" + e))
        self.dma_bufs = []
        self.free_sems = []
        self.store_eng = "sp"
        self.nsem = 4
        self.n_names = 0

    def sbuf(self, name, shape, dtype):
        t = self.stack.enter_context(self.nc.sbuf_tensor(name, list(shape), dtype))
        return Buf(name, t.ap() if hasattr(t, "ap") and callable(t.ap) else t[:], "sbuf")

    def psum(self, name, shape, dtype):
        t = self.stack.enter_context(self.nc.psum_tensor(name, list(shape), dtype))
        return Buf(name, t.ap() if hasattr(t, "ap") and callable(t.ap) else t[:], "psum")

    def dram(self, name, shape, dtype, kind="Internal"):
        return self.nc.dram_tensor(name, list(shape), dtype, kind=kind).ap()

    def tok(self, name):
        return Buf(name, None, "dram")

    def sub(self, buf, name=None):
        return Buf(name or buf.name + "_s", buf.ap, buf.kind)

    def _record(self, op, reads, writes):
        deps = op.deps
        for r in reads:
            for w_ in r.writers:
                deps.append(w_)
                op.rawdeps.add(id(w_))
            if r.kind == "psum":
                deps.extend(r.readers)
        for w in writes:
            if w.kind == "dram" and not w.readers:
                continue
            deps.extend(w.writers)
            deps.extend(w.readers)
        for r in reads:
            r.readers.append(op)
        for w in writes:
            if w.kind == "dram" and not w.readers:
                w.writers.append(op)
            else:
                w.writers = [op]
                w.readers = []
        op.idx = len(self.ops[op.eng])
        self.ops[op.eng].append(op)
        self.all_ops.append(op)

    def op(self, eng, emit, reads=(), writes=()):
        o = Op(eng, emit)
        self._record(o, reads, writes)
        return o

    def dma(self, out, in_, reads=(), writes=(), eng="sp", **kw):
        if eng == "sp" and not any(b.kind == "sbuf" for b in writes):
            eng = self.store_eng
        o = Op(eng, lambda e: e.dma_start(out=out, in_=in_, **kw))
        o.is_dma = True
        owner = None
        for b in list(writes) + list(reads):
            if b.kind == "sbuf":
                owner = b
                break
        if owner is None:
            owner = (list(writes) + list(reads))[0]
        if owner.sem is None:
            if self.free_sems:
                owner.sem, owner.semcnt = self.free_sems.pop()
            else:
                owner.sem = self.stack.enter_context(self.nc.semaphore("dsem%d" % self.nsem))
                self.nsem += 1
            self.dma_bufs.append(owner)
        owner.semcnt += 16
        o.sem = owner.sem
        o.val = owner.semcnt
        o.sig = True
        self._record(o, reads, writes)
        return o

    @staticmethod
    def _needs_wait(op, d):
        if d.is_dma:
            return True
        if d.eng != op.eng:
            return True
        if op.eng == "pe":
            return False
        return id(d) in op.rawdeps

    def finish(self):
        nc = self.nc
        for op in self.all_ops:
            for d in op.deps:
                if not d.is_dma and self._needs_wait(op, d):
                    d.sig = True
        for e in ("pe", "act", "dve", "pool"):
            c = 0
            for op in self.ops[e]:
                if op.is_dma:
                    continue
                if op.sig:
                    c += 1
                    op.val = c
                    op.sem = self.esem[e]
        fin = {}
        for b in self.dma_bufs:
            if id(b.sem) not in fin or fin[id(b.sem)][1] < b.semcnt:
                fin[id(b.sem)] = (b.sem, b.semcnt)
        finals = list(fin.values())
        self.nwaits = 0

        def emit_engine(ename, eng):
            seen = {}
            for op in self.ops[ename]:
                need = {}
                for d in op.deps:
                    if not self._needs_wait(op, d):
                        continue
                    k = id(d.sem)
                    if seen.get(k, 0) >= d.val:
                        continue
                    if k not in need or need[k][1] < d.val:
                        need[k] = (d.sem, d.val)
                for k, (sem, val) in need.items():
                    eng.wait_ge(sem, val)
                    seen[k] = val
                    self.nwaits += 1
                inst = op.emit(eng)
                if op.sig:
                    inst.then_inc(op.sem, 16 if op.is_dma else 1)
            if ename == "sp":
                for sem, val in finals:
                    eng.wait_ge(sem, val)

        with nc.Block() as block:
            @block.tensor
            def _(e):
                emit_engine("pe", e)

            @block.scalar
            def _(e):
                emit_engine("act", e)

            @block.vector
            def _(e):
                emit_engine("dve", e)

            @block.gpsimd
            def _(e):
                emit_engine("pool", e)

            @block.sync
            def _(e):
                emit_engine("sp", e)
        self.stack.close()


class Arena:
    def __init__(self, S, nbytes):
        self.S = S
        self.n32 = nbytes // 4
        self.base = S.sbuf("arena", [128, self.n32], F32)
        self.off = 0
        self.cur = []
        self.prev_ops = []

    def reset(self):
        ops = {}
        for b in self.cur:
            for w_ in b.writers:
                ops[id(w_)] = w_
            for r in b.readers:
                ops[id(r)] = r
        self.prev_ops = list(ops.values())
        for b in self.cur:
            if b.sem is not None:
                self.S.free_sems.append((b.sem, b.semcnt))
        self.cur = []
        self.off = 0

    def alloc_at(self, name, off32, free_shape, dtype):
        save = self.off
        self.off = off32
        b = self.alloc(name, free_shape, dtype)
        self.off = max(save, self.off)
        return b

    def alloc(self, name, free_shape, dtype):
        esz = 4 if dtype == F32 else 2
        n = 1
        for s in free_shape:
            n *= s
        n32 = (n * esz + 3) // 4
        n32 = (n32 + 7) // 8 * 8
        assert self.off + n32 <= self.n32, "arena overflow %s: %d + %d > %d" % (name, self.off, n32, self.n32)
        ap = self.base.ap[:, self.off:self.off + n32]
        if dtype != F32:
            ap = ap.bitcast(dtype)
        ap = ap[:, 0:n]
        if len(free_shape) == 2:
            ap = ap.rearrange("p (a b) -> p a b", a=free_shape[0])
        elif len(free_shape) == 3:
            ap = ap.rearrange("p (a b c) -> p a b c", a=free_shape[0], b=free_shape[1])
        self.off += n32
        b = Buf(name, ap, "sbuf")
        b.readers = list(self.prev_ops)
        self.cur.append(b)
        return b


D = 2048
KC = 16
NF = 4096
NM = 16
L = NM + NF
TS = 64
PAST = 4096
DEPTH = 2
C = 128 + NF
NIN = 10248
DFF = 8192
ALPHA = (2 * DEPTH) ** 0.25
FOX_SCALE = 128 ** -0.5
DIFF_SCALE = 128 ** -0.5
NEG = -1e30
GROUPS = [("qa", 0, 1024), ("ka", 1024, 1024), ("va", 2048, 1024), ("fa", 3072, 8),
          ("qb", 3080, 1024), ("kb", 4104, 1024), ("vb", 5128, 1024), ("ga", 6152, 2048), ("gb", 8200, 2048)]


def supertiles():
    sts = [(0, [(0, 80)])]
    for j in range(8):
        sts.append((128 + 512 * j, [(128 * i, 128) for i in range(4)]))
    return sts


class WStream:
    def __init__(self, S, nbuf, bufs):
        self.S = S
        self.bufs = bufs
        self.nbuf = nbuf
        self.order = []
        self.next_load = 0
        self.next_use = 0

    def plan(self, pieces):
        self.order = pieces

    def _issue(self, i):
        name, ap, tok = self.order[i]
        b = self.bufs[i % self.nbuf]
        kcp, ncols = ap.shape[1], ap.shape[2]
        self.S.dma(b.ap[:, 0:kcp, 0:ncols], ap, reads=list(tok), writes=[b])

    def get(self, name):
        i = self.next_use
        assert self.order[i][0] == name, (self.order[i][0], name)
        while self.next_load < len(self.order) and self.next_load < i + self.nbuf:
            self._issue(self.next_load)
            self.next_load += 1
        self.next_use += 1
        return self.bufs[i % self.nbuf]


class WMat:
    def __init__(self, S, name, src, bw=512):
        self.name = name
        self.src = src
        self.K, self.N = src.shape
        self.kc = self.K // 128
        self.bw = min(bw, self.N)
        self.nblk = self.N // self.bw
        self.scr = S.dram(name + "_bf", [self.nblk, 128, self.kc, self.bw], BF16)
        self.cp = min(self.N, 2048)
        self.ncp = self.N // self.cp
        self.toks = [[S.tok("%s_t%d_%d" % (name, k, c)) for c in range(self.ncp)] for k in range(self.kc)]

    def piece(self, blk, kc0=0, kcp=None):
        kcp = self.kc if kcp is None else kcp
        cpi = (blk * self.bw) // self.cp
        toks = [self.toks[k][cpi] for k in range(kc0, kc0 + kcp)]
        return ("%s_b%d_k%d" % (self.name, blk, kc0), self.scr[blk, :, kc0:kc0 + kcp, :], toks)


def cast_weights(S, mats, stage32, stage16):
    engs = ("act", "dve", "pool")
    items = [(m, k, c) for m in mats for k in range(m.kc) for c in range(m.ncp)]
    n32 = len(stage32)

    def load(i):
        m, k, c = items[i]
        w = m.cp
        s32 = stage32[i % n32]
        S.dma(s32.ap[:, 0:w], m.src[k * 128:(k + 1) * 128, c * w:(c + 1) * w], writes=[s32])

    LA = n32 - 1
    for i in range(min(LA, len(items))):
        load(i)
    for i, (m, k, c) in enumerate(items):
        if i + LA < len(items):
            load(i + LA)
        s32 = stage32[i % n32]
        s16 = stage16[i % len(stage16)]
        w = m.cp
        e = engs[i % 3]
        if e == "act":
            S.op("act", lambda en, s32=s32, s16=s16, w=w: en.activation(out=s16.ap[:, 0:w], in_=s32.ap[:, 0:w], func=ACTF.Copy),
                 reads=[s32], writes=[s16])
        else:
            S.op(e, lambda en, s32=s32, s16=s16, w=w: en.tensor_copy(out=s16.ap[:, 0:w], in_=s32.ap[:, 0:w]),
                 reads=[s32], writes=[s16])
        nb = w // m.bw
        b0 = (c * w) // m.bw
        S.dma(m.scr[b0:b0 + nb, :, k, :].rearrange("b p n -> p b n"),
              s16.ap[:, 0:w].rearrange("p (b n) -> p b n", b=nb), reads=[s16], writes=[m.toks[k][c]])


class BGCast:
    def __init__(self, S, mats):
        self.S = S
        self.items = [(m, k, c) for m in mats for k in range(m.kc) for c in range(m.ncp)]
        self.pos = 0
        self.pending = []
        self.n = 0

    def issue(self, n, s32):
        assert not self.pending
        for j in range(n):
            if self.pos >= len(self.items):
                break
            m, k, c = self.items[self.pos]
            self.pos += 1
            b = s32[j]
            w = m.cp
            self.S.dma(b.ap[:, 0:w], m.src[k * 128:(k + 1) * 128, c * w:(c + 1) * w], writes=[b])
            self.pending.append((m, k, c, b))

    def finish(self, s16):
        engs = ("act", "dve", "pool")
        for j, (m, k, c, b) in enumerate(self.pending):
            o = s16[j]
            w = m.cp
            e = engs[self.n % 3]
            self.n += 1
            if e == "act":
                self.S.op("act", lambda en, b=b, o=o, w=w: en.activation(out=o.ap[:, 0:w], in_=b.ap[:, 0:w], func=ACTF.Copy), reads=[b], writes=[o])
            else:
                self.S.op(e, lambda en, b=b, o=o, w=w: en.tensor_copy(out=o.ap[:, 0:w], in_=b.ap[:, 0:w]), reads=[b], writes=[o])
            nb = w // m.bw
            b0 = (c * w) // m.bw
            self.S.dma(m.scr[b0:b0 + nb, :, k, :].rearrange("b p n -> p b n"),
                       o.ap[:, 0:w].rearrange("p (b n) -> p b n", b=nb), reads=[o], writes=[m.toks[k][c]])
        self.pending = []

    def done(self):
        return self.pos >= len(self.items) and not self.pending


def out_rows(c, rows):
    if c < 128:
        return [(0, 64, "s", 0), (64, 80, "p", 0)]
    return [(0, rows, "p", c - 128 + NM)]


class K:
    pass


def build(stop_after=None):
    nc = bass.Bass("TRN2", target_bir_lowering=False)
    S = Sched(nc)
    k = K()

    def I(name, shape):
        return nc.dram_tensor(name, list(shape), F32, kind="ExternalInput").ap()

    def O(name, shape):
        return nc.dram_tensor(name, list(shape), F32, kind="ExternalOutput").ap()

    x_p = I("x_p", [NF, D]); x_s = I("x_s", [TS, D]); meta = I("meta", [NM, D])
    cfk = I("cfk", [DEPTH, PAST, 1024]); cfv = I("cfv", [DEPTH, PAST, 1024]); cfl = I("cfl", [DEPTH, PAST, 8])
    cdk = I("cdk", [DEPTH, PAST, 1024]); cdv = I("cdv", [DEPTH, PAST, 1024])
    ln_in_g = I("ln_in_g", [D]); ln_in_b = I("ln_in_b", [D])
    w_in = I("w_in", [DEPTH, D, NIN]); b_f = I("b_f", [DEPTH, 8])
    lq1 = I("lq1", [DEPTH, 128]); lk1 = I("lk1", [DEPTH, 128]); lq2 = I("lq2", [DEPTH, 128]); lk2 = I("lk2", [DEPTH, 128])
    subg = I("subg", [DEPTH, 256])
    w_br_a = I("w_br_a", [DEPTH, 1024, D]); w_br_b = I("w_br_b", [DEPTH, 1024, D]); w_out = I("w_out", [DEPTH, D, D])
    ln1_g = I("ln1_g", [DEPTH, D]); ln1_b = I("ln1_b", [DEPTH, D])
    w_up = I("w_up", [DEPTH, D, DFF]); w_down = I("w_down", [DEPTH, DFF, D])
    ln2_g = I("ln2_g", [DEPTH, D]); ln2_b = I("ln2_b", [DEPTH, D])
    c_ident = I("c_ident", [128, 128]); c_cmask = I("c_cmask", [128, 128]); c_kmask = I("c_kmask", [128, 128])
    c_cos = I("c_cos", [C, 16]); c_sin = I("c_sin", [C, 16])

    y_p = O("y_p", [NF, D]); y_s = O("y_s", [TS, D])
    outs = {}
    for nm, w in (("fk", 1024), ("fv", 1024), ("fl", 8), ("dk", 1024), ("dv", 1024)):
        outs[nm + "p"] = O(nm + "_p", [DEPTH, L, w])
        outs[nm + "s"] = O(nm + "_s", [DEPTH, TS, w])

    import os
    hbuf = S.dram("hbuf", [C, D], F32, kind=("ExternalOutput" if os.environ.get("DBG_DUMP") else "Internal"))
    hT = S.dram("hT", [128, KC, C], BF16)
    scrT = {n: S.dram(n + "T", [128, 8, C + PAST], BF16) for n in ("qa", "ka", "qb", "kb")}
    vA = S.dram("vA", [C + PAST, 8 * 130], BF16)
    vB = S.dram("vB", [C + PAST, 4 * 258], BF16)
    gsc = S.dram("gsc", [C, 4096], BF16)
    import os
    dk_ = "ExternalOutput" if os.environ.get("DBG_DUMP") else "Internal"
    oaT = S.dram("oaT", [128, 8, C], BF16, kind=dk_)
    obT = S.dram("obT", [128, 8, C], BF16, kind=dk_)
    t_oa = [S.tok("oa_st%d" % i) for i in range(9)]
    t_ob = [S.tok("ob_st%d" % i) for i in range(9)]
    lfbuf = S.dram("lfbuf", [C, 8], F32)
    lfT_d = S.dram("lfT_d", [8, C], F32)
    cbuf = S.dram("cbuf", [8, C], F32, kind=("ExternalOutput" if os.environ.get("DBG_DUMP") else "Internal"))
    t_lfT = S.tok("lfT_tok")
    t_cbuf = S.tok("cbuf_tok")
    t_cache = {n: S.tok("cache_" + n) for n in ("ka", "kb", "vA", "vB")}
    t_h = [S.tok("h_st%d" % i) for i in range(9)]
    t_hT = [S.tok("hT_st%d" % i) for i in range(9)]
    t_scr = {n: [S.tok("%s_st%d" % (n, i)) for i in range(9)] for n in ("qa", "ka", "qb", "kb", "vA", "vB", "gs", "lf")}

    identf = S.sbuf("identf", [128, 128], F32)
    identb = S.sbuf("identb", [128, 128], BF16)
    cmask = S.sbuf("cmask", [128, 128], F32)
    kmask = S.sbuf("kmask", [128, 128], F32)
    lnp = [S.sbuf("lnp%d" % i, [128, D], F32) for i in range(4)]
    bfb = S.sbuf("bfb", [128, 8], F32)
    nstat = 4
    st_bn = [S.sbuf("st_bn%d" % i, [128, 4, 6], F32) for i in range(nstat)]
    st_mv = [S.sbuf("st_mv%d" % i, [128, 2], F32) for i in range(nstat)]
    st_rs = [S.sbuf("st_rs%d" % i, [128, 1], F32) for i in range(nstat)]
    k.stat_i = 0
    banks = [S.psum("bank%d" % i, [128, 512], F32) for i in range(8)]
    banks16 = [Buf("bank%d_16" % i, b.ap.bitcast(BF16), "psum") for i, b in enumerate(banks)]
    for b16, b in zip(banks16, banks):
        pass
    k.tr_i = 0
    k.bank_i = 0
    A = Arena(S, 170 * 1024)
    negc = S.sbuf("negc", [128, 66, 8], F32)
    nlam = S.sbuf("nlam", [128, 1], F32)
    gsb = S.sbuf("gsb", [128, 256], F32)

    S.dma(identf.ap, c_ident, writes=[identf])
    S.op("dve", lambda e: e.tensor_copy(out=identb.ap, in_=identf.ap), reads=[identf], writes=[identb])
    S.dma(cmask.ap, c_cmask, writes=[cmask])
    S.dma(kmask.ap, c_kmask, writes=[kmask])

    def trbank():
        i = 6 + (k.tr_i % 2)
        k.tr_i += 1
        return banks[i], banks16[i].ap

    def transpose_chunks(src_buf, src_fn, n, rows, dst_buf, dst_ap, eng):
        pb, p16 = trbank()
        for c in range(n):
            S.op("pe", lambda e, c=c: e.transpose(p16[:, c * 128:c * 128 + rows], src_fn(c), identb.ap[:rows, :rows]),
                 reads=[src_buf, identb], writes=[pb])
        src = p16[:, 0:n * 128].rearrange("p (c r) -> p c r", c=n)[:, :, 0:rows]
        if eng == "act":
            S.op("act", lambda e: e.activation(out=dst_ap, in_=src, func=ACTF.Copy), reads=[pb], writes=[dst_buf])
        else:
            S.op(eng, lambda e: e.tensor_copy(out=dst_ap, in_=src), reads=[pb], writes=[dst_buf])

    def layernorm(xb, x_ap, rows, g_b, b_b, out_b, out_ap):
        i = k.stat_i % nstat
        k.stat_i += 1
        bn, mv, rs = st_bn[i], st_mv[i], st_rs[i]
        for c in range(4):
            S.op("dve", lambda e, c=c: e.bn_stats(out=bn.ap[:rows, c, :], in_=x_ap[:, c * 512:(c + 1) * 512]), reads=[xb], writes=[bn])
        S.op("dve", lambda e: e.bn_aggr(out=mv.ap[:rows], in_=bn.ap[:rows]), reads=[bn], writes=[mv])
        S.op("dve", lambda e: e.tensor_scalar(out=rs.ap[:rows], in0=mv.ap[:rows, 1:2], scalar1=1e-5, scalar2=None, op0=ALU.add),
             reads=[mv], writes=[rs])
        S.op("act", lambda e: e.activation(out=rs.ap[:rows], in_=rs.ap[:rows], func=ACTF.Ln), reads=[rs], writes=[rs])
        S.op("act", lambda e: e.activation(out=rs.ap[:rows], in_=rs.ap[:rows], func=ACTF.Exp, scale=-0.5), reads=[rs], writes=[rs])
        S.op("dve", lambda e: e.tensor_scalar(out=out_ap, in0=x_ap, scalar1=mv.ap[:rows, 0:1], scalar2=rs.ap[:rows],
                                              op0=ALU.subtract, op1=ALU.mult), reads=[xb, mv, rs], writes=[out_b])
        S.op("pool", lambda e: e.tensor_tensor(out=out_ap, in0=out_ap, in1=g_b.ap[:rows], op=ALU.mult), reads=[out_b, g_b], writes=[out_b])
        S.op("pool", lambda e: e.tensor_tensor(out=out_ap, in0=out_ap, in1=b_b.ap[:rows], op=ALU.add), reads=[out_b, b_b], writes=[out_b])

    mats = {}
    for l in range(DEPTH):
        for g, off, w in GROUPS:
            mats[(l, g)] = WMat(S, "w%d%s" % (l, g), w_in[l][:, off:off + w])
        mats[(l, "bra")] = WMat(S, "w%dbra" % l, w_br_a[l])
        mats[(l, "brb")] = WMat(S, "w%dbrb" % l, w_br_b[l])
        mats[(l, "out")] = WMat(S, "w%dout" % l, w_out[l])
        mats[(l, "up")] = WMat(S, "w%dup" % l, w_up[l])
        mats[(l, "down")] = WMat(S, "w%ddown" % l, w_down[l])
    k.mats = mats

    A.reset()
    s32 = [A.alloc("s32_%d" % i, [2048], F32) for i in range(8)]
    s16 = [A.alloc("s16_%d" % i, [2048], BF16) for i in range(6)]
    order = []
    for l in range(DEPTH):
        order += [mats[(l, g)] for g, _, _ in GROUPS] + [mats[(l, n)] for n in ("bra", "brb", "out", "up", "down")]
    if stop_after in ("A0", "P0", "PRE"):
        order = [mats[(0, g)] for g, _, _ in GROUPS]
    n0 = len(GROUPS) + 5
    k.bg = BGCast(S, order[n0:])
    cast_weights(S, order[:n0], s32, s16)
    if stop_after == "P0":
        S.finish()
        return nc

    sts = supertiles()

    A.reset()
    xin = [A.alloc("xin%d" % i, [D], F32) for i in range(4)]
    xn16 = [A.alloc("xn16_%d" % i, [D], BF16) for i in range(2)]
    hTst = A.alloc("hTst", [KC, 512], BF16)
    S.dma(lnp[0].ap, ln_in_g.partition_broadcast(128), writes=[lnp[0]])
    S.dma(lnp[1].ap, ln_in_b.partition_broadcast(128), writes=[lnp[1]])
    allsubs = [(si, c0, off, rows) for si, (c0, subs) in enumerate(sts) for (off, rows) in subs]

    def pre_load(i):
        si_, c0_, off_, rows_ = allsubs[i]
        xb_ = xin[i % 4]
        if c0_ == 0:
            S.dma(xb_.ap[0:64], x_s, writes=[xb_])
            S.dma(xb_.ap[64:80], meta, writes=[xb_])
        else:
            f0 = c0_ - 128 + off_
            S.dma(xb_.ap[:rows_], x_p[f0:f0 + rows_], writes=[xb_])

    for i in range(3):
        pre_load(i)
    it = 0
    for si, (c0, subs) in enumerate(sts):
        T = subs[-1][0] + subs[-1][1]
        for (off, rows) in subs:
            if it + 3 < len(allsubs):
                pre_load(it + 3)
            xb = xin[it % 4]; x16 = xn16[it % 2]; it += 1
            layernorm(xb, xb.ap[:rows], rows, lnp[0], lnp[1], xb, xb.ap[:rows])
            S.dma(hbuf[c0 + off:c0 + off + rows], xb.ap[:rows], reads=[xb], writes=[t_h[si]])
            S.op("act", lambda e, xb=xb, x16=x16, rows=rows: e.activation(out=x16.ap[:rows], in_=xb.ap[:rows], func=ACTF.Copy),
                 reads=[xb], writes=[x16])
            for hf in range(2):
                transpose_chunks(x16, lambda c, x16=x16, rows=rows, hf=hf: x16.ap[:rows, (hf * 8 + c) * 128:(hf * 8 + c + 1) * 128],
                                 8, rows, hTst, hTst.ap[:, hf * 8:hf * 8 + 8, off:off + rows], "dve" if hf == 0 else "act")
        S.dma(hT[:, :, c0:c0 + T], hTst.ap[:, :, 0:T], reads=[hTst], writes=[t_hT[si]])

    if stop_after == "PRE":
        S.finish()
        return nc
    k.S = S; k.A = A; k.nc = nc
    k.__dict__.update(locals())
    for l in range(DEPTH):
        phase_A(k, l)
        if stop_after == "A0":
            break
        phase_B(k, l)
        if stop_after == "B0":
            break
        phase_C(k, l)
        if stop_after == "C0":
            break
    S.finish()
    return nc


def dense(k, xT_b, xT_fn, kc_total, mat, blocks, subs, evac, kc_piece=16, nbanks=6):
    S = k.S
    bw = mat.bw
    npiece = kc_total // kc_piece
    for blk in blocks:
        pbs = []
        for j in range(len(subs)):
            pbs.append(k.banks[k.bank_i % nbanks])
            k.bank_i += 1
        for pi in range(npiece):
            wb = k.ws.get(mat.piece(blk, pi * kc_piece, kc_piece)[0])
            for j, (off, rows) in enumerate(subs):
                pb = pbs[j]
                for kk in range(kc_piece):
                    kg = pi * kc_piece + kk
                    S.op("pe", lambda e, pb=pb, rows=rows, off=off, kg=kg, kk=kk, wb=wb: e.matmul(
                        pb.ap[:rows, :bw], lhsT=xT_fn(kg, off, rows), rhs=wb.ap[:, kk, :bw],
                        start=(kg == 0), stop=(kg == kc_total - 1)), reads=(list(xT_b) if isinstance(xT_b, (list, tuple)) else [xT_b]) + [wb], writes=[pb])
                if pi == npiece - 1:
                    evac(j, blk, pb, off, rows)


def phase_A(k, l):
    S, A = k.S, k.A
    sts = k.sts
    outs = k.outs
    A.reset()
    xT = [A.alloc("xT%d" % i, [KC, 512], BF16) for i in range(2)]
    wbufs = [A.alloc("wbuf%d" % i, [KC, 512], BF16) for i in range(4)]
    stT = {n: A.alloc("stT_" + n, [8, 512], BF16) for n in ("qa", "ka", "qb", "kb")}
    ev32 = [A.alloc("ev32_%d" % i, [512], F32) for i in range(3)]
    tm16 = [A.alloc("tm16_%d" % i, [512], BF16) for i in range(3)]
    vAt = [A.alloc("vAt%d" % i, [8, 130], BF16) for i in range(4)]
    vBt = [A.alloc("vBt%d" % i, [4, 258], BF16) for i in range(4)]
    gt = [A.alloc("gt%d" % i, [512], BF16) for i in range(3)]
    cs = [A.alloc("cs%d" % i, [2, 16], F32) for i in range(4)]
    rt = [A.alloc("rt%d" % i, [4, 4, 16], F32) for i in range(2)]
    lf = [A.alloc("lf%d" % i, [4, 8], F32) for i in range(2)]
    lft = [A.alloc("lft%d" % i, [128], F32) for i in range(2)]
    k.lft_i = 0
    k.ev_i = 0; k.tm_i = 0; k.gt_i = 0; k.rt_i = 0; k.lf_i = 0; k.ce = 0
    import os
    k.dbgva = os.environ.get("DBG_VA", "")
    for t in vAt + vBt:
        if "nomemset" in k.dbgva:
            break
        S.op("pool", lambda e, t=t: e.memset(t.ap, 1.0), writes=[t])
    S.dma(k.bfb.ap, k.b_f[l:l + 1, :].partition_broadcast(128) if False else k.b_f[l].partition_broadcast(128), writes=[k.bfb])

    ws = WStream(S, 4, wbufs)
    order = []
    import os
    k.groups = [g for g in GROUPS if g[0] in os.environ.get("DBG_GROUPS", "qa,ka,va,fa,qb,kb,vb,ga,gb").split(",")]
    for si in range(len(sts)):
        for g, off, w in k.groups:
            m = k.mats[(l, g)]
            for blk in range(m.nblk):
                order.append(m.piece(blk))
    ws.plan(order)
    k.ws = ws

    def outdma(name, c, rows, src_b, src_ap, col0, ncols):
        for (p0, p1, kind, r0) in out_rows(c, rows):
            dst = outs[name + kind][l, r0:r0 + (p1 - p0), col0:col0 + ncols]
            S.dma(dst, src_ap[p0:p1], reads=[src_b])

    def nxt(lst, attr):
        i = getattr(k, attr)
        setattr(k, attr, i + 1)
        return lst[i % len(lst)]

    def copy_eng():
        k.ce += 1
        return "act" if k.ce % 2 else "dve"

    for si, (c0, subs) in enumerate(sts):
        T = subs[-1][0] + subs[-1][1]
        xb = xT[si % 2]
        S.dma(xb.ap[:, :, 0:T], k.hT[:, :, c0:c0 + T], reads=[k.t_hT[si]], writes=[xb])
        xfn = lambda kg, off, rows, xb=xb: xb.ap[:, kg, off:off + rows]
        cst = []
        for j, (off, rows) in enumerate(subs):
            t = cs[j]
            S.dma(t.ap[:rows, 0, :], k.c_cos[c0 + off:c0 + off + rows], writes=[t])
            S.dma(t.ap[:rows, 1, :], k.c_sin[c0 + off:c0 + off + rows], writes=[t])
            cst.append(t)

        def ev_q(name):
            def f(j, blk, pb, off, rows):
                tm = nxt(tm16, "tm_i")
                S.op("act", lambda e: e.activation(out=tm.ap[:rows], in_=pb.ap[:rows], func=ACTF.Copy), reads=[pb], writes=[tm])
                k.transpose_chunks(tm, lambda c: tm.ap[:rows, c * 128:(c + 1) * 128], 4, rows, stT[name],
                                   stT[name].ap[:, 4 * blk:4 * blk + 4, off:off + rows], "dve")
            return f

        def ev_ka(j, blk, pb, off, rows):
            ev = nxt(ev32, "ev_i"); tm = nxt(tm16, "tm_i")
            S.op("act", lambda e: e.activation(out=ev.ap[:rows], in_=pb.ap[:rows], func=ACTF.Copy), reads=[pb], writes=[ev])
            outdma("fk", c0 + off, rows, ev, ev.ap, 512 * blk, 512)
            S.op("dve", lambda e: e.tensor_copy(out=tm.ap[:rows], in_=pb.ap[:rows]), reads=[pb], writes=[tm])
            k.transpose_chunks(tm, lambda c: tm.ap[:rows, c * 128:(c + 1) * 128], 4, rows, stT["ka"],
                               stT["ka"].ap[:, 4 * blk:4 * blk + 4, off:off + rows], "act")

        def ev_va(j, blk, pb, off, rows):
            ev = nxt(ev32, "ev_i")
            S.op("dve", lambda e: e.tensor_copy(out=ev.ap[:rows], in_=pb.ap[:rows]), reads=[pb], writes=[ev])
            outdma("fv", c0 + off, rows, ev, ev.ap, 512 * blk, 512)
            vt = vAt[j]
            S.op("act", lambda e: e.activation(out=vt.ap[:rows, 4 * blk:4 * blk + 4, 0:128],
                                               in_=pb.ap[:rows].rearrange("p (h d) -> p h d", h=4), func=ACTF.Copy), reads=[pb], writes=[vt])
            if blk == 1 and "nodma" not in k.dbgva:
                S.dma(k.vA[c0 + off:c0 + off + rows], vt.ap[:rows].rearrange("p h d -> p (h d)"), reads=[vt], writes=[k.t_scr["vA"][si]])

        def ev_fa(j, blk, pb, off, rows):
            t = nxt(lf, "lf_i")
            a = t.ap
            S.op("dve", lambda e: e.tensor_tensor(out=a[:rows, 0], in0=pb.ap[:rows, 0:8], in1=k.bfb.ap[:rows], op=ALU.add),
                 reads=[pb, k.bfb], writes=[t])
            S.op("act", lambda e: e.activation(out=a[:rows, 1], in_=a[:rows, 0], func=ACTF.Abs), reads=[t], writes=[t])
            S.op("act", lambda e: e.activation(out=a[:rows, 1], in_=a[:rows, 1], func=ACTF.Exp, scale=-1.0), reads=[t], writes=[t])
            S.op("act", lambda e: e.activation(out=a[:rows, 1], in_=a[:rows, 1], func=ACTF.Ln, bias=1.0), reads=[t], writes=[t])
            S.op("dve", lambda e: e.tensor_scalar(out=a[:rows, 2], in0=a[:rows, 0], scalar1=0.0, scalar2=None, op0=ALU.min), reads=[t], writes=[t])
            S.op("dve", lambda e: e.tensor_tensor(out=a[:rows, 3], in0=a[:rows, 2], in1=a[:rows, 1], op=ALU.subtract), reads=[t], writes=[t])
            outdma("fl", c0 + off, rows, t, a[:, 3], 0, 8)
            pbk, _ = k.trbank()
            S.op("pe", lambda e: e.matmul(pbk.ap[0:8, 0:rows], lhsT=a[:rows, 3], rhs=k.identf.ap[:rows, :rows], start=True, stop=True),
                 reads=[t, k.identf], writes=[pbk])
            lt = nxt(lft, "lft_i")
            S.op("dve", lambda e: e.tensor_copy(out=lt.ap[0:8, 0:rows], in_=pbk.ap[0:8, 0:rows]), reads=[pbk], writes=[lt])
            S.dma(k.lfT_d[:, c0 + off:c0 + off + rows], lt.ap[0:8, 0:rows], reads=[lt], writes=[k.t_lfT])
            S.dma(k.lfbuf[c0 + off:c0 + off + rows], a[:rows, 3], reads=[t], writes=[k.t_scr["lf"][si]])

        def rope(ev, rows, ct):
            r = nxt(rt, "rt_i")
            x = ev.ap[:rows].rearrange("p (c d) -> p c d", c=4)
            x1 = x[:, :, 0:16]; x2 = x[:, :, 16:32]
            cosb = ct.ap[:rows, 0:1, :].to_broadcast([rows, 4, 16])
            sinb = ct.ap[:rows, 1:2, :].to_broadcast([rows, 4, 16])
            ra = r.ap
            for (dst, a_, b_) in ((0, x1, cosb), (1, x2, sinb), (2, x2, cosb), (3, x1, sinb)):
                S.op("pool", lambda e, dst=dst, a_=a_, b_=b_: e.tensor_tensor(out=ra[:rows, dst], in0=a_, in1=b_, op=ALU.mult),
                     reads=[ev, ct], writes=[r])
            S.op("pool", lambda e: e.tensor_tensor(out=x1, in0=ra[:rows, 0], in1=ra[:rows, 1], op=ALU.subtract), reads=[r], writes=[ev])
            S.op("pool", lambda e: e.tensor_tensor(out=x2, in0=ra[:rows, 2], in1=ra[:rows, 3], op=ALU.add), reads=[r], writes=[ev])

        def ev_rope(name):
            def f(j, blk, pb, off, rows):
                ev = nxt(ev32, "ev_i"); tm = nxt(tm16, "tm_i")
                S.op("act", lambda e: e.activation(out=ev.ap[:rows], in_=pb.ap[:rows], func=ACTF.Copy), reads=[pb], writes=[ev])
                rope(ev, rows, cst[j])
                if name == "kb":
                    outdma("dk", c0 + off, rows, ev, ev.ap, 512 * blk, 512)
                S.op("dve", lambda e: e.tensor_copy(out=tm.ap[:rows], in_=ev.ap[:rows]), reads=[ev], writes=[tm])
                k.transpose_chunks(tm, lambda c: tm.ap[:rows, c * 128:(c + 1) * 128], 4, rows, stT[name],
                                   stT[name].ap[:, 4 * blk:4 * blk + 4, off:off + rows], copy_eng())
            return f

        def ev_vb(j, blk, pb, off, rows):
            ev = nxt(ev32, "ev_i")
            S.op("dve", lambda e: e.tensor_copy(out=ev.ap[:rows], in_=pb.ap[:rows]), reads=[pb], writes=[ev])
            outdma("dv", c0 + off, rows, ev, ev.ap, 512 * blk, 512)
            vt = vBt[j]
            S.op("act", lambda e: e.activation(out=vt.ap[:rows, 2 * blk:2 * blk + 2, 0:256],
                                               in_=pb.ap[:rows].rearrange("p (h d) -> p h d", h=2), func=ACTF.Copy), reads=[pb], writes=[vt])
            if blk == 1:
                S.dma(k.vB[c0 + off:c0 + off + rows], vt.ap[:rows].rearrange("p h d -> p (h d)"), reads=[vt], writes=[k.t_scr["vB"][si]])

        def ev_gate(goff):
            def f(j, blk, pb, off, rows):
                g = nxt(gt, "gt_i")
                S.op("act", lambda e: e.activation(out=g.ap[:rows], in_=pb.ap[:rows], func=ACTF.Sigmoid), reads=[pb], writes=[g])
                S.dma(k.gsc[c0 + off:c0 + off + rows, goff + 512 * blk:goff + 512 * blk + 512], g.ap[:rows], reads=[g],
                      writes=[k.t_scr["gs"][si]])
            return f

        evs = {"qa": ev_q("qa"), "ka": ev_ka, "va": ev_va, "fa": ev_fa, "qb": ev_rope("qb"), "kb": ev_rope("kb"),
               "vb": ev_vb, "ga": ev_gate(0), "gb": ev_gate(2048)}
        for g, goff, w in k.groups:
            m = k.mats[(l, g)]
            dense(k, xb, xfn, KC, m, range(m.nblk), subs, evs[g])
            if g in k.scrT:
                S.dma(k.scrT[g][:, :, c0:c0 + T], stT[g].ap[:, :, 0:T], reads=[stT[g]], writes=[k.t_scr[g][si]])


def _consts():
    ident = np.eye(128, dtype=np.float32)
    kk = np.arange(128)[:, None]
    qq = np.arange(128)[None, :]
    cmask = np.where(kk <= qq, 0.0, NEG).astype(np.float32)
    kmask = np.where((kk // 64) <= (qq // 64), 0.0, NEG).astype(np.float32)
    pos = np.zeros(C, dtype=np.float32)
    pos[0:64] = PAST + np.arange(64)
    pos[64:80] = np.arange(16)
    pos[128:] = NM + np.arange(NF)
    half = 16
    inv_freq = (np.float32(500000.0) ** (-np.arange(half, dtype=np.float32) / np.float32(half))).astype(np.float32)
    ang = (pos[:, None] * inv_freq[None, :]).astype(np.float32)
    return {"c_ident": ident, "c_cmask": cmask, "c_kmask": kmask,
            "c_cos": np.cos(ang).astype(np.float32), "c_sin": np.sin(ang).astype(np.float32)}


def make_in_maps(inp):
    cs = _consts()
    f = lambda a: np.ascontiguousarray(np.asarray(a, dtype=np.float32))
    shared = {
        "meta": f(inp["meta_tokens"]), "ln_in_g": f(inp["ln_in_g"]), "ln_in_b": f(inp["ln_in_b"]),
        "w_in": f(inp["w_in"]), "b_f": f(inp["b_f"]),
        "lq1": f(inp["lambda_q1"]), "lk1": f(inp["lambda_k1"]), "lq2": f(inp["lambda_q2"]), "lk2": f(inp["lambda_k2"]),
        "subg": f(inp["subln_g"]), "w_br_a": f(inp["w_br_a"]), "w_br_b": f(inp["w_br_b"]), "w_out": f(inp["w_out"]),
        "ln1_g": f(inp["ln1_g"]), "ln1_b": f(inp["ln1_b"]), "w_up": f(inp["w_up"]), "w_down": f(inp["w_down"]),
        "ln2_g": f(inp["ln2_g"]), "ln2_b": f(inp["ln2_b"]),
    }
    shared.update(cs)
    maps = []
    for c in range(8):
        m = dict(shared)
        m["x_p"] = f(inp["x_prompt"][c]); m["x_s"] = f(inp["x_sample"][c])
        m["cfk"] = f(np.asarray(inp["cache_fox_k"])[:, c].reshape(DEPTH, PAST, 1024))
        m["cfv"] = f(np.asarray(inp["cache_fox_v"])[:, c].reshape(DEPTH, PAST, 1024))
        m["cfl"] = f(np.asarray(inp["cache_fox_logf"])[:, c])
        m["cdk"] = f(np.asarray(inp["cache_diff_k"])[:, c].reshape(DEPTH, PAST, 1024))
        m["cdv"] = f(np.asarray(inp["cache_diff_v"])[:, c].reshape(DEPTH, PAST, 1024))
        maps.append(m)
    return maps


def gather(results):
    st = lambda n: np.stack([np.asarray(r[n]) for r in results], axis=0)
    y_p = st("y_p"); y_s = st("y_s")

    def kv(n, shp):
        a = st(n)
        a = np.moveaxis(a, 0, 1)
        return np.ascontiguousarray(a.reshape(a.shape[:3] + shp))
    return (y_p, y_s,
            kv("fk_p", (8, 128)), kv("fv_p", (8, 128)), kv("fl_p", (8,)), kv("dk_p", (4, 256)), kv("dv_p", (4, 256)),
            kv("fk_s", (8, 128)), kv("fv_s", (8, 128)), kv("fl_s", (8,)), kv("dk_s", (4, 256)), kv("dv_s", (4, 256)))


_NC_CACHE = {}


def kernel(**inputs):
    if "nc" not in _NC_CACHE:
        _NC_CACHE["nc"] = build()
    nc = _NC_CACHE["nc"]
    maps = make_in_maps(inputs)
    res = run_bass_kernel_spmd(nc, maps, core_ids=list(range(8)))
    return gather(res.results)


def phase_C(k, l):
    S, A = k.S, k.A
    sts = k.sts
    A.reset()
    wbufs = [A.alloc("wbuf%d" % i, [KC, 512], BF16) for i in range(2)]
    XT = A.alloc("XT", [KC, 512], BF16)
    xres = [A.alloc("xres%d" % i, [D], F32) for i in range(4)]
    m16 = [A.alloc("m16_%d" % i, [D], BF16) for i in range(4)]
    brt = [A.alloc("brt%d" % i, [512], F32) for i in range(2)]
    reg0 = A.off
    oTa = A.alloc("oTa", [8, 512], BF16)
    oTb = A.alloc("oTb", [8, 512], BF16)
    gpa = [A.alloc("gpa%d" % i, [512], BF16) for i in range(4)]
    gpb = [A.alloc("gpb%d" % i, [512], BF16) for i in range(4)]
    brt += [A.alloc("brt%d" % (i + 2), [512], F32) for i in range(6)]
    OV = [oTa, oTb] + gpa + gpb + brt[2:8]
    hidT = A.alloc_at("hidT", reg0, [64, 512], BF16)
    k.gp_i = 0; k.brt_i = 0
    lnp = k.lnp
    S.dma(lnp[0].ap, k.ln1_g[l].partition_broadcast(128), writes=[lnp[0]])
    S.dma(lnp[1].ap, k.ln1_b[l].partition_broadcast(128), writes=[lnp[1]])
    S.dma(lnp[2].ap, k.ln2_g[l].partition_broadcast(128), writes=[lnp[2]])
    S.dma(lnp[3].ap, k.ln2_b[l].partition_broadcast(128), writes=[lnp[3]])
    M = k.mats
    ws = WStream(S, 2, wbufs)
    order = []
    for si in range(len(sts)):
        for blk in range(4):
            order.append(M[(l, "bra")].piece(blk, 0, 8)); order.append(M[(l, "brb")].piece(blk, 0, 8))
        for blk in range(4):
            order.append(M[(l, "out")].piece(blk))
        for blk in range(16):
            order.append(M[(l, "up")].piece(blk))
        for blk in range(4):
            for pi in range(4):
                order.append(M[(l, "down")].piece(blk, 16 * pi, 16))
    ws.plan(order)
    k.ws = ws

    for si, (c0, subs) in enumerate(sts):
        T = subs[-1][0] + subs[-1][1]
        S.dma(oTa.ap[:, :, 0:T], k.oaT[:, :, c0:c0 + T], reads=[k.t_oa[si]], writes=[oTa])
        S.dma(oTb.ap[:, :, 0:T], k.obT[:, :, c0:c0 + T], reads=[k.t_ob[si]], writes=[oTb])
        for j, (off, rows) in enumerate(subs):
            S.dma(xres[j].ap[:rows], k.hbuf[c0 + off:c0 + off + rows], reads=[k.t_h[si]], writes=[xres[j]])
        gps = {}

        def ev_bra(j, blk, pb, off, rows):
            g = gpa[j]
            S.dma(g.ap[:rows], k.gsc[c0 + off:c0 + off + rows, 512 * blk:512 * blk + 512], reads=[k.t_scr["gs"][si]], writes=[g])
            t = brt[k.brt_i % 8]; k.brt_i += 1
            gps[(j, blk, "t")] = t
            S.op("dve", lambda e: e.tensor_tensor(out=t.ap[:rows], in0=pb.ap[:rows], in1=g.ap[:rows], op=ALU.mult), reads=[pb, g], writes=[t])

        def ev_brb(j, blk, pb, off, rows):
            g = gpb[j]; t = gps[(j, blk, "t")]
            S.dma(g.ap[:rows], k.gsc[c0 + off:c0 + off + rows, 2048 + 512 * blk:2048 + 512 * blk + 512], reads=[k.t_scr["gs"][si]], writes=[g])
            t2 = brt[k.brt_i % 8]; k.brt_i += 1
            S.op("dve", lambda e: e.tensor_tensor(out=t2.ap[:rows], in0=pb.ap[:rows], in1=g.ap[:rows], op=ALU.mult), reads=[pb, g], writes=[t2])
            S.op("pool", lambda e: e.tensor_tensor(out=m16[j].ap[:rows, 512 * blk:512 * blk + 512], in0=t2.ap[:rows],
                                                   in1=t.ap[:rows], op=ALU.add), reads=[t2, t], writes=[m16[j]])

        for blk in range(4):
            dense(k, oTa, lambda kg, off, rows: oTa.ap[:, kg, off:off + rows], 8, M[(l, "bra")], [blk], subs, ev_bra, kc_piece=8)
            dense(k, oTb, lambda kg, off, rows: oTb.ap[:, kg, off:off + rows], 8, M[(l, "brb")], [blk], subs, ev_brb, kc_piece=8)
        for j, (off, rows) in enumerate(subs):
            for hf in range(2):
                k.transpose_chunks(m16[j], lambda c, j=j, rows=rows, hf=hf: m16[j].ap[:rows, (hf * 8 + c) * 128:(hf * 8 + c + 1) * 128],
                                   8, rows, XT, XT.ap[:, hf * 8:hf * 8 + 8, off:off + rows], "dve" if hf == 0 else "act")

        def ev_res(j, blk, pb, off, rows):
            xs = xres[j].ap[:rows, 512 * blk:512 * blk + 512]
            S.op("dve", lambda e: e.scalar_tensor_tensor(out=xs, in0=xs, scalar=float(ALPHA), in1=pb.ap[:rows], op0=ALU.mult, op1=ALU.add),
                 reads=[xres[j], pb], writes=[xres[j]])

        dense(k, XT, lambda kg, off, rows: XT.ap[:, kg, off:off + rows], KC, M[(l, "out")], range(4), subs, ev_res)
        for j, (off, rows) in enumerate(subs):
            k.layernorm(xres[j], xres[j].ap[:rows], rows, lnp[0], lnp[1], xres[j], xres[j].ap[:rows])
            S.op("act", lambda e, j=j, rows=rows: e.activation(out=m16[j].ap[:rows], in_=xres[j].ap[:rows], func=ACTF.Copy), reads=[xres[j]], writes=[m16[j]])
            for hf in range(2):
                k.transpose_chunks(m16[j], lambda c, j=j, rows=rows, hf=hf: m16[j].ap[:rows, (hf * 8 + c) * 128:(hf * 8 + c + 1) * 128],
                                   8, rows, XT, XT.ap[:, hf * 8:hf * 8 + 8, off:off + rows], "dve" if hf == 0 else "act")
        mu = M[(l, "up")]
        halves = [subs]
        for hs in halves:
            h0 = hs[0][0]
            Th = hs[-1][0] + hs[-1][1] - h0
            for blk in range(16):
                wb = ws.get(mu.piece(blk)[0])
                for hc in range(4):
                    pb = k.banks[k.bank_i % 6]; k.bank_i += 1
                    for kg in range(KC):
                        S.op("pe", lambda e, pb=pb, wb=wb, hc=hc, kg=kg, h0=h0, Th=Th: e.matmul(
                            pb.ap[:, :Th], lhsT=wb.ap[:, kg, hc * 128:(hc + 1) * 128], rhs=XT.ap[:, kg, h0:h0 + Th],
                            start=(kg == 0), stop=(kg == KC - 1)), reads=[XT, wb], writes=[pb])
                    t = brt[k.brt_i % 2]; k.brt_i += 1
                    S.op("act", lambda e, pb=pb, t=t, Th=Th: e.activation(out=t.ap[:, :Th], in_=pb.ap[:, :Th], func=ACTF.Relu), reads=[pb], writes=[t])
                    S.op("pool", lambda e, t=t, blk=blk, hc=hc, Th=Th: e.tensor_tensor(out=hidT.ap[:, blk * 4 + hc, 0:Th], in0=t.ap[:, :Th], in1=t.ap[:, :Th], op=ALU.mult),
                         reads=[t], writes=[hidT] + OV)
            j0 = subs.index(hs[0])

            def ev_res2(j, blk, pb, off, rows, j0=j0):
                ev_res(j + j0, blk, pb, off, rows)
            dense(k, [hidT] + OV, lambda kg, off, rows, h0=h0: hidT.ap[:, kg, off - h0:off - h0 + rows], 64, M[(l, "down")], range(4), hs, ev_res2, kc_piece=16, nbanks=6)
        for j, (off, rows) in enumerate(subs):
            k.layernorm(xres[j], xres[j].ap[:rows], rows, lnp[2], lnp[3], xres[j], xres[j].ap[:rows])
            c = c0 + off
            if l == DEPTH - 1:
                if c < 128:
                    S.dma(k.y_s, xres[j].ap[0:64], reads=[xres[j]])
                else:
                    S.dma(k.y_p[c - 128:c - 128 + rows], xres[j].ap[:rows], reads=[xres[j]])
            else:
                S.dma(k.hbuf[c:c + rows], xres[j].ap[:rows], reads=[xres[j]], writes=[k.t_h[si]])
                S.op("act", lambda e, j=j, rows=rows: e.activation(out=m16[j].ap[:rows], in_=xres[j].ap[:rows], func=ACTF.Copy), reads=[xres[j]], writes=[m16[j]])
                for hf in range(2):
                    k.transpose_chunks(m16[j], lambda cc, j=j, rows=rows, hf=hf: m16[j].ap[:rows, (hf * 8 + cc) * 128:(hf * 8 + cc + 1) * 128],
                                       8, rows, XT, XT.ap[:, hf * 8:hf * 8 + 8, off:off + rows], "dve" if hf == 0 else "act")
        if l < DEPTH - 1:
            S.dma(k.hT[:, :, c0:c0 + T], XT.ap[:, :, 0:T], reads=[XT], writes=[k.t_hT[si]])


def ktile_table():
    kts = [("meta", 64, 16), ("snew", 0, 64)]
    for t in range(32):
        kts.append(("f%d" % t, 128 + 128 * t, 128))
    for t in range(32):
        kts.append(("c%d" % t, C + 128 * t, 128))
    return kts


KT_META, KT_SNEW, KT_F0, KT_C0 = 0, 1, 2, 34
import os as _os
ATT_LA = int(_os.environ.get('ATT_LA', '3'))


def phase_B(k, l):
    S, A = k.S, k.A
    lam_init = 0.8 - 0.6 * float(np.exp(-0.3 * l))
    kts = ktile_table()
    sts = k.sts
    all_scr = lambda n: list(k.t_scr[n])

    def nxt(lst, attr):
        i = getattr(k, attr, 0)
        setattr(k, attr, i + 1)
        return lst[i % len(lst)]

    A.reset()
    ck32 = [A.alloc("ck32_%d" % i, [1024], F32) for i in range(2)]
    ck16 = [A.alloc("ck16_%d" % i, [1024], BF16) for i in range(2)]
    cstg = [A.alloc("cstg%d" % i, [8, 512], BF16) for i in range(2)]
    cvA = [A.alloc("cvA%d" % i, [8, 130], BF16) for i in range(2)]
    cvB = [A.alloc("cvB%d" % i, [4, 258], BF16) for i in range(2)]
    lT = A.alloc("lT", [C], F32)
    cT = A.alloc("cT", [C], F32)
    caT = A.alloc("caT", [PAST], F32)
    ccT = A.alloc("ccT", [PAST], F32)
    zer = A.alloc("zer", [512], F32)
    cl = A.alloc("cl", [32, 8], F32)
    lam4 = [A.alloc("lam4_%d" % i, [128], F32) for i in range(4)]
    lamt = A.alloc("lamt", [2, 128], F32)
    lamd = A.alloc("lamd", [2], F32)
    for t in cvA + cvB:
        S.op("pool", lambda e, t=t: e.memset(t.ap, 1.0), writes=[t])
    S.op("pool", lambda e: e.memset(zer.ap, 0.0), writes=[zer])
    for i, src in enumerate((k.lq1, k.lk1, k.lq2, k.lk2)):
        S.dma(lam4[i].ap, src[l].partition_broadcast(128), writes=[lam4[i]])
    for j in range(2):
        S.op("dve", lambda e, j=j: e.tensor_tensor(out=lamt.ap[:, j, :], in0=lam4[2 * j].ap, in1=lam4[2 * j + 1].ap, op=ALU.mult),
             reads=[lam4[2 * j], lam4[2 * j + 1]], writes=[lamt])
    S.op("dve", lambda e: e.tensor_reduce(out=lamd.ap, in_=lamt.ap, axis=AX.X, op=ALU.add), reads=[lamt], writes=[lamd])
    S.op("act", lambda e: e.activation(out=lamd.ap, in_=lamd.ap, func=ACTF.Exp), reads=[lamd], writes=[lamd])
    S.op("dve", lambda e: e.tensor_tensor(out=k.nlam.ap, in0=lamd.ap[:, 1:2], in1=lamd.ap[:, 0:1], op=ALU.subtract), reads=[lamd], writes=[k.nlam])
    S.op("dve", lambda e: e.tensor_scalar(out=k.nlam.ap, in0=k.nlam.ap, scalar1=-lam_init, scalar2=None, op0=ALU.add), reads=[k.nlam], writes=[k.nlam])
    S.dma(k.gsb.ap, k.subg[l].partition_broadcast(128), writes=[k.gsb])
    S.op("dve", lambda e: e.tensor_scalar(out=k.gsb.ap, in0=k.gsb.ap, scalar1=1.0 - lam_init, scalar2=None, op0=ALU.mult), reads=[k.gsb], writes=[k.gsb])

    it = 0
    for (src, name) in ((k.cfk, "ka"), (k.cdk, "kb")):
        for g4 in range(8):
            stg = cstg[g4 % 2]
            for tt in range(4):
                t = g4 * 4 + tt
                c32 = ck32[it % 2]; c16 = ck16[it % 2]; it += 1
                S.dma(c32.ap, src[l, t * 128:(t + 1) * 128, :], writes=[c32])
                S.op("pool", lambda e, c32=c32, c16=c16: e.tensor_copy(out=c16.ap, in_=c32.ap), reads=[c32], writes=[c16])
                k.transpose_chunks(c16, lambda c, c16=c16: c16.ap[:, c * 128:(c + 1) * 128], 8, 128, stg,
                                   stg.ap[:, :, tt * 128:(tt + 1) * 128], "act" if tt % 2 else "dve")
            S.dma(k.scrT[name][:, :, C + g4 * 512:C + (g4 + 1) * 512], stg.ap, reads=[stg], writes=[k.t_cache[name]])
    for t in range(32):
        c32 = ck32[it % 2]; it += 1
        vt = cvA[t % 2]
        S.dma(c32.ap, k.cfv[l, t * 128:(t + 1) * 128, :], writes=[c32])
        S.op("pool", lambda e, c32=c32, vt=vt: e.tensor_copy(out=vt.ap[:, :, 0:128], in_=c32.ap.rearrange("p (h d) -> p h d", h=8)), reads=[c32], writes=[vt])
        S.dma(k.vA[C + t * 128:C + (t + 1) * 128], vt.ap.rearrange("p h d -> p (h d)"), reads=[vt], writes=[k.t_cache["vA"]])
        c32 = ck32[it % 2]; it += 1
        vt = cvB[t % 2]
        S.dma(c32.ap, k.cdv[l, t * 128:(t + 1) * 128, :], writes=[c32])
        S.op("pool", lambda e, c32=c32, vt=vt: e.tensor_copy(out=vt.ap[:, :, 0:256], in_=c32.ap.rearrange("p (h d) -> p h d", h=4)), reads=[c32], writes=[vt])
        S.dma(k.vB[C + t * 128:C + (t + 1) * 128], vt.ap.rearrange("p h d -> p (h d)"), reads=[vt], writes=[k.t_cache["vB"]])

    S.dma(lT.ap[0:8], k.lfT_d, reads=[k.t_lfT], writes=[lT])
    S.dma(cl.ap, k.cfl[l].rearrange("(t p) h -> p t h", p=128), writes=[cl])
    for g in range(8):
        pbk, _ = k.trbank()
        for tt in range(4):
            t = g * 4 + tt
            S.op("pe", lambda e, pbk=pbk, tt=tt, t=t: e.matmul(pbk.ap[0:8, tt * 128:(tt + 1) * 128], lhsT=cl.ap[:, t, :], rhs=k.identf.ap,
                                                               start=True, stop=True), reads=[cl, k.identf], writes=[pbk])
        S.op("dve", lambda e, pbk=pbk, g=g: e.tensor_copy(out=caT.ap[0:8, g * 512:(g + 1) * 512], in_=pbk.ap[0:8, :]), reads=[pbk], writes=[caT])

    def scan(dst_b, dst_ap, src_b, src_ap, n, init, init_b=None):
        last = init
        for o in range(0, n, 512):
            w = min(512, n - o)
            ini = 0.0 if last is None else last
            S.op("dve", lambda e, o=o, w=w, ini=ini: e.tensor_tensor_scan(out=dst_ap[0:8, o:o + w], data0=src_ap[0:8, o:o + w], data1=zer.ap[0:8, 0:w],
                                                                         initial=ini, op0=ALU.add, op1=ALU.add),
                 reads=[src_b, zer, dst_b] + ([init_b] if (init_b is not None and o == 0) else []), writes=[dst_b])
            last = dst_ap[0:8, o + w - 1:o + w]
        return last

    last_m = scan(cT, cT.ap[:, 64:80], lT, lT.ap[:, 64:80], 16, None)
    scan(cT, cT.ap[:, 128:C], lT, lT.ap[:, 128:C], NF, last_m)
    last_c = scan(ccT, ccT.ap, caT, caT.ap, PAST, None)
    scan(cT, cT.ap[:, 0:64], lT, lT.ap[:, 0:64], TS, last_c, ccT)
    S.dma(k.cbuf, cT.ap[0:8], reads=[cT], writes=[k.t_cbuf])
    for half, (lo, hi) in enumerate(((0, 64), (64, 66))):
        pbk, _ = k.trbank()
        for kt in range(lo, hi):
            name, col, K_ = kts[kt]
            srcb, srcap = (ccT, ccT.ap[0:8, col - C:col - C + K_]) if kt >= KT_C0 else (cT, cT.ap[0:8, col:col + K_])
            S.op("pe", lambda e, pbk=pbk, kt=kt, lo=lo, K_=K_, srcap=srcap: e.matmul(pbk.ap[0:K_, (kt - lo) * 8:(kt - lo + 1) * 8], lhsT=srcap,
                                                                                   rhs=k.identf.ap[0:8, 0:8], start=True, stop=True),
                 reads=[srcb, k.identf], writes=[pbk])
        S.op("act", lambda e, pbk=pbk, lo=lo, hi=hi: e.activation(out=k.negc.ap[:, lo:hi, :], in_=pbk.ap[:, 0:(hi - lo) * 8].rearrange("p (t h) -> p t h", h=8),
                                                                 func=ACTF.Copy, scale=-1.0), reads=[pbk], writes=[k.negc])

    def groups(nq):
        gs = [(64, 16, [(KT_META, 0, True)])]
        per = nq // 128
        for J in range(NF // nq):
            lst = [(KT_META, 0, False)] + [(KT_F0 + t, 0, False) for t in range(per * J)]
            lst += [(KT_F0 + per * J + i, 128 * i, True) for i in range(per)]
            gs.append((128 + nq * J, nq, lst))
        gs.append((0, 64, [(KT_C0 + t, 0, False) for t in range(32)] + [(KT_SNEW, 0, True)]))
        return gs

    def load_V(Vh, src, h, w):
        S.dma(Vh.ap[0:16, KT_META, :], src[64:80, h * w:(h + 1) * w], reads=all_scr("vA" if w == 130 else "vB"), writes=[Vh])
        S.dma(Vh.ap[0:64, KT_SNEW, :], src[0:64, h * w:(h + 1) * w], reads=[], writes=[Vh])
        S.dma(Vh.ap[:, KT_F0:KT_F0 + 32, :], src[128:C, h * w:(h + 1) * w].rearrange("(t p) d -> p t d", p=128), reads=[], writes=[Vh])
        S.dma(Vh.ap[:, KT_C0:KT_C0 + 32, :], src[C:C + PAST, h * w:(h + 1) * w].rearrange("(t p) d -> p t d", p=128),
              reads=[k.t_cache["vA" if w == 130 else "vB"]], writes=[Vh])

    A.reset()
    banks = k.banks
    Kh = A.alloc("Kh", [C + PAST], BF16)
    Qh = A.alloc("Qh", [C], BF16)
    Vh = A.alloc("Vh", [66, 130], BF16)
    crow = [A.alloc("crow%d" % i, [512], F32) for i in range(2)]
    tt32 = [A.alloc("tt32_%d" % i, [512], F32) for i in range(5)]
    PT = [A.alloc("PT%d" % i, [512], BF16) for i in range(6)]
    SB = [banks[0], banks[1], banks[6], banks[7]]
    o16 = [A.alloc("o16_%d" % i, [256], BF16) for i in range(2)]
    oTs = [A.alloc("oTs%d" % i, [2, 512], BF16) for i in range(2)]
    rz = [A.alloc("rz%d" % i, [4], F32) for i in range(4)]
    osq = [A.alloc("osq%d" % i, [256], F32) for i in range(2)]
    o32 = [A.alloc("o32_%d" % i, [256], F32) for i in range(2)]
    k.sb_i = 0
    bg32 = [A.alloc("bg32_%d" % i, [2048], F32) for i in range(4)]
    bg16 = [A.alloc("bg16_%d" % i, [2048], BF16) for i in range(4)]
    fgroups = groups(512)
    for h in range(8):
        S.dma(Kh.ap, k.scrT["ka"][:, h, :], reads=all_scr("ka") + [k.t_cache["ka"]], writes=[Kh])
        S.dma(Qh.ap, k.scrT["qa"][:, h, 0:C], reads=all_scr("qa"), writes=[Qh])
        load_V(Vh, k.vA, h, 130)
        for gi, (qc, Nq, lst) in enumerate(fgroups):
            k.bg.finish(bg16)
            k.bg.issue(4, bg32)
            cr = nxt(crow, "cr_i")
            S.dma(cr.ap[:, 0:Nq], k.cbuf[h, qc:qc + Nq].partition_broadcast(128), reads=[k.t_cbuf], writes=[cr])
            nsub = (Nq + 127) // 128
            first = {}; last = {}
            for idx, (kt, qoff, msk) in enumerate(lst):
                for s_ in range(nsub):
                    if qoff <= s_ * 128:
                        first.setdefault(s_, idx); last[s_] = idx
            pend = []
            LA = ATT_LA
            for idx, (kt, qoff, msk) in enumerate(lst):
                name, col, K_ = kts[kt]
                n = Nq - qoff
                ps = SB[k.sb_i % 4]; k.sb_i += 1
                S.op("pe", lambda e, ps=ps, K_=K_, n=n, col=col, qc=qc, qoff=qoff, Nq=Nq: e.matmul(
                    ps.ap[:K_, :n], lhsT=Kh.ap[:, col:col + K_], rhs=Qh.ap[:, qc + qoff:qc + Nq], start=True, stop=True), reads=[Kh, Qh], writes=[ps])
                t32 = nxt(tt32, "t32_i"); pt = nxt(PT, "pt_i")
                S.op("dve", lambda e, ps=ps, K_=K_, n=n, t32=t32, cr=cr, qoff=qoff, Nq=Nq: e.scalar_tensor_tensor(
                    out=t32.ap[:K_, :n], in0=ps.ap[:K_, :n], scalar=float(FOX_SCALE), in1=cr.ap[:K_, qoff:Nq], op0=ALU.mult, op1=ALU.add),
                    reads=[ps, cr], writes=[t32])
                if msk:
                    mw = min(128, n)
                    S.op("pool", lambda e, t32=t32, K_=K_, mw=mw: e.tensor_tensor(out=t32.ap[:K_, :mw], in0=t32.ap[:K_, :mw], in1=k.cmask.ap[:K_, :mw], op=ALU.add),
                         reads=[t32, k.cmask], writes=[t32])
                S.op("act", lambda e, t32=t32, pt=pt, K_=K_, n=n, kt=kt, h=h: e.activation(out=pt.ap[:K_, :n], in_=t32.ap[:K_, :n], func=ACTF.Exp,
                                                                                          bias=k.negc.ap[:K_, kt, h:h + 1]), reads=[t32, k.negc], writes=[pt])

                def pv(idx=idx, kt=kt, qoff=qoff, K_=K_, pt=pt):
                    for s_ in range(nsub):
                        if qoff > s_ * 128:
                            continue
                        qs = s_ * 128 - qoff
                        qn = min(128, Nq - s_ * 128)
                        ob = banks[2 + s_]
                        S.op("pe", lambda e, ob=ob, pt=pt, K_=K_, qs=qs, qn=qn, kt=kt, st=(first[s_] == idx), sp=(last[s_] == idx): e.matmul(
                            ob.ap[:qn, 0:129], lhsT=pt.ap[:K_, qs:qs + qn], rhs=Vh.ap[:K_, kt, 0:129], start=st, stop=sp), reads=[pt, Vh], writes=[ob])
                pend.append(pv)
                if len(pend) > LA:
                    pend.pop(0)()
            while pend:
                pend.pop(0)()
            ot = nxt(oTs, "ots_i")
            for s_ in range(nsub):
                qn = min(128, Nq - s_ * 128)
                ob = banks[2 + s_]
                r = nxt(rz, "rz_i"); o6 = nxt(o16, "o16_i")
                S.op("dve", lambda e, ob=ob, r=r, qn=qn: e.reciprocal(out=r.ap[:qn, 0:1], in_=ob.ap[:qn, 128:129]), reads=[ob], writes=[r])
                S.op("dve", lambda e, ob=ob, r=r, o6=o6, qn=qn: e.tensor_scalar(out=o6.ap[:qn, 0:128], in0=ob.ap[:qn, 0:128], scalar1=r.ap[:qn, 0:1], scalar2=None,
                                                                              op0=ALU.mult), reads=[ob, r], writes=[o6])
                k.transpose_chunks(o6, lambda c, o6=o6, qn=qn: o6.ap[:qn, 0:128], 1, qn, ot, ot.ap[:, 0:1, s_ * 128:s_ * 128 + qn], "act")
            S.dma(k.oaT[:, h, qc:qc + Nq], ot.ap[:, 0, 0:Nq], reads=[ot], writes=[k.t_oa[0 if qc < 128 else (qc - 128) // 512 + 1]])

    k.bg.finish(bg16)
    while not k.bg.done():
        k.bg.issue(4, bg32)
        k.bg.finish(bg16)
    A.reset()
    Kd = A.alloc("Kd", [2, C + PAST], BF16)
    Qd = A.alloc("Qd", [2, C], BF16)
    Vd = A.alloc("Vd", [66, 258], BF16)
    tt32 = [A.alloc("dtt32_%d" % i, [128], F32) for i in range(3)]
    PT = [A.alloc("dPT%d" % i, [256], BF16) for i in range(10)]
    o16 = [A.alloc("do16_%d" % i, [256], BF16) for i in range(2)]
    oTs = [A.alloc("doTs%d" % i, [2, 256], BF16) for i in range(2)]
    rz = [A.alloc("drz%d" % i, [4], F32) for i in range(4)]
    osq = [A.alloc("dosq%d" % i, [256], F32) for i in range(2)]
    o32 = [A.alloc("do32_%d" % i, [256], F32) for i in range(2)]
    t1 = [A.alloc("dt1_%d" % i, [256], F32) for i in range(2)]
    dgroups = groups(256)
    for h in range(4):
        S.dma(Kd.ap, k.scrT["kb"][:, 2 * h:2 * h + 2, :], reads=all_scr("kb") + [k.t_cache["kb"]], writes=[Kd])
        S.dma(Qd.ap, k.scrT["qb"][:, 2 * h:2 * h + 2, 0:C], reads=all_scr("qb"), writes=[Qd])
        load_V(Vd, k.vB, h, 258)
        for gi, (qc, Nq, lst) in enumerate(dgroups):
            nsub = (Nq + 127) // 128
            first = {}; last = {}
            for idx, (kt, qoff, msk) in enumerate(lst):
                for s_ in range(nsub):
                    if qoff <= s_ * 128:
                        first.setdefault(s_, idx); last[s_] = idx
            is_meta_or_s = qc < 128
            pend = []
            LA = ATT_LA
            for idx, (kt, qoff, msk) in enumerate(lst):
                name, col, K_ = kts[kt]
                n = Nq - qoff
                msk = msk and not is_meta_or_s
                ps = SB[k.sb_i % 4]; k.sb_i += 1
                psv = ps.ap.rearrange("p (c n) -> p c n", c=2)
                pts = []
                for c_ in range(2):
                    S.op("pe", lambda e, psv=psv, c_=c_, K_=K_, n=n, col=col, qc=qc, qoff=qoff, Nq=Nq: e.matmul(
                        psv[:K_, c_, :n], lhsT=Kd.ap[:, c_, col:col + K_], rhs=Qd.ap[:, c_, qc + qoff:qc + Nq], start=True, stop=True),
                        reads=[Kd, Qd], writes=[ps])
                for c_ in range(2):
                    pt = nxt(PT, "dpt_i"); pts.append(pt)
                    if msk:
                        mw = min(128, n)
                        t32 = nxt(tt32, "dt32_i")
                        S.op("dve", lambda e, psv=psv, c_=c_, K_=K_, mw=mw, t32=t32: e.scalar_tensor_tensor(
                            out=t32.ap[:K_, :mw], in0=psv[:K_, c_, :mw], scalar=float(DIFF_SCALE), in1=k.kmask.ap[:K_, :mw], op0=ALU.mult, op1=ALU.add),
                            reads=[ps, k.kmask], writes=[t32])
                        S.op("act", lambda e, t32=t32, pt=pt, K_=K_, mw=mw: e.activation(out=pt.ap[:K_, :mw], in_=t32.ap[:K_, :mw], func=ACTF.Exp),
                             reads=[t32], writes=[pt])
                        if n > mw:
                            S.op("act", lambda e, psv=psv, c_=c_, pt=pt, K_=K_, mw=mw, n=n: e.activation(out=pt.ap[:K_, mw:n], in_=psv[:K_, c_, mw:n], func=ACTF.Exp,
                                                                                                        scale=float(DIFF_SCALE)), reads=[ps], writes=[pt])
                    else:
                        S.op("act", lambda e, psv=psv, c_=c_, pt=pt, K_=K_, n=n: e.activation(out=pt.ap[:K_, :n], in_=psv[:K_, c_, :n], func=ACTF.Exp,
                                                                                            scale=float(DIFF_SCALE)), reads=[ps], writes=[pt])

                def pv(idx=idx, kt=kt, qoff=qoff, K_=K_, pts=pts):
                    for c_ in range(2):
                        pt = pts[c_]
                        for s_ in range(nsub):
                            if qoff > s_ * 128:
                                continue
                            qs = s_ * 128 - qoff
                            qn = min(128, Nq - s_ * 128)
                            ob = banks[2 + 2 * c_ + s_]
                            S.op("pe", lambda e, ob=ob, pt=pt, K_=K_, qs=qs, qn=qn, kt=kt, st=(first[s_] == idx), sp=(last[s_] == idx): e.matmul(
                                ob.ap[:qn, 0:257], lhsT=pt.ap[:K_, qs:qs + qn], rhs=Vd.ap[:K_, kt, 0:257], start=st, stop=sp), reads=[pt, Vd], writes=[ob])
                pend.append(pv)
                if len(pend) > LA:
                    pend.pop(0)()
            while pend:
                pend.pop(0)()
            ot = nxt(oTs, "dots_i")
            for s_ in range(nsub):
                qn = min(128, Nq - s_ * 128)
                ob0 = banks[2 + s_]; ob1 = banks[4 + s_]
                r = nxt(rz, "drz_i"); o6 = nxt(o16, "do16_i"); tt = nxt(t1, "dt1_i"); oo = nxt(o32, "do32_i"); sq = nxt(osq, "dosq_i")
                S.op("dve", lambda e, ob0=ob0, r=r, qn=qn: e.reciprocal(out=r.ap[:qn, 0:1], in_=ob0.ap[:qn, 256:257]), reads=[ob0], writes=[r])
                S.op("dve", lambda e, ob1=ob1, r=r, qn=qn: e.reciprocal(out=r.ap[:qn, 1:2], in_=ob1.ap[:qn, 256:257]), reads=[ob1], writes=[r])
                S.op("dve", lambda e, r=r, qn=qn: e.tensor_tensor(out=r.ap[:qn, 1:2], in0=r.ap[:qn, 1:2], in1=k.nlam.ap[:qn], op=ALU.mult), reads=[r, k.nlam], writes=[r])
                S.op("dve", lambda e, ob0=ob0, r=r, tt=tt, qn=qn: e.tensor_scalar(out=tt.ap[:qn], in0=ob0.ap[:qn, 0:256], scalar1=r.ap[:qn, 0:1], scalar2=None, op0=ALU.mult),
                     reads=[ob0, r], writes=[tt])
                S.op("dve", lambda e, ob1=ob1, r=r, tt=tt, oo=oo, qn=qn: e.scalar_tensor_tensor(out=oo.ap[:qn], in0=ob1.ap[:qn, 0:256], scalar=r.ap[:qn, 1:2], in1=tt.ap[:qn],
                                                                                            op0=ALU.mult, op1=ALU.add), reads=[ob1, r, tt], writes=[oo])
                S.op("act", lambda e, oo=oo, sq=sq, r=r, qn=qn: e.activation(out=sq.ap[:qn], in_=oo.ap[:qn], func=ACTF.Square, accum_out=r.ap[:qn, 2:3]),
                     reads=[oo], writes=[sq, r])
                S.op("dve", lambda e, r=r, qn=qn: e.tensor_scalar(out=r.ap[:qn, 2:3], in0=r.ap[:qn, 2:3], scalar1=1.0 / 256.0, scalar2=1e-5, op0=ALU.mult, op1=ALU.add),
                     reads=[r], writes=[r])
                S.op("act", lambda e, r=r, qn=qn: e.activation(out=r.ap[:qn, 2:3], in_=r.ap[:qn, 2:3], func=ACTF.Ln), reads=[r], writes=[r])
                S.op("act", lambda e, r=r, qn=qn: e.activation(out=r.ap[:qn, 2:3], in_=r.ap[:qn, 2:3], func=ACTF.Exp, scale=-0.5), reads=[r], writes=[r])
                S.op("dve", lambda e, oo=oo, r=r, o6=o6, qn=qn: e.scalar_tensor_tensor(out=o6.ap[:qn], in0=oo.ap[:qn], scalar=r.ap[:qn, 2:3], in1=k.gsb.ap[:qn],
                                                                                   op0=ALU.mult, op1=ALU.mult), reads=[oo, r, k.gsb], writes=[o6])
                k.transpose_chunks(o6, lambda c, o6=o6, qn=qn: o6.ap[:qn, c * 128:(c + 1) * 128], 2, qn, ot, ot.ap[:, :, s_ * 128:s_ * 128 + qn], "act")
            si = 0 if qc < 128 else (qc - 128) // 512 + 1
            S.dma(k.obT[:, 2 * h:2 * h + 2, qc:qc + Nq], ot.ap[:, :, 0:Nq], reads=[ot], writes=[k.t_ob[si]])
```

```python
import contextlib
import numpy as np
import concourse.bass as bass
import concourse.mybir as mybir
from concourse.bass_utils import run_bass_kernel_spmd

F32 = mybir.dt.float32
BF16 = mybir.dt.bfloat16
ALU = mybir.AluOpType
ACTF = mybir.ActivationFunctionType
AX = mybir.AxisListType


class Buf:
    __slots__ = ("name", "ap", "kind", "writers", "readers", "sem", "semcnt")

    def __init__(self, name, ap=None, kind="sbuf"):
        self.name = name
        self.ap = ap
        self.kind = kind
        self.writers = []
        self.readers = []
        self.sem = None
        self.semcnt = 0


class Op:
    __slots__ = ("eng", "emit", "deps", "is_dma", "sem", "val", "sig", "idx", "rawdeps")

    def __init__(self, eng, emit):
        self.eng = eng
        self.emit = emit
        self.deps = []
        self.rawdeps = set()
        self.is_dma = False
        self.sem = None
        self.val = 0
        self.sig = False


class Sched:
    ENGS = ("pe", "act", "dve", "pool", "sp")

    def __init__(self, nc):
        self.nc = nc
        self.stack = contextlib.ExitStack()
        self.ops = {e: [] for e in self.ENGS}
        self.all_ops = []
        self.esem = {}
        for e in ("pe", "act", "dve", "pool"):
            self.esem[e] = self.stack.enter_context(nc.semaphore("sem_" + e))
        self.dma_bufs = []
        self.free_sems = []
        self.store_eng = "sp"
        self.nsem = 4
        self.n_names = 0

    def sbuf(self, name, shape, dtype):
        t = self.stack.enter_context(self.nc.sbuf_tensor(name, list(shape), dtype))
        return Buf(name, t.ap() if hasattr(t, "ap") and callable(t.ap) else t[:], "sbuf")

    def psum(self, name, shape, dtype):
        t = self.stack.enter_context(self.nc.psum_tensor(name, list(shape), dtype))
        return Buf(name, t.ap() if hasattr(t, "ap") and callable(t.ap) else t[:], "psum")

    def dram(self, name, shape, dtype, kind="Internal"):
        return self.nc.dram_tensor(name, list(shape), dtype, kind=kind).ap()

    def tok(self, name):
        return Buf(name, None, "dram")

    def sub(self, buf, name=None):
        return Buf(name or buf.name + "_s", buf.ap, buf.kind)

    def _record(self, op, reads, writes):
        deps = op.deps
        for r in reads:
            for w_ in r.writers:
                deps.append(w_)
                op.rawdeps.add(id(w_))
            if r.kind == "psum":
                deps.extend(r.readers)
        for w in writes:
            if w.kind == "dram" and not w.readers:
                continue
            deps.extend(w.writers)
            deps.extend(w.readers)
        for r in reads:
            r.readers.append(op)
        for w in writes:
            if w.kind == "dram" and not w.readers:
                w.writers.append(op)
            else:
                w.writers = [op]
                w.readers = []
        op.idx = len(self.ops[op.eng])
        self.ops[op.eng].append(op)
        self.all_ops.append(op)

    def op(self, eng, emit, reads=(), writes=()):
        o = Op(eng, emit)
        self._record(o, reads, writes)
        return o

    def dma(self, out, in_, reads=(), writes=(), eng="sp", **kw):
        if eng == "sp" and not any(b.kind == "sbuf" for b in writes):
            eng = self.store_eng
        o = Op(eng, lambda e: e.dma_start(out=out, in_=in_, **kw))
        o.is_dma = True
        owner = None
        for b in list(writes) + list(reads):
            if b.kind == "sbuf":
                owner = b
                break
        if owner is None:
            owner = (list(writes) + list(reads))[0]
        if owner.sem is None:
            if self.free_sems:
                owner.sem, owner.semcnt = self.free_sems.pop()
            else:
                owner.sem = self.stack.enter_context(self.nc.semaphore("dsem%d" % self.nsem))
                self.nsem += 1
            self.dma_bufs.append(owner)
        owner.semcnt += 16
        o.sem = owner.sem
        o.val = owner.semcnt
        o.sig = True
        self._record(o, reads, writes)
        return o

    @staticmethod
    def _needs_wait(op, d):
        if d.is_dma:
            return True
        if d.eng != op.eng:
            return True
        if op.eng == "pe":
            return False
        return id(d) in op.rawdeps

    def finish(self):
        nc = self.nc
        for op in self.all_ops:
            for d in op.deps:
                if not d.is_dma and self._needs_wait(op, d):
                    d.sig = True
        for e in ("pe", "act", "dve", "pool"):
            c = 0
            for op in self.ops[e]:
                if op.is_dma:
                    continue
                if op.sig:
                    c += 1
                    op.val = c
                    op.sem = self.esem[e]
        fin = {}
        for b in self.dma_bufs:
            if id(b.sem) not in fin or fin[id(b.sem)][1] < b.semcnt:
                fin[id(b.sem)] = (b.sem, b.semcnt)
        finals = list(fin.values())
        self.nwaits = 0

        def emit_engine(ename, eng):
            seen = {}
            for op in self.ops[ename]:
                need = {}
                for d in op.deps:
                    if not self._needs_wait(op, d):
                        continue
                    k = id(d.sem)
                    if seen.get(k, 0) >= d.val:
                        continue
                    if k not in need or need[k][1] < d.val:
                        need[k] = (d.sem, d.val)
                for k, (sem, val) in need.items():
                    eng.wait_ge(sem, val)
                    seen[k] = val
                    self.nwaits += 1
                inst = op.emit(eng)
                if op.sig:
                    inst.then_inc(op.sem, 16 if op.is_dma else 1)
            if ename == "sp":
                for sem, val in finals:
                    eng.wait_ge(sem, val)

        with nc.Block() as block:
            @block.tensor
            def _(e):
                emit_engine("pe", e)

            @block.scalar
            def _(e):
                emit_engine("act", e)

            @block.vector
            def _(e):
                emit_engine("dve", e)

            @block.gpsimd
            def _(e):
                emit_engine("pool", e)

            @block.sync
            def _(e):
                emit_engine("sp", e)
        self.stack.close()


class Arena:
    def __init__(self, S, nbytes):
        self.S = S
        self.n32 = nbytes // 4
        self.base = S.sbuf("arena", [128, self.n32], F32)
        self.off = 0
        self.cur = []
        self.prev_ops = []

    def reset(self):
        ops = {}
        for b in self.cur:
            for w_ in b.writers:
                ops[id(w_)] = w_
            for r in b.readers:
                ops[id(r)] = r
        self.prev_ops = list(ops.values())
        for b in self.cur:
            if b.sem is not None:
                self.S.free_sems.append((b.sem, b.semcnt))
        self.cur = []
        self.off = 0

    def alloc_at(self, name, off32, free_shape, dtype):
        save = self.off
        self.off = off32
        b = self.alloc(name, free_shape, dtype)
        self.off = max(save, self.off)
        return b

    def alloc(self, name, free_shape, dtype):
        esz = 4 if dtype == F32 else 2
        n = 1
        for s in free_shape:
            n *= s
        n32 = (n * esz + 3) // 4
        n32 = (n32 + 7) // 8 * 8
        assert self.off + n32 <= self.n32, "arena overflow %s: %d + %d > %d" % (name, self.off, n32, self.n32)
        ap = self.base.ap[:, self.off:self.off + n32]
        if dtype != F32:
            ap = ap.bitcast(dtype)
        ap = ap[:, 0:n]
        if len(free_shape) == 2:
            ap = ap.rearrange("p (a b) -> p a b", a=free_shape[0])
        elif len(free_shape) == 3:
            ap = ap.rearrange("p (a b c) -> p a b c", a=free_shape[0], b=free_shape[1])
        self.off += n32
        b = Buf(name, ap, "sbuf")
        b.readers = list(self.prev_ops)
        self.cur.append(b)
        return b


D = 2048
KC = 16
NF = 4096
NM = 16
L = NM + NF
TS = 64
PAST = 4096
DEPTH = 2
C = 128 + NF
NIN = 10248
DFF = 8192
ALPHA = (2 * DEPTH) ** 0.25
FOX_SCALE = 128 ** -0.5
DIFF_SCALE = 128 ** -0.5
NEG = -1e30
GROUPS = [("qa", 0, 1024), ("ka", 1024, 1024), ("va", 2048, 1024), ("fa", 3072, 8),
          ("qb", 3080, 1024), ("kb", 4104, 1024), ("vb", 5128, 1024), ("ga", 6152, 2048), ("gb", 8200, 2048)]


def supertiles():
    sts = [(0, [(0, 80)])]
    for j in range(8):
        sts.append((128 + 512 * j, [(128 * i, 128) for i in range(4)]))
    return sts


class WStream:
    def __init__(self, S, nbuf, bufs):
        self.S = S
        self.bufs = bufs
        self.nbuf = nbuf
        self.order = []
        self.next_load = 0
        self.next_use = 0

    def plan(self, pieces):
        self.order = pieces

    def _issue(self, i):
        name, ap, tok = self.order[i]
        b = self.bufs[i % self.nbuf]
        kcp, ncols = ap.shape[1], ap.shape[2]
        self.S.dma(b.ap[:, 0:kcp, 0:ncols], ap, reads=list(tok), writes=[b])

    def get(self, name):
        i = self.next_use
        assert self.order[i][0] == name, (self.order[i][0], name)
        while self.next_load < len(self.order) and self.next_load < i + self.nbuf:
            self._issue(self.next_load)
            self.next_load += 1
        self.next_use += 1
        return self.bufs[i % self.nbuf]


class WMat:
    def __init__(self, S, name, src, bw=512):
        self.name = name
        self.src = src
        self.K, self.N = src.shape
        self.kc = self.K // 128
        self.bw = min(bw, self.N)
        self.nblk = self.N // self.bw
        self.scr = S.dram(name + "_bf", [self.nblk, 128, self.kc, self.bw], BF16)
        self.cp = min(self.N, 2048)
        self.ncp = self.N // self.cp
        self.toks = [[S.tok("%s_t%d_%d" % (name, k, c)) for c in range(self.ncp)] for k in range(self.kc)]

    def piece(self, blk, kc0=0, kcp=None):
        kcp = self.kc if kcp is None else kcp
        cpi = (blk * self.bw) // self.cp
        toks = [self.toks[k][cpi] for k in range(kc0, kc0 + kcp)]
        return ("%s_b%d_k%d" % (self.name, blk, kc0), self.scr[blk, :, kc0:kc0 + kcp, :], toks)


def cast_weights(S, mats, stage32, stage16):
    engs = ("act", "dve", "pool")
    items = [(m, k, c) for m in mats for k in range(m.kc) for c in range(m.ncp)]
    n32 = len(stage32)

    def load(i):
        m, k, c = items[i]
        w = m.cp
        s32 = stage32[i % n32]
        S.dma(s32.ap[:, 0:w], m.src[k * 128:(k + 1) * 128, c * w:(c + 1) * w], writes=[s32])

    LA = n32 - 1
    for i in range(min(LA, len(items))):
        load(i)
    for i, (m, k, c) in enumerate(items):
        if i + LA < len(items):
            load(i + LA)
        s32 = stage32[i % n32]
        s16 = stage16[i % len(stage16)]
        w = m.cp
        e = engs[i % 3]
        if e == "act":
            S.op("act", lambda en, s32=s32, s16=s16, w=w: en.activation(out=s16.ap[:, 0:w], in_=s32.ap[:, 0:w], func=ACTF.Copy),
                 reads=[s32], writes=[s16])
        else:
            S.op(e, lambda en, s32=s32, s16=s16, w=w: en.tensor_copy(out=s16.ap[:, 0:w], in_=s32.ap[:, 0:w]),
                 reads=[s32], writes=[s16])
        nb = w // m.bw
        b0 = (c * w) // m.bw
        S.dma(m.scr[b0:b0 + nb, :, k, :].rearrange("b p n -> p b n"),
              s16.ap[:, 0:w].rearrange("p (b n) -> p b n", b=nb), reads=[s16], writes=[m.toks[k][c]])


class BGCast:
    def __init__(self, S, mats):
        self.S = S
        self.items = [(m, k, c) for m in mats for k in range(m.kc) for c in range(m.ncp)]
        self.pos = 0
        self.pending = []
        self.n = 0

    def issue(self, n, s32):
        assert not self.pending
        for j in range(n):
            if self.pos >= len(self.items):
                break
            m, k, c = self.items[self.pos]
            self.pos += 1
            b = s32[j]
            w = m.cp
            self.S.dma(b.ap[:, 0:w], m.src[k * 128:(k + 1) * 128, c * w:(c + 1) * w], writes=[b])
            self.pending.append((m, k, c, b))

    def finish(self, s16):
        engs = ("act", "dve", "pool")
        for j, (m, k, c, b) in enumerate(self.pending):
            o = s16[j]
            w = m.cp
            e = engs[self.n % 3]
            self.n += 1
            if e == "act":
                self.S.op("act", lambda en, b=b, o=o, w=w: en.activation(out=o.ap[:, 0:w], in_=b.ap[:, 0:w], func=ACTF.Copy), reads=[b], writes=[o])
            else:
                self.S.op(e, lambda en, b=b, o=o, w=w: en.tensor_copy(out=o.ap[:, 0:w], in_=b.ap[:, 0:w]), reads=[b], writes=[o])
            nb = w // m.bw
            b0 = (c * w) // m.bw
            self.S.dma(m.scr[b0:b0 + nb, :, k, :].rearrange("b p n -> p b n"),
                       o.ap[:, 0:w].rearrange("p (b n) -> p b n", b=nb), reads=[o], writes=[m.toks[k][c]])
        self.pending = []

    def done(self):
        return self.pos >= len(self.items) and not self.pending


def out_rows(c, rows):
    if c < 128:
        return [(0, 64, "s", 0), (64, 80, "p", 0)]
    return [(0, rows, "p", c - 128 + NM)]


class K:
    pass


def build(stop_after=None):
    nc = bass.Bass("TRN2", target_bir_lowering=False)
    S = Sched(nc)
    k = K()

    def I(name, shape):
        return nc.dram_tensor(name, list(shape), F32, kind="ExternalInput").ap()

    def O(name, shape):
        return nc.dram_tensor(name, list(shape), F32, kind="ExternalOutput").ap()

    x_p = I("x_p", [NF, D]); x_s = I("x_s", [TS, D]); meta = I("meta", [NM, D])
    cfk = I("cfk", [DEPTH, PAST, 1024]); cfv = I("cfv", [DEPTH, PAST, 1024]); cfl = I("cfl", [DEPTH, PAST, 8])
    cdk = I("cdk", [DEPTH, PAST, 1024]); cdv = I("cdv", [DEPTH, PAST, 1024])
    ln_in_g = I("ln_in_g", [D]); ln_in_b = I("ln_in_b", [D])
    w_in = I("w_in", [DEPTH, D, NIN]); b_f = I("b_f", [DEPTH, 8])
    lq1 = I("lq1", [DEPTH, 128]); lk1 = I("lk1", [DEPTH, 128]); lq2 = I("lq2", [DEPTH, 128]); lk2 = I("lk2", [DEPTH, 128])
    subg = I("subg", [DEPTH, 256])
    w_br_a = I("w_br_a", [DEPTH, 1024, D]); w_br_b = I("w_br_b", [DEPTH, 1024, D]); w_out = I("w_out", [DEPTH, D, D])
    ln1_g = I("ln1_g", [DEPTH, D]); ln1_b = I("ln1_b", [DEPTH, D])
    w_up = I("w_up", [DEPTH, D, DFF]); w_down = I("w_down", [DEPTH, DFF, D])
    ln2_g = I("ln2_g", [DEPTH, D]); ln2_b = I("ln2_b", [DEPTH, D])
    c_ident = I("c_ident", [128, 128]); c_cmask = I("c_cmask", [128, 128]); c_kmask = I("c_kmask", [128, 128])
    c_cos = I("c_cos", [C, 16]); c_sin = I("c_sin", [C, 16])

    y_p = O("y_p", [NF, D]); y_s = O("y_s", [TS, D])
    outs = {}
    for nm, w in (("fk", 1024), ("fv", 1024), ("fl", 8), ("dk", 1024), ("dv", 1024)):
        outs[nm + "p"] = O(nm + "_p", [DEPTH, L, w])
        outs[nm + "s"] = O(nm + "_s", [DEPTH, TS, w])

    import os
    hbuf = S.dram("hbuf", [C, D], F32, kind=("ExternalOutput" if os.environ.get("DBG_DUMP") else "Internal"))
    hT = S.dram("hT", [128, KC, C], BF16)
    scrT = {n: S.dram(n + "T", [128, 8, C + PAST], BF16) for n in ("qa", "ka", "qb", "kb")}
    vA = S.dram("vA", [C + PAST, 8 * 130], BF16)
    vB = S.dram("vB", [C + PAST, 4 * 258], BF16)
    gsc = S.dram("gsc", [C, 4096], BF16)
    import os
    dk_ = "ExternalOutput" if os.environ.get("DBG_DUMP") else "Internal"
    oaT = S.dram("oaT", [128, 8, C], BF16, kind=dk_)
    obT = S.dram("obT", [128, 8, C], BF16, kind=dk_)
    t_oa = [S.tok("oa_st%d" % i) for i in range(9)]
    t_ob = [S.tok("ob_st%d" % i) for i in range(9)]
    lfbuf = S.dram("lfbuf", [C, 8], F32)
    lfT_d = S.dram("lfT_d", [8, C], F32)
    cbuf = S.dram("cbuf", [8, C], F32, kind=("ExternalOutput" if os.environ.get("DBG_DUMP") else "Internal"))
    t_lfT = S.tok("lfT_tok")
    t_cbuf = S.tok("cbuf_tok")
    t_cache = {n: S.tok("cache_" + n) for n in ("ka", "kb", "vA", "vB")}
    t_h = [S.tok("h_st%d" % i) for i in range(9)]
    t_hT = [S.tok("hT_st%d" % i) for i in range(9)]
    t_scr = {n: [S.tok("%s_st%d" % (n, i)) for i in range(9)] for n in ("qa", "ka", "qb", "kb", "vA", "vB", "gs", "lf")}

    identf = S.sbuf("identf", [128, 128], F32)
    identb = S.sbuf("identb", [128, 128], BF16)
    cmask = S.sbuf("cmask", [128, 128], F32)
    kmask = S.sbuf("kmask", [128, 128], F32)
    lnp = [S.sbuf("lnp%d" % i, [128, D], F32) for i in range(4)]
    bfb = S.sbuf("bfb", [128, 8], F32)
    nstat = 4
    st_bn = [S.sbuf("st_bn%d" % i, [128, 4, 6], F32) for i in range(nstat)]
    st_mv = [S.sbuf("st_mv%d" % i, [128, 2], F32) for i in range(nstat)]
    st_rs = [S.sbuf("st_rs%d" % i, [128, 1], F32) for i in range(nstat)]
    k.stat_i = 0
    banks = [S.psum("bank%d" % i, [128, 512], F32) for i in range(8)]
    banks16 = [Buf("bank%d_16" % i, b.ap.bitcast(BF16), "psum") for i, b in enumerate(banks)]
    for b16, b in zip(banks16, banks):
        pass
    k.tr_i = 0
    k.bank_i = 0
    A = Arena(S, 170 * 1024)
    negc = S.sbuf("negc", [128, 66, 8], F32)
    nlam = S.sbuf("nlam", [128, 1], F32)
    gsb = S.sbuf("gsb", [128, 256], F32)

    S.dma(identf.ap, c_ident, writes=[identf])
    S.op("dve", lambda e: e.tensor_copy(out=identb.ap, in_=identf.ap), reads=[identf], writes=[identb])
    S.dma(cmask.ap, c_cmask, writes=[cmask])
    S.dma(kmask.ap, c_kmask, writes=[kmask])

    def trbank():
        i = 7 if getattr(k, "tr_single", False) else 6 + (k.tr_i % 2)
        k.tr_i += 1
        return banks[i], banks16[i].ap

    def transpose_chunks(src_buf, src_fn, n, rows, dst_buf, dst_ap, eng):
        pb, p16 = trbank()
        for c in range(n):
            S.op("pe", lambda e, c=c: e.transpose(p16[:, c * 128:c * 128 + rows], src_fn(c), identb.ap[:rows, :rows]),
                 reads=[src_buf, identb], writes=[pb])
        src = p16[:, 0:n * 128].rearrange("p (c r) -> p c r", c=n)[:, :, 0:rows]
        if eng == "act":
            S.op("act", lambda e: e.activation(out=dst_ap, in_=src, func=ACTF.Copy), reads=[pb], writes=[dst_buf])
        else:
            S.op(eng, lambda e: e.tensor_copy(out=dst_ap, in_=src), reads=[pb], writes=[dst_buf])

    def layernorm(xb, x_ap, rows, g_b, b_b, out_b, out_ap):
        i = k.stat_i % nstat
        k.stat_i += 1
        bn, mv, rs = st_bn[i], st_mv[i], st_rs[i]
        for c in range(4):
            S.op("dve", lambda e, c=c: e.bn_stats(out=bn.ap[:rows, c, :], in_=x_ap[:, c * 512:(c + 1) * 512]), reads=[xb], writes=[bn])
        S.op("dve", lambda e: e.bn_aggr(out=mv.ap[:rows], in_=bn.ap[:rows]), reads=[bn], writes=[mv])
        S.op("dve", lambda e: e.tensor_scalar(out=rs.ap[:rows], in0=mv.ap[:rows, 1:2], scalar1=1e-5, scalar2=None, op0=ALU.add),
             reads=[mv], writes=[rs])
        S.op("act", lambda e: e.activation(out=rs.ap[:rows], in_=rs.ap[:rows], func=ACTF.Ln), reads=[rs], writes=[rs])
        S.op("act", lambda e: e.activation(out=rs.ap[:rows], in_=rs.ap[:rows], func=ACTF.Exp, scale=-0.5), reads=[rs], writes=[rs])
        S.op("dve", lambda e: e.tensor_scalar(out=out_ap, in0=x_ap, scalar1=mv.ap[:rows, 0:1], scalar2=rs.ap[:rows],
                                              op0=ALU.subtract, op1=ALU.mult), reads=[xb, mv, rs], writes=[out_b])
        S.op("pool", lambda e: e.tensor_tensor(out=out_ap, in0=out_ap, in1=g_b.ap[:rows], op=ALU.mult), reads=[out_b, g_b], writes=[out_b])
        S.op("pool", lambda e: e.tensor_tensor(out=out_ap, in0=out_ap, in1=b_b.ap[:rows], op=ALU.add), reads=[out_b, b_b], writes=[out_b])

    mats = {}
    for l in range(DEPTH):
        for g, off, w in GROUPS:
            mats[(l, g)] = WMat(S, "w%d%s" % (l, g), w_in[l][:, off:off + w])
        mats[(l, "bra")] = WMat(S, "w%dbra" % l, w_br_a[l])
        mats[(l, "brb")] = WMat(S, "w%dbrb" % l, w_br_b[l])
        mats[(l, "out")] = WMat(S, "w%dout" % l, w_out[l])
        mats[(l, "up")] = WMat(S, "w%dup" % l, w_up[l])
        mats[(l, "down")] = WMat(S, "w%ddown" % l, w_down[l])
    k.mats = mats

    A.reset()
    s32 = [A.alloc("s32_%d" % i, [2048], F32) for i in range(8)]
    s16 = [A.alloc("s16_%d" % i, [2048], BF16) for i in range(6)]
    order = []
    for l in range(DEPTH):
        order += [mats[(l, g)] for g, _, _ in GROUPS] + [mats[(l, n)] for n in ("bra", "brb", "out", "up", "down")]
    if stop_after in ("A0", "P0", "PRE"):
        order = [mats[(0, g)] for g, _, _ in GROUPS]
    n0 = len(GROUPS) + 5
    k.bg = BGCast(S, order[n0:])
    cast_weights(S, order[:n0], s32, s16)
    if stop_after == "P0":
        S.finish()
        return nc

    sts = supertiles()

    A.reset()
    xin = [A.alloc("xin%d" % i, [D], F32) for i in range(4)]
    xn16 = [A.alloc("xn16_%d" % i, [D], BF16) for i in range(2)]
    hTst = A.alloc("hTst", [KC, 512], BF16)
    S.dma(lnp[0].ap, ln_in_g.partition_broadcast(128), writes=[lnp[0]])
    S.dma(lnp[1].ap, ln_in_b.partition_broadcast(128), writes=[lnp[1]])
    allsubs = [(si, c0, off, rows) for si, (c0, subs) in enumerate(sts) for (off, rows) in subs]

    def pre_load(i):
        si_, c0_, off_, rows_ = allsubs[i]
        xb_ = xin[i % 4]
        if c0_ == 0:
            S.dma(xb_.ap[0:64], x_s, writes=[xb_])
            S.dma(xb_.ap[64:80], meta, writes=[xb_])
        else:
            f0 = c0_ - 128 + off_
            S.dma(xb_.ap[:rows_], x_p[f0:f0 + rows_], writes=[xb_])

    for i in range(3):
        pre_load(i)
    it = 0
    for si, (c0, subs) in enumerate(sts):
        T = subs[-1][0] + subs[-1][1]
        for (off, rows) in subs:
            if it + 3 < len(allsubs):
                pre_load(it + 3)
            xb = xin[it % 4]; x16 = xn16[it % 2]; it += 1
            layernorm(xb, xb.ap[:rows], rows, lnp[0], lnp[1], xb, xb.ap[:rows])
            S.dma(hbuf[c0 + off:c0 + off + rows], xb.ap[:rows], reads=[xb], writes=[t_h[si]])
            S.op("act", lambda e, xb=xb, x16=x16, rows=rows: e.activation(out=x16.ap[:rows], in_=xb.ap[:rows], func=ACTF.Copy),
                 reads=[xb], writes=[x16])
            for hf in range(2):
                transpose_chunks(x16, lambda c, x16=x16, rows=rows, hf=hf: x16.ap[:rows, (hf * 8 + c) * 128:(hf * 8 + c + 1) * 128],
                                 8, rows, hTst, hTst.ap[:, hf * 8:hf * 8 + 8, off:off + rows], "dve" if hf == 0 else "act")
        S.dma(hT[:, :, c0:c0 + T], hTst.ap[:, :, 0:T], reads=[hTst], writes=[t_hT[si]])

    if stop_after == "PRE":
        S.finish()
        return nc
    k.S = S; k.A = A; k.nc = nc
    k.__dict__.update(locals())
    for l in range(DEPTH):
        phase_A(k, l)
        if stop_after == "A0":
            break
        phase_B(k, l)
        if stop_after == "B0":
            break
        phase_C(k, l)
        if stop_after == "C0":
            break
    S.finish()
    return nc


def dense(k, xT_b, xT_fn, kc_total, mat, blocks, subs, evac, kc_piece=16, nbanks=6):
    S = k.S
    bw = mat.bw
    npiece = kc_total // kc_piece
    for blk in blocks:
        pbs = []
        for j in range(len(subs)):
            pbs.append(k.banks[k.bank_i % nbanks])
            k.bank_i += 1
        for pi in range(npiece):
            wb = k.ws.get(mat.piece(blk, pi * kc_piece, kc_piece)[0])
            for j, (off, rows) in enumerate(subs):
                pb = pbs[j]
                for kk in range(kc_piece):
                    kg = pi * kc_piece + kk
                    S.op("pe", lambda e, pb=pb, rows=rows, off=off, kg=kg, kk=kk, wb=wb: e.matmul(
                        pb.ap[:rows, :bw], lhsT=xT_fn(kg, off, rows), rhs=wb.ap[:, kk, :bw],
                        start=(kg == 0), stop=(kg == kc_total - 1)), reads=(list(xT_b) if isinstance(xT_b, (list, tuple)) else [xT_b]) + [wb], writes=[pb])
                if pi == npiece - 1:
                    evac(j, blk, pb, off, rows)


def phase_A(k, l):
    S, A = k.S, k.A
    sts = k.sts
    outs = k.outs
    A.reset()
    xT = [A.alloc("xT%d" % i, [KC, 512], BF16) for i in range(2)]
    wbufs = [A.alloc("wbuf%d" % i, [KC, 512], BF16) for i in range(4)]
    stT = {n: A.alloc("stT_" + n, [8, 512], BF16) for n in ("qa", "ka", "qb", "kb")}
    ev32 = [A.alloc("ev32_%d" % i, [512], F32) for i in range(3)]
    tm16 = [A.alloc("tm16_%d" % i, [512], BF16) for i in range(3)]
    vAt = [A.alloc("vAt%d" % i, [8, 130], BF16) for i in range(4)]
    vBt = [A.alloc("vBt%d" % i, [4, 258], BF16) for i in range(4)]
    gt = [A.alloc("gt%d" % i, [512], BF16) for i in range(3)]
    cs = [A.alloc("cs%d" % i, [2, 16], F32) for i in range(4)]
    rt = [A.alloc("rt%d" % i, [4, 4, 16], F32) for i in range(2)]
    lf = [A.alloc("lf%d" % i, [4, 8], F32) for i in range(2)]
    lft = [A.alloc("lft%d" % i, [128], F32) for i in range(2)]
    k.lft_i = 0
    k.ev_i = 0; k.tm_i = 0; k.gt_i = 0; k.rt_i = 0; k.lf_i = 0; k.ce = 0
    import os
    k.dbgva = os.environ.get("DBG_VA", "")
    for t in vAt + vBt:
        if "nomemset" in k.dbgva:
            break
        S.op("pool", lambda e, t=t: e.memset(t.ap, 1.0), writes=[t])
    S.dma(k.bfb.ap, k.b_f[l:l + 1, :].partition_broadcast(128) if False else k.b_f[l].partition_broadcast(128), writes=[k.bfb])

    ws = WStream(S, 4, wbufs)
    order = []
    import os
    k.groups = [g for g in GROUPS if g[0] in os.environ.get("DBG_GROUPS", "qa,ka,va,fa,qb,kb,vb,ga,gb").split(",")]
    for si in range(len(sts)):
        for g, off, w in k.groups:
            m = k.mats[(l, g)]
            for blk in range(m.nblk):
                order.append(m.piece(blk))
    ws.plan(order)
    k.ws = ws

    def outdma(name, c, rows, src_b, src_ap, col0, ncols):
        for (p0, p1, kind, r0) in out_rows(c, rows):
            dst = outs[name + kind][l, r0:r0 + (p1 - p0), col0:col0 + ncols]
            S.dma(dst, src_ap[p0:p1], reads=[src_b])

    def nxt(lst, attr):
        i = getattr(k, attr)
        setattr(k, attr, i + 1)
        return lst[i % len(lst)]

    def copy_eng():
        k.ce += 1
        return "act" if k.ce % 2 else "dve"

    for si, (c0, subs) in enumerate(sts):
        T = subs[-1][0] + subs[-1][1]
        xb = xT[si % 2]
        S.dma(xb.ap[:, :, 0:T], k.hT[:, :, c0:c0 + T], reads=[k.t_hT[si]], writes=[xb])
        xfn = lambda kg, off, rows, xb=xb: xb.ap[:, kg, off:off + rows]
        cst = []
        for j, (off, rows) in enumerate(subs):
            t = cs[j]
            S.dma(t.ap[:rows, 0, :], k.c_cos[c0 + off:c0 + off + rows], writes=[t])
            S.dma(t.ap[:rows, 1, :], k.c_sin[c0 + off:c0 + off + rows], writes=[t])
            cst.append(t)

        def ev_q(name):
            def f(j, blk, pb, off, rows):
                tm = nxt(tm16, "tm_i")
                S.op("act", lambda e: e.activation(out=tm.ap[:rows], in_=pb.ap[:rows], func=ACTF.Copy), reads=[pb], writes=[tm])
                k.transpose_chunks(tm, lambda c: tm.ap[:rows, c * 128:(c + 1) * 128], 4, rows, stT[name],
                                   stT[name].ap[:, 4 * blk:4 * blk + 4, off:off + rows], "dve")
            return f

        def ev_ka(j, blk, pb, off, rows):
            ev = nxt(ev32, "ev_i"); tm = nxt(tm16, "tm_i")
            S.op("act", lambda e: e.activation(out=ev.ap[:rows], in_=pb.ap[:rows], func=ACTF.Copy), reads=[pb], writes=[ev])
            outdma("fk", c0 + off, rows, ev, ev.ap, 512 * blk, 512)
            S.op("dve", lambda e: e.tensor_copy(out=tm.ap[:rows], in_=pb.ap[:rows]), reads=[pb], writes=[tm])
            k.transpose_chunks(tm, lambda c: tm.ap[:rows, c * 128:(c + 1) * 128], 4, rows, stT["ka"],
                               stT["ka"].ap[:, 4 * blk:4 * blk + 4, off:off + rows], "act")

        def ev_va(j, blk, pb, off, rows):
            ev = nxt(ev32, "ev_i")
            S.op("dve", lambda e: e.tensor_copy(out=ev.ap[:rows], in_=pb.ap[:rows]), reads=[pb], writes=[ev])
            outdma("fv", c0 + off, rows, ev, ev.ap, 512 * blk, 512)
            vt = vAt[j]
            S.op("act", lambda e: e.activation(out=vt.ap[:rows, 4 * blk:4 * blk + 4, 0:128],
                                               in_=pb.ap[:rows].rearrange("p (h d) -> p h d", h=4), func=ACTF.Copy), reads=[pb], writes=[vt])
            if blk == 1 and "nodma" not in k.dbgva:
                S.dma(k.vA[c0 + off:c0 + off + rows], vt.ap[:rows].rearrange("p h d -> p (h d)"), reads=[vt], writes=[k.t_scr["vA"][si]])

        def ev_fa(j, blk, pb, off, rows):
            t = nxt(lf, "lf_i")
            a = t.ap
            S.op("dve", lambda e: e.tensor_tensor(out=a[:rows, 0], in0=pb.ap[:rows, 0:8], in1=k.bfb.ap[:rows], op=ALU.add),
                 reads=[pb, k.bfb], writes=[t])
            S.op("act", lambda e: e.activation(out=a[:rows, 1], in_=a[:rows, 0], func=ACTF.Abs), reads=[t], writes=[t])
            S.op("act", lambda e: e.activation(out=a[:rows, 1], in_=a[:rows, 1], func=ACTF.Exp, scale=-1.0), reads=[t], writes=[t])
            S.op("act", lambda e: e.activation(out=a[:rows, 1], in_=a[:rows, 1], func=ACTF.Ln, bias=1.0), reads=[t], writes=[t])
            S.op("dve", lambda e: e.tensor_scalar(out=a[:rows, 2], in0=a[:rows, 0], scalar1=0.0, scalar2=None, op0=ALU.min), reads=[t], writes=[t])
            S.op("dve", lambda e: e.tensor_tensor(out=a[:rows, 3], in0=a[:rows, 2], in1=a[:rows, 1], op=ALU.subtract), reads=[t], writes=[t])
            outdma("fl", c0 + off, rows, t, a[:, 3], 0, 8)
            pbk, _ = k.trbank()
            S.op("pe", lambda e: e.matmul(pbk.ap[0:8, 0:rows], lhsT=a[:rows, 3], rhs=k.identf.ap[:rows, :rows], start=True, stop=True),
                 reads=[t, k.identf], writes=[pbk])
            lt = nxt(lft, "lft_i")
            S.op("dve", lambda e: e.tensor_copy(out=lt.ap[0:8, 0:rows], in_=pbk.ap[0:8, 0:rows]), reads=[pbk], writes=[lt])
            S.dma(k.lfT_d[:, c0 + off:c0 + off + rows], lt.ap[0:8, 0:rows], reads=[lt], writes=[k.t_lfT])
            S.dma(k.lfbuf[c0 + off:c0 + off + rows], a[:rows, 3], reads=[t], writes=[k.t_scr["lf"][si]])

        def rope(ev, rows, ct):
            r = nxt(rt, "rt_i")
            x = ev.ap[:rows].rearrange("p (c d) -> p c d", c=4)
            x1 = x[:, :, 0:16]; x2 = x[:, :, 16:32]
            cosb = ct.ap[:rows, 0:1, :].to_broadcast([rows, 4, 16])
            sinb = ct.ap[:rows, 1:2, :].to_broadcast([rows, 4, 16])
            ra = r.ap
            for (dst, a_, b_) in ((0, x1, cosb), (1, x2, sinb), (2, x2, cosb), (3, x1, sinb)):
                S.op("pool", lambda e, dst=dst, a_=a_, b_=b_: e.tensor_tensor(out=ra[:rows, dst], in0=a_, in1=b_, op=ALU.mult),
                     reads=[ev, ct], writes=[r])
            S.op("pool", lambda e: e.tensor_tensor(out=x1, in0=ra[:rows, 0], in1=ra[:rows, 1], op=ALU.subtract), reads=[r], writes=[ev])
            S.op("pool", lambda e: e.tensor_tensor(out=x2, in0=ra[:rows, 2], in1=ra[:rows, 3], op=ALU.add), reads=[r], writes=[ev])

        def ev_rope(name):
            def f(j, blk, pb, off, rows):
                ev = nxt(ev32, "ev_i"); tm = nxt(tm16, "tm_i")
                S.op("act", lambda e: e.activation(out=ev.ap[:rows], in_=pb.ap[:rows], func=ACTF.Copy), reads=[pb], writes=[ev])
                rope(ev, rows, cst[j])
                if name == "kb":
                    outdma("dk", c0 + off, rows, ev, ev.ap, 512 * blk, 512)
                S.op("dve", lambda e: e.tensor_copy(out=tm.ap[:rows], in_=ev.ap[:rows]), reads=[ev], writes=[tm])
                k.transpose_chunks(tm, lambda c: tm.ap[:rows, c * 128:(c + 1) * 128], 4, rows, stT[name],
                                   stT[name].ap[:, 4 * blk:4 * blk + 4, off:off + rows], copy_eng())
            return f

        def ev_vb(j, blk, pb, off, rows):
            ev = nxt(ev32, "ev_i")
            S.op("dve", lambda e: e.tensor_copy(out=ev.ap[:rows], in_=pb.ap[:rows]), reads=[pb], writes=[ev])
            outdma("dv", c0 + off, rows, ev, ev.ap, 512 * blk, 512)
            vt = vBt[j]
            S.op("act", lambda e: e.activation(out=vt.ap[:rows, 2 * blk:2 * blk + 2, 0:256],
                                               in_=pb.ap[:rows].rearrange("p (h d) -> p h d", h=2), func=ACTF.Copy), reads=[pb], writes=[vt])
            if blk == 1:
                S.dma(k.vB[c0 + off:c0 + off + rows], vt.ap[:rows].rearrange("p h d -> p (h d)"), reads=[vt], writes=[k.t_scr["vB"][si]])

        def ev_gate(goff):
            def f(j, blk, pb, off, rows):
                g = nxt(gt, "gt_i")
                S.op("act", lambda e: e.activation(out=g.ap[:rows], in_=pb.ap[:rows], func=ACTF.Sigmoid), reads=[pb], writes=[g])
                S.dma(k.gsc[c0 + off:c0 + off + rows, goff + 512 * blk:goff + 512 * blk + 512], g.ap[:rows], reads=[g],
                      writes=[k.t_scr["gs"][si]])
            return f

        evs = {"qa": ev_q("qa"), "ka": ev_ka, "va": ev_va, "fa": ev_fa, "qb": ev_rope("qb"), "kb": ev_rope("kb"),
               "vb": ev_vb, "ga": ev_gate(0), "gb": ev_gate(2048)}
        for g, goff, w in k.groups:
            m = k.mats[(l, g)]
            dense(k, xb, xfn, KC, m, range(m.nblk), subs, evs[g])
            if g in k.scrT:
                S.dma(k.scrT[g][:, :, c0:c0 + T], stT[g].ap[:, :, 0:T], reads=[stT[g]], writes=[k.t_scr[g][si]])


def _consts():
    ident = np.eye(128, dtype=np.float32)
    kk = np.arange(128)[:, None]
    qq = np.arange(128)[None, :]
    cmask = np.where(kk <= qq, 0.0, NEG).astype(np.float32)
    kmask = np.where((kk // 64) <= (qq // 64), 0.0, NEG).astype(np.float32)
    pos = np.zeros(C, dtype=np.float32)
    pos[0:64] = PAST + np.arange(64)
    pos[64:80] = np.arange(16)
    pos[128:] = NM + np.arange(NF)
    half = 16
    inv_freq = (np.float32(500000.0) ** (-np.arange(half, dtype=np.float32) / np.float32(half))).astype(np.float32)
    ang = (pos[:, None] * inv_freq[None, :]).astype(np.float32)
    return {"c_ident": ident, "c_cmask": cmask, "c_kmask": kmask,
            "c_cos": np.cos(ang).astype(np.float32), "c_sin": np.sin(ang).astype(np.float32)}


def make_in_maps(inp):
    cs = _consts()
    f = lambda a: np.ascontiguousarray(np.asarray(a, dtype=np.float32))
    shared = {
        "meta": f(inp["meta_tokens"]), "ln_in_g": f(inp["ln_in_g"]), "ln_in_b": f(inp["ln_in_b"]),
        "w_in": f(inp["w_in"]), "b_f": f(inp["b_f"]),
        "lq1": f(inp["lambda_q1"]), "lk1": f(inp["lambda_k1"]), "lq2": f(inp["lambda_q2"]), "lk2": f(inp["lambda_k2"]),
        "subg": f(inp["subln_g"]), "w_br_a": f(inp["w_br_a"]), "w_br_b": f(inp["w_br_b"]), "w_out": f(inp["w_out"]),
        "ln1_g": f(inp["ln1_g"]), "ln1_b": f(inp["ln1_b"]), "w_up": f(inp["w_up"]), "w_down": f(inp["w_down"]),
        "ln2_g": f(inp["ln2_g"]), "ln2_b": f(inp["ln2_b"]),
    }
    shared.update(cs)
    maps = []
    for c in range(8):
        m = dict(shared)
        m["x_p"] = f(inp["x_prompt"][c]); m["x_s"] = f(inp["x_sample"][c])
        m["cfk"] = f(np.asarray(inp["cache_fox_k"])[:, c].reshape(DEPTH, PAST, 1024))
        m["cfv"] = f(np.asarray(inp["cache_fox_v"])[:, c].reshape(DEPTH, PAST, 1024))
        m["cfl"] = f(np.asarray(inp["cache_fox_logf"])[:, c])
        m["cdk"] = f(np.asarray(inp["cache_diff_k"])[:, c].reshape(DEPTH, PAST, 1024))
        m["cdv"] = f(np.asarray(inp["cache_diff_v"])[:, c].reshape(DEPTH, PAST, 1024))
        maps.append(m)
    return maps


def gather(results):
    st = lambda n: np.stack([np.asarray(r[n]) for r in results], axis=0)
    y_p = st("y_p"); y_s = st("y_s")

    def kv(n, shp):
        a = st(n)
        a = np.moveaxis(a, 0, 1)
        return np.ascontiguousarray(a.reshape(a.shape[:3] + shp))
    return (y_p, y_s,
            kv("fk_p", (8, 128)), kv("fv_p", (8, 128)), kv("fl_p", (8,)), kv("dk_p", (4, 256)), kv("dv_p", (4, 256)),
            kv("fk_s", (8, 128)), kv("fv_s", (8, 128)), kv("fl_s", (8,)), kv("dk_s", (4, 256)), kv("dv_s", (4, 256)))


_NC_CACHE = {}


def kernel(**inputs):
    if "nc" not in _NC_CACHE:
        _NC_CACHE["nc"] = build()
    nc = _NC_CACHE["nc"]
    maps = make_in_maps(inputs)
    res = run_bass_kernel_spmd(nc, maps, core_ids=list(range(8)))
    return gather(res.results)


def phase_C(k, l):
    S, A = k.S, k.A
    sts = k.sts
    A.reset()
    wbufs = [A.alloc("wbuf%d" % i, [KC, 512], BF16) for i in range(2)]
    XT = A.alloc("XT", [KC, 512], BF16)
    xres = [A.alloc("xres%d" % i, [D], F32) for i in range(4)]
    m16 = [A.alloc("m16_%d" % i, [D], BF16) for i in range(4)]
    brt = [A.alloc("brt%d" % i, [512], F32) for i in range(2)]
    reg0 = A.off
    oTa = A.alloc("oTa", [8, 512], BF16)
    oTb = A.alloc("oTb", [8, 512], BF16)
    gpa = [A.alloc("gpa%d" % i, [512], BF16) for i in range(4)]
    gpb = [A.alloc("gpb%d" % i, [512], BF16) for i in range(4)]
    brt += [A.alloc("brt%d" % (i + 2), [512], F32) for i in range(6)]
    OV = [oTa, oTb] + gpa + gpb + brt[2:8]
    hidT = A.alloc_at("hidT", reg0, [64, 512], BF16)
    k.gp_i = 0; k.brt_i = 0
    lnp = k.lnp
    S.dma(lnp[0].ap, k.ln1_g[l].partition_broadcast(128), writes=[lnp[0]])
    S.dma(lnp[1].ap, k.ln1_b[l].partition_broadcast(128), writes=[lnp[1]])
    S.dma(lnp[2].ap, k.ln2_g[l].partition_broadcast(128), writes=[lnp[2]])
    S.dma(lnp[3].ap, k.ln2_b[l].partition_broadcast(128), writes=[lnp[3]])
    M = k.mats
    ws = WStream(S, 2, wbufs)
    order = []
    for si in range(len(sts)):
        for blk in range(4):
            order.append(M[(l, "bra")].piece(blk, 0, 8)); order.append(M[(l, "brb")].piece(blk, 0, 8))
        for blk in range(4):
            order.append(M[(l, "out")].piece(blk))
        for blk in range(16):
            order.append(M[(l, "up")].piece(blk))
        for blk in range(4):
            for pi in range(4):
                order.append(M[(l, "down")].piece(blk, 16 * pi, 16))
    ws.plan(order)
    k.ws = ws

    for si, (c0, subs) in enumerate(sts):
        T = subs[-1][0] + subs[-1][1]
        S.dma(oTa.ap[:, :, 0:T], k.oaT[:, :, c0:c0 + T], reads=[k.t_oa[si]], writes=[oTa])
        S.dma(oTb.ap[:, :, 0:T], k.obT[:, :, c0:c0 + T], reads=[k.t_ob[si]], writes=[oTb])
        for j, (off, rows) in enumerate(subs):
            S.dma(xres[j].ap[:rows], k.hbuf[c0 + off:c0 + off + rows], reads=[k.t_h[si]], writes=[xres[j]])
        gps = {}

        def ev_bra(j, blk, pb, off, rows):
            g = gpa[j]
            S.dma(g.ap[:rows], k.gsc[c0 + off:c0 + off + rows, 512 * blk:512 * blk + 512], reads=[k.t_scr["gs"][si]], writes=[g])
            t = brt[k.brt_i % 8]; k.brt_i += 1
            gps[(j, blk, "t")] = t
            S.op("dve", lambda e: e.tensor_tensor(out=t.ap[:rows], in0=pb.ap[:rows], in1=g.ap[:rows], op=ALU.mult), reads=[pb, g], writes=[t])

        def ev_brb(j, blk, pb, off, rows):
            g = gpb[j]; t = gps[(j, blk, "t")]
            S.dma(g.ap[:rows], k.gsc[c0 + off:c0 + off + rows, 2048 + 512 * blk:2048 + 512 * blk + 512], reads=[k.t_scr["gs"][si]], writes=[g])
            t2 = brt[k.brt_i % 8]; k.brt_i += 1
            S.op("dve", lambda e: e.tensor_tensor(out=t2.ap[:rows], in0=pb.ap[:rows], in1=g.ap[:rows], op=ALU.mult), reads=[pb, g], writes=[t2])
            S.op("pool", lambda e: e.tensor_tensor(out=m16[j].ap[:rows, 512 * blk:512 * blk + 512], in0=t2.ap[:rows],
                                                   in1=t.ap[:rows], op=ALU.add), reads=[t2, t], writes=[m16[j]])

        for blk in range(4):
            dense(k, oTa, lambda kg, off, rows: oTa.ap[:, kg, off:off + rows], 8, M[(l, "bra")], [blk], subs, ev_bra, kc_piece=8)
            dense(k, oTb, lambda kg, off, rows: oTb.ap[:, kg, off:off + rows], 8, M[(l, "brb")], [blk], subs, ev_brb, kc_piece=8)
        for j, (off, rows) in enumerate(subs):
            for hf in range(2):
                k.transpose_chunks(m16[j], lambda c, j=j, rows=rows, hf=hf: m16[j].ap[:rows, (hf * 8 + c) * 128:(hf * 8 + c + 1) * 128],
                                   8, rows, XT, XT.ap[:, hf * 8:hf * 8 + 8, off:off + rows], "dve" if hf == 0 else "act")

        def ev_res(j, blk, pb, off, rows):
            xs = xres[j].ap[:rows, 512 * blk:512 * blk + 512]
            S.op("dve", lambda e: e.scalar_tensor_tensor(out=xs, in0=xs, scalar=float(ALPHA), in1=pb.ap[:rows], op0=ALU.mult, op1=ALU.add),
                 reads=[xres[j], pb], writes=[xres[j]])

        dense(k, XT, lambda kg, off, rows: XT.ap[:, kg, off:off + rows], KC, M[(l, "out")], range(4), subs, ev_res)
        for j, (off, rows) in enumerate(subs):
            k.layernorm(xres[j], xres[j].ap[:rows], rows, lnp[0], lnp[1], xres[j], xres[j].ap[:rows])
            S.op("act", lambda e, j=j, rows=rows: e.activation(out=m16[j].ap[:rows], in_=xres[j].ap[:rows], func=ACTF.Copy), reads=[xres[j]], writes=[m16[j]])
            for hf in range(2):
                k.transpose_chunks(m16[j], lambda c, j=j, rows=rows, hf=hf: m16[j].ap[:rows, (hf * 8 + c) * 128:(hf * 8 + c + 1) * 128],
                                   8, rows, XT, XT.ap[:, hf * 8:hf * 8 + 8, off:off + rows], "dve" if hf == 0 else "act")
        mu = M[(l, "up")]
        halves = [subs]
        for hs in halves:
            h0 = hs[0][0]
            Th = hs[-1][0] + hs[-1][1] - h0
            for blk in range(16):
                wb = ws.get(mu.piece(blk)[0])
                for hc in range(4):
                    pb = k.banks[k.bank_i % 6]; k.bank_i += 1
                    for kg in range(KC):
                        S.op("pe", lambda e, pb=pb, wb=wb, hc=hc, kg=kg, h0=h0, Th=Th: e.matmul(
                            pb.ap[:, :Th], lhsT=wb.ap[:, kg, hc * 128:(hc + 1) * 128], rhs=XT.ap[:, kg, h0:h0 + Th],
                            start=(kg == 0), stop=(kg == KC - 1)), reads=[XT, wb], writes=[pb])
                    t = brt[k.brt_i % 2]; k.brt_i += 1
                    S.op("act", lambda e, pb=pb, t=t, Th=Th: e.activation(out=t.ap[:, :Th], in_=pb.ap[:, :Th], func=ACTF.Relu), reads=[pb], writes=[t])
                    S.op("pool", lambda e, t=t, blk=blk, hc=hc, Th=Th: e.tensor_tensor(out=hidT.ap[:, blk * 4 + hc, 0:Th], in0=t.ap[:, :Th], in1=t.ap[:, :Th], op=ALU.mult),
                         reads=[t], writes=[hidT] + OV)
            j0 = subs.index(hs[0])

            def ev_res2(j, blk, pb, off, rows, j0=j0):
                ev_res(j + j0, blk, pb, off, rows)
            dense(k, [hidT] + OV, lambda kg, off, rows, h0=h0: hidT.ap[:, kg, off - h0:off - h0 + rows], 64, M[(l, "down")], range(4), hs, ev_res2, kc_piece=16, nbanks=6)
        for j, (off, rows) in enumerate(subs):
            k.layernorm(xres[j], xres[j].ap[:rows], rows, lnp[2], lnp[3], xres[j], xres[j].ap[:rows])
            c = c0 + off
            if l == DEPTH - 1:
                if c < 128:
                    S.dma(k.y_s, xres[j].ap[0:64], reads=[xres[j]])
                else:
                    S.dma(k.y_p[c - 128:c - 128 + rows], xres[j].ap[:rows], reads=[xres[j]])
            else:
                S.dma(k.hbuf[c:c + rows], xres[j].ap[:rows], reads=[xres[j]], writes=[k.t_h[si]])
                S.op("act", lambda e, j=j, rows=rows: e.activation(out=m16[j].ap[:rows], in_=xres[j].ap[:rows], func=ACTF.Copy), reads=[xres[j]], writes=[m16[j]])
                for hf in range(2):
                    k.transpose_chunks(m16[j], lambda cc, j=j, rows=rows, hf=hf: m16[j].ap[:rows, (hf * 8 + cc) * 128:(hf * 8 + cc + 1) * 128],
                                       8, rows, XT, XT.ap[:, hf * 8:hf * 8 + 8, off:off + rows], "dve" if hf == 0 else "act")
        if l < DEPTH - 1:
            S.dma(k.hT[:, :, c0:c0 + T], XT.ap[:, :, 0:T], reads=[XT], writes=[k.t_hT[si]])


def ktile_table():
    kts = [("meta", 64, 16), ("snew", 0, 64)]
    for t in range(32):
        kts.append(("f%d" % t, 128 + 128 * t, 128))
    for t in range(32):
        kts.append(("c%d" % t, C + 128 * t, 128))
    return kts


KT_META, KT_SNEW, KT_F0, KT_C0 = 0, 1, 2, 34
import os as _os
ATT_LA = int(_os.environ.get('ATT_LA', '3'))


def phase_B(k, l):
    S, A = k.S, k.A
    lam_init = 0.8 - 0.6 * float(np.exp(-0.3 * l))
    kts = ktile_table()
    sts = k.sts
    all_scr = lambda n: list(k.t_scr[n])

    def nxt(lst, attr):
        i = getattr(k, attr, 0)
        setattr(k, attr, i + 1)
        return lst[i % len(lst)]

    A.reset()
    ck32 = [A.alloc("ck32_%d" % i, [1024], F32) for i in range(2)]
    ck16 = [A.alloc("ck16_%d" % i, [1024], BF16) for i in range(2)]
    cstg = [A.alloc("cstg%d" % i, [8, 512], BF16) for i in range(2)]
    cvA = [A.alloc("cvA%d" % i, [8, 130], BF16) for i in range(2)]
    cvB = [A.alloc("cvB%d" % i, [4, 258], BF16) for i in range(2)]
    lT = A.alloc("lT", [C], F32)
    cT = A.alloc("cT", [C], F32)
    caT = A.alloc("caT", [PAST], F32)
    ccT = A.alloc("ccT", [PAST], F32)
    zer = A.alloc("zer", [512], F32)
    cl = A.alloc("cl", [32, 8], F32)
    lam4 = [A.alloc("lam4_%d" % i, [128], F32) for i in range(4)]
    lamt = A.alloc("lamt", [2, 128], F32)
    lamd = A.alloc("lamd", [2], F32)
    for t in cvA + cvB:
        S.op("pool", lambda e, t=t: e.memset(t.ap, 1.0), writes=[t])
    S.op("pool", lambda e: e.memset(zer.ap, 0.0), writes=[zer])
    for i, src in enumerate((k.lq1, k.lk1, k.lq2, k.lk2)):
        S.dma(lam4[i].ap, src[l].partition_broadcast(128), writes=[lam4[i]])
    for j in range(2):
        S.op("dve", lambda e, j=j: e.tensor_tensor(out=lamt.ap[:, j, :], in0=lam4[2 * j].ap, in1=lam4[2 * j + 1].ap, op=ALU.mult),
             reads=[lam4[2 * j], lam4[2 * j + 1]], writes=[lamt])
    S.op("dve", lambda e: e.tensor_reduce(out=lamd.ap, in_=lamt.ap, axis=AX.X, op=ALU.add), reads=[lamt], writes=[lamd])
    S.op("act", lambda e: e.activation(out=lamd.ap, in_=lamd.ap, func=ACTF.Exp), reads=[lamd], writes=[lamd])
    S.op("dve", lambda e: e.tensor_tensor(out=k.nlam.ap, in0=lamd.ap[:, 1:2], in1=lamd.ap[:, 0:1], op=ALU.subtract), reads=[lamd], writes=[k.nlam])
    S.op("dve", lambda e: e.tensor_scalar(out=k.nlam.ap, in0=k.nlam.ap, scalar1=-lam_init, scalar2=None, op0=ALU.add), reads=[k.nlam], writes=[k.nlam])
    S.dma(k.gsb.ap, k.subg[l].partition_broadcast(128), writes=[k.gsb])
    S.op("dve", lambda e: e.tensor_scalar(out=k.gsb.ap, in0=k.gsb.ap, scalar1=1.0 - lam_init, scalar2=None, op0=ALU.mult), reads=[k.gsb], writes=[k.gsb])

    it = 0
    for (src, name) in ((k.cfk, "ka"), (k.cdk, "kb")):
        for g4 in range(8):
            stg = cstg[g4 % 2]
            for tt in range(4):
                t = g4 * 4 + tt
                c32 = ck32[it % 2]; c16 = ck16[it % 2]; it += 1
                S.dma(c32.ap, src[l, t * 128:(t + 1) * 128, :], writes=[c32])
                S.op("pool", lambda e, c32=c32, c16=c16: e.tensor_copy(out=c16.ap, in_=c32.ap), reads=[c32], writes=[c16])
                k.transpose_chunks(c16, lambda c, c16=c16: c16.ap[:, c * 128:(c + 1) * 128], 8, 128, stg,
                                   stg.ap[:, :, tt * 128:(tt + 1) * 128], "act" if tt % 2 else "dve")
            S.dma(k.scrT[name][:, :, C + g4 * 512:C + (g4 + 1) * 512], stg.ap, reads=[stg], writes=[k.t_cache[name]])
    for t in range(32):
        c32 = ck32[it % 2]; it += 1
        vt = cvA[t % 2]
        S.dma(c32.ap, k.cfv[l, t * 128:(t + 1) * 128, :], writes=[c32])
        S.op("pool", lambda e, c32=c32, vt=vt: e.tensor_copy(out=vt.ap[:, :, 0:128], in_=c32.ap.rearrange("p (h d) -> p h d", h=8)), reads=[c32], writes=[vt])
        S.dma(k.vA[C + t * 128:C + (t + 1) * 128], vt.ap.rearrange("p h d -> p (h d)"), reads=[vt], writes=[k.t_cache["vA"]])
        c32 = ck32[it % 2]; it += 1
        vt = cvB[t % 2]
        S.dma(c32.ap, k.cdv[l, t * 128:(t + 1) * 128, :], writes=[c32])
        S.op("pool", lambda e, c32=c32, vt=vt: e.tensor_copy(out=vt.ap[:, :, 0:256], in_=c32.ap.rearrange("p (h d) -> p h d", h=4)), reads=[c32], writes=[vt])
        S.dma(k.vB[C + t * 128:C + (t + 1) * 128], vt.ap.rearrange("p h d -> p (h d)"), reads=[vt], writes=[k.t_cache["vB"]])

    S.dma(lT.ap[0:8], k.lfT_d, reads=[k.t_lfT], writes=[lT])
    S.dma(cl.ap, k.cfl[l].rearrange("(t p) h -> p t h", p=128), writes=[cl])
    for g in range(8):
        pbk, _ = k.trbank()
        for tt in range(4):
            t = g * 4 + tt
            S.op("pe", lambda e, pbk=pbk, tt=tt, t=t: e.matmul(pbk.ap[0:8, tt * 128:(tt + 1) * 128], lhsT=cl.ap[:, t, :], rhs=k.identf.ap,
                                                               start=True, stop=True), reads=[cl, k.identf], writes=[pbk])
        S.op("dve", lambda e, pbk=pbk, g=g: e.tensor_copy(out=caT.ap[0:8, g * 512:(g + 1) * 512], in_=pbk.ap[0:8, :]), reads=[pbk], writes=[caT])

    def scan(dst_b, dst_ap, src_b, src_ap, n, init, init_b=None):
        last = init
        for o in range(0, n, 512):
            w = min(512, n - o)
            ini = 0.0 if last is None else last
            S.op("dve", lambda e, o=o, w=w, ini=ini: e.tensor_tensor_scan(out=dst_ap[0:8, o:o + w], data0=src_ap[0:8, o:o + w], data1=zer.ap[0:8, 0:w],
                                                                         initial=ini, op0=ALU.add, op1=ALU.add),
                 reads=[src_b, zer, dst_b] + ([init_b] if (init_b is not None and o == 0) else []), writes=[dst_b])
            last = dst_ap[0:8, o + w - 1:o + w]
        return last

    last_m = scan(cT, cT.ap[:, 64:80], lT, lT.ap[:, 64:80], 16, None)
    scan(cT, cT.ap[:, 128:C], lT, lT.ap[:, 128:C], NF, last_m)
    last_c = scan(ccT, ccT.ap, caT, caT.ap, PAST, None)
    scan(cT, cT.ap[:, 0:64], lT, lT.ap[:, 0:64], TS, last_c, ccT)
    S.dma(k.cbuf, cT.ap[0:8], reads=[cT], writes=[k.t_cbuf])
    for half, (lo, hi) in enumerate(((0, 64), (64, 66))):
        pbk, _ = k.trbank()
        for kt in range(lo, hi):
            name, col, K_ = kts[kt]
            srcb, srcap = (ccT, ccT.ap[0:8, col - C:col - C + K_]) if kt >= KT_C0 else (cT, cT.ap[0:8, col:col + K_])
            S.op("pe", lambda e, pbk=pbk, kt=kt, lo=lo, K_=K_, srcap=srcap: e.matmul(pbk.ap[0:K_, (kt - lo) * 8:(kt - lo + 1) * 8], lhsT=srcap,
                                                                                   rhs=k.identf.ap[0:8, 0:8], start=True, stop=True),
                 reads=[srcb, k.identf], writes=[pbk])
        S.op("act", lambda e, pbk=pbk, lo=lo, hi=hi: e.activation(out=k.negc.ap[:, lo:hi, :], in_=pbk.ap[:, 0:(hi - lo) * 8].rearrange("p (t h) -> p t h", h=8),
                                                                 func=ACTF.Copy, scale=-1.0), reads=[pbk], writes=[k.negc])

    def groups(nq):
        gs = [(64, 16, [(KT_META, 0, True)])]
        per = nq // 128
        for J in range(NF // nq):
            lst = [(KT_META, 0, False)] + [(KT_F0 + t, 0, False) for t in range(per * J)]
            lst += [(KT_F0 + per * J + i, 128 * i, True) for i in range(per)]
            gs.append((128 + nq * J, nq, lst))
        gs.append((0, 64, [(KT_C0 + t, 0, False) for t in range(32)] + [(KT_SNEW, 0, True)]))
        return gs

    def load_V(Vh, src, h, w):
        S.dma(Vh.ap[0:16, KT_META, :], src[64:80, h * w:(h + 1) * w], reads=all_scr("vA" if w == 130 else "vB"), writes=[Vh])
        S.dma(Vh.ap[0:64, KT_SNEW, :], src[0:64, h * w:(h + 1) * w], reads=[], writes=[Vh])
        S.dma(Vh.ap[:, KT_F0:KT_F0 + 32, :], src[128:C, h * w:(h + 1) * w].rearrange("(t p) d -> p t d", p=128), reads=[], writes=[Vh])
        S.dma(Vh.ap[:, KT_C0:KT_C0 + 32, :], src[C:C + PAST, h * w:(h + 1) * w].rearrange("(t p) d -> p t d", p=128),
              reads=[k.t_cache["vA" if w == 130 else "vB"]], writes=[Vh])

    A.reset()
    banks = k.banks
    Kh = A.alloc("Kh", [C + PAST], BF16)
    Qh = A.alloc("Qh", [C], BF16)
    Vh = A.alloc("Vh", [66, 130], BF16)
    crow = [A.alloc("crow%d" % i, [512], F32) for i in range(2)]
    tt32 = [A.alloc("tt32_%d" % i, [512], F32) for i in range(5)]
    PT = [A.alloc("PT%d" % i, [512], BF16) for i in range(6)]
    SB = [banks[0], banks[1], banks[6]]
    k.tr_single = True
    o16 = [A.alloc("o16_%d" % i, [256], BF16) for i in range(2)]
    oTs = [A.alloc("oTs%d" % i, [2, 512], BF16) for i in range(2)]
    rz = [A.alloc("rz%d" % i, [4], F32) for i in range(4)]
    osq = [A.alloc("osq%d" % i, [256], F32) for i in range(2)]
    o32 = [A.alloc("o32_%d" % i, [256], F32) for i in range(2)]
    k.sb_i = 0
    bg32 = [A.alloc("bg32_%d" % i, [2048], F32) for i in range(4)]
    bg16 = [A.alloc("bg16_%d" % i, [2048], BF16) for i in range(4)]
    fgroups = groups(512)
    for h in range(8):
        S.dma(Kh.ap, k.scrT["ka"][:, h, :], reads=all_scr("ka") + [k.t_cache["ka"]], writes=[Kh])
        S.dma(Qh.ap, k.scrT["qa"][:, h, 0:C], reads=all_scr("qa"), writes=[Qh])
        load_V(Vh, k.vA, h, 130)
        for gi, (qc, Nq, lst) in enumerate(fgroups):
            k.bg.finish(bg16)
            k.bg.issue(4, bg32)
            cr = nxt(crow, "cr_i")
            S.dma(cr.ap[:, 0:Nq], k.cbuf[h, qc:qc + Nq].partition_broadcast(128), reads=[k.t_cbuf], writes=[cr])
            nsub = (Nq + 127) // 128
            first = {}; last = {}
            for idx, (kt, qoff, msk) in enumerate(lst):
                for s_ in range(nsub):
                    if qoff <= s_ * 128:
                        first.setdefault(s_, idx); last[s_] = idx
            pend = []
            LA = min(ATT_LA, 2)
            for idx, (kt, qoff, msk) in enumerate(lst):
                name, col, K_ = kts[kt]
                n = Nq - qoff
                ps = SB[k.sb_i % len(SB)]; k.sb_i += 1
                S.op("pe", lambda e, ps=ps, K_=K_, n=n, col=col, qc=qc, qoff=qoff, Nq=Nq: e.matmul(
                    ps.ap[:K_, :n], lhsT=Kh.ap[:, col:col + K_], rhs=Qh.ap[:, qc + qoff:qc + Nq], start=True, stop=True), reads=[Kh, Qh], writes=[ps])
                t32 = nxt(tt32, "t32_i"); pt = nxt(PT, "pt_i")
                S.op("dve", lambda e, ps=ps, K_=K_, n=n, t32=t32, cr=cr, qoff=qoff, Nq=Nq: e.scalar_tensor_tensor(
                    out=t32.ap[:K_, :n], in0=ps.ap[:K_, :n], scalar=float(FOX_SCALE), in1=cr.ap[:K_, qoff:Nq], op0=ALU.mult, op1=ALU.add),
                    reads=[ps, cr], writes=[t32])
                if msk:
                    mw = min(128, n)
                    S.op("pool", lambda e, t32=t32, K_=K_, mw=mw: e.tensor_tensor(out=t32.ap[:K_, :mw], in0=t32.ap[:K_, :mw], in1=k.cmask.ap[:K_, :mw], op=ALU.add),
                         reads=[t32, k.cmask], writes=[t32])
                S.op("act", lambda e, t32=t32, pt=pt, K_=K_, n=n, kt=kt, h=h: e.activation(out=pt.ap[:K_, :n], in_=t32.ap[:K_, :n], func=ACTF.Exp,
                                                                                          bias=k.negc.ap[:K_, kt, h:h + 1]), reads=[t32, k.negc], writes=[pt])

                def pv(idx=idx, kt=kt, qoff=qoff, K_=K_, pt=pt):
                    for s_ in range(nsub):
                        if qoff > s_ * 128:
                            continue
                        qs = s_ * 128 - qoff
                        qn = min(128, Nq - s_ * 128)
                        ob = banks[2 + s_]
                        S.op("pe", lambda e, ob=ob, pt=pt, K_=K_, qs=qs, qn=qn, kt=kt, st=(first[s_] == idx), sp=(last[s_] == idx): e.matmul(
                            ob.ap[:qn, 0:129], lhsT=pt.ap[:K_, qs:qs + qn], rhs=Vh.ap[:K_, kt, 0:129], start=st, stop=sp), reads=[pt, Vh], writes=[ob])
                pend.append(pv)
                if len(pend) > LA:
                    pend.pop(0)()
            while pend:
                pend.pop(0)()
            ot = nxt(oTs, "ots_i")
            for s_ in range(nsub):
                qn = min(128, Nq - s_ * 128)
                ob = banks[2 + s_]
                r = nxt(rz, "rz_i"); o6 = nxt(o16, "o16_i")
                S.op("dve", lambda e, ob=ob, r=r, qn=qn: e.reciprocal(out=r.ap[:qn, 0:1], in_=ob.ap[:qn, 128:129]), reads=[ob], writes=[r])
                S.op("dve", lambda e, ob=ob, r=r, o6=o6, qn=qn: e.tensor_scalar(out=o6.ap[:qn, 0:128], in0=ob.ap[:qn, 0:128], scalar1=r.ap[:qn, 0:1], scalar2=None,
                                                                              op0=ALU.mult), reads=[ob, r], writes=[o6])
                k.transpose_chunks(o6, lambda c, o6=o6, qn=qn: o6.ap[:qn, 0:128], 1, qn, ot, ot.ap[:, 0:1, s_ * 128:s_ * 128 + qn], "act")
            S.dma(k.oaT[:, h, qc:qc + Nq], ot.ap[:, 0, 0:Nq], reads=[ot], writes=[k.t_oa[0 if qc < 128 else (qc - 128) // 512 + 1]])

    k.bg.finish(bg16)
    while not k.bg.done():
        k.bg.issue(4, bg32)
        k.bg.finish(bg16)
    A.reset()
    Kd = A.alloc("Kd", [2, C + PAST], BF16)
    Qd = A.alloc("Qd", [2, C], BF16)
    Vd = A.alloc("Vd", [66, 258], BF16)
    tt32 = [A.alloc("dtt32_%d" % i, [128], F32) for i in range(3)]
    PT = [A.alloc("dPT%d" % i, [256], BF16) for i in range(10)]
    o16 = [A.alloc("do16_%d" % i, [256], BF16) for i in range(2)]
    oTs = [A.alloc("doTs%d" % i, [2, 256], BF16) for i in range(2)]
    rz = [A.alloc("drz%d" % i, [4], F32) for i in range(4)]
    osq = [A.alloc("dosq%d" % i, [256], F32) for i in range(2)]
    o32 = [A.alloc("do32_%d" % i, [256], F32) for i in range(2)]
    t1 = [A.alloc("dt1_%d" % i, [256], F32) for i in range(2)]
    dgroups = groups(256)
    for h in range(4):
        S.dma(Kd.ap, k.scrT["kb"][:, 2 * h:2 * h + 2, :], reads=all_scr("kb") + [k.t_cache["kb"]], writes=[Kd])
        S.dma(Qd.ap, k.scrT["qb"][:, 2 * h:2 * h + 2, 0:C], reads=all_scr("qb"), writes=[Qd])
        load_V(Vd, k.vB, h, 258)
        for gi, (qc, Nq, lst) in enumerate(dgroups):
            nsub = (Nq + 127) // 128
            first = {}; last = {}
            for idx, (kt, qoff, msk) in enumerate(lst):
                for s_ in range(nsub):
                    if qoff <= s_ * 128:
                        first.setdefault(s_, idx); last[s_] = idx
            is_meta_or_s = qc < 128
            pend = []
            LA = min(ATT_LA, 2)
            for idx, (kt, qoff, msk) in enumerate(lst):
                name, col, K_ = kts[kt]
                n = Nq - qoff
                msk = msk and not is_meta_or_s
                ps = SB[k.sb_i % len(SB)]; k.sb_i += 1
                psv = ps.ap.rearrange("p (c n) -> p c n", c=2)
                pts = []
                for c_ in range(2):
                    S.op("pe", lambda e, psv=psv, c_=c_, K_=K_, n=n, col=col, qc=qc, qoff=qoff, Nq=Nq: e.matmul(
                        psv[:K_, c_, :n], lhsT=Kd.ap[:, c_, col:col + K_], rhs=Qd.ap[:, c_, qc + qoff:qc + Nq], start=True, stop=True),
                        reads=[Kd, Qd], writes=[ps])
                for c_ in range(2):
                    pt = nxt(PT, "dpt_i"); pts.append(pt)
                    if msk:
                        mw = min(128, n)
                        t32 = nxt(tt32, "dt32_i")
                        S.op("dve", lambda e, psv=psv, c_=c_, K_=K_, mw=mw, t32=t32: e.scalar_tensor_tensor(
                            out=t32.ap[:K_, :mw], in0=psv[:K_, c_, :mw], scalar=float(DIFF_SCALE), in1=k.kmask.ap[:K_, :mw], op0=ALU.mult, op1=ALU.add),
                            reads=[ps, k.kmask], writes=[t32])
                        S.op("act", lambda e, t32=t32, pt=pt, K_=K_, mw=mw: e.activation(out=pt.ap[:K_, :mw], in_=t32.ap[:K_, :mw], func=ACTF.Exp),
                             reads=[t32], writes=[pt])
                        if n > mw:
                            S.op("act", lambda e, psv=psv, c_=c_, pt=pt, K_=K_, mw=mw, n=n: e.activation(out=pt.ap[:K_, mw:n], in_=psv[:K_, c_, mw:n], func=ACTF.Exp,
                                                                                                        scale=float(DIFF_SCALE)), reads=[ps], writes=[pt])
                    else:
                        S.op("act", lambda e, psv=psv, c_=c_, pt=pt, K_=K_, n=n: e.activation(out=pt.ap[:K_, :n], in_=psv[:K_, c_, :n], func=ACTF.Exp,
                                                                                            scale=float(DIFF_SCALE)), reads=[ps], writes=[pt])

                def pv(idx=idx, kt=kt, qoff=qoff, K_=K_, pts=pts):
                    for c_ in range(2):
                        pt = pts[c_]
                        for s_ in range(nsub):
                            if qoff > s_ * 128:
                                continue
                            qs = s_ * 128 - qoff
                            qn = min(128, Nq - s_ * 128)
                            ob = banks[2 + 2 * c_ + s_]
                            S.op("pe", lambda e, ob=ob, pt=pt, K_=K_, qs=qs, qn=qn, kt=kt, st=(first[s_] == idx), sp=(last[s_] == idx): e.matmul(
                                ob.ap[:qn, 0:257], lhsT=pt.ap[:K_, qs:qs + qn], rhs=Vd.ap[:K_, kt, 0:257], start=st, stop=sp), reads=[pt, Vd], writes=[ob])
                pend.append(pv)
                if len(pend) > LA:
                    pend.pop(0)()
            while pend:
                pend.pop(0)()
            ot = nxt(oTs, "dots_i")
            for s_ in range(nsub):
                qn = min(128, Nq - s_ * 128)
                ob0 = banks[2 + s_]; ob1 = banks[4 + s_]
                r = nxt(rz, "drz_i"); o6 = nxt(o16, "do16_i"); tt = nxt(t1, "dt1_i"); oo = nxt(o32, "do32_i"); sq = nxt(osq, "dosq_i")
                S.op("dve", lambda e, ob0=ob0, r=r, qn=qn: e.reciprocal(out=r.ap[:qn, 0:1], in_=ob0.ap[:qn, 256:257]), reads=[ob0], writes=[r])
                S.op("dve", lambda e, ob1=ob1, r=r, qn=qn: e.reciprocal(out=r.ap[:qn, 1:2], in_=ob1.ap[:qn, 256:257]), reads=[ob1], writes=[r])
                S.op("dve", lambda e, r=r, qn=qn: e.tensor_tensor(out=r.ap[:qn, 1:2], in0=r.ap[:qn, 1:2], in1=k.nlam.ap[:qn], op=ALU.mult), reads=[r, k.nlam], writes=[r])
                S.op("dve", lambda e, ob0=ob0, r=r, tt=tt, qn=qn: e.tensor_scalar(out=tt.ap[:qn], in0=ob0.ap[:qn, 0:256], scalar1=r.ap[:qn, 0:1], scalar2=None, op0=ALU.mult),
                     reads=[ob0, r], writes=[tt])
                S.op("dve", lambda e, ob1=ob1, r=r, tt=tt, oo=oo, qn=qn: e.scalar_tensor_tensor(out=oo.ap[:qn], in0=ob1.ap[:qn, 0:256], scalar=r.ap[:qn, 1:2], in1=tt.ap[:qn],
                                                                                            op0=ALU.mult, op1=ALU.add), reads=[ob1, r, tt], writes=[oo])
                S.op("act", lambda e, oo=oo, sq=sq, r=r, qn=qn: e.activation(out=sq.ap[:qn], in_=oo.ap[:qn], func=ACTF.Square, accum_out=r.ap[:qn, 2:3]),
                     reads=[oo], writes=[sq, r])
                S.op("dve", lambda e, r=r, qn=qn: e.tensor_scalar(out=r.ap[:qn, 2:3], in0=r.ap[:qn, 2:3], scalar1=1.0 / 256.0, scalar2=1e-5, op0=ALU.mult, op1=ALU.add),
                     reads=[r], writes=[r])
                S.op("act", lambda e, r=r, qn=qn: e.activation(out=r.ap[:qn, 2:3], in_=r.ap[:qn, 2:3], func=ACTF.Ln), reads=[r], writes=[r])
                S.op("act", lambda e, r=r, qn=qn: e.activation(out=r.ap[:qn, 2:3], in_=r.ap[:qn, 2:3], func=ACTF.Exp, scale=-0.5), reads=[r], writes=[r])
                S.op("dve", lambda e, oo=oo, r=r, o6=o6, qn=qn: e.scalar_tensor_tensor(out=o6.ap[:qn], in0=oo.ap[:qn], scalar=r.ap[:qn, 2:3], in1=k.gsb.ap[:qn],
                                                                                   op0=ALU.mult, op1=ALU.mult), reads=[oo, r, k.gsb], writes=[o6])
                k.transpose_chunks(o6, lambda c, o6=o6, qn=qn: o6.ap[:qn, c * 128:(c + 1) * 128], 2, qn, ot, ot.ap[:, :, s_ * 128:s_ * 128 + qn], "act")
            si = 0 if qc < 128 else (qc - 128) // 512 + 1
            S.dma(k.obT[:, 2 * h:2 * h + 2, qc:qc + Nq], ot.ap[:, :, 0:Nq], reads=[ot], writes=[k.t_ob[si]])
    k.tr_single = False
```
